# Optimizing a Trainium2 kernel written in Bass

```python
import jax, jax.numpy as jnp
from jax import lax
import numpy as np

D_MODEL = 1024
BATCH = 4
SEQ = 4096
DEPTH = 1

GRID_W = 64
CTX_LEN = 256
RET_HEADS = 4
RET_QK = 128
RET_V = 256
RET_CHUNK = 128
RET_ROPE_BASE = 10000.0
MLA_HEADS = 8
MLA_NOPE = 64
MLA_ROPE = 32
MLA_V = 64
MLA_Q_RANK = 384
MLA_KV_RANK = 256
ROPE_BASE = 10000.0
Q_BLOCK = 128
D_FF = -(-8 * D_MODEL // (3 * 256)) * 256
IN_SPLITS = (RET_HEADS * RET_QK, RET_HEADS * RET_QK, RET_HEADS * RET_V, RET_HEADS * RET_V,
             MLA_Q_RANK, MLA_KV_RANK, MLA_ROPE, D_MODEL, D_MODEL)
IN_WIDTH = sum(IN_SPLITS)
LN_EPS = 1e-5
RMS_EPS = 1e-6

kernel_name = 'hybrid_retention_mla_prefix_dit_block'


def _layer_norm(x, g, b):
    xf = x.astype(jnp.float32)
    xc = xf - jnp.mean(xf, -1, keepdims=True)
    var = jnp.mean(xc * xc, -1, keepdims=True)
    return (xc * lax.rsqrt(var + LN_EPS) * g + b).astype(x.dtype)


def _rms_norm(x, g):
    xf = x.astype(jnp.float32)
    return (xf * lax.rsqrt(jnp.mean(xf * xf, -1, keepdims=True) + RMS_EPS) * g).astype(x.dtype)


def _head_norm(x):
    xf = x.astype(jnp.float32)
    xc = xf - jnp.mean(xf, -1, keepdims=True)
    var = jnp.mean(xc * xc, -1, keepdims=True)
    return (xc * lax.rsqrt(var + LN_EPS)).astype(x.dtype)


def _rope(x, pos, base):
    d = x.shape[-1]
    half = d // 2
    freqs = base ** (-jnp.arange(half, dtype=jnp.float32) / half)
    ang = pos[:, None] * freqs[None, :]
    cos = jnp.cos(ang)[None, :, None, :]
    sin = jnp.sin(ang)[None, :, None, :]
    xf = x.astype(jnp.float32)
    x1, x2 = xf[..., :half], xf[..., half:]
    return jnp.concatenate([x1 * cos - x2 * sin, x2 * cos + x1 * sin], -1).astype(x.dtype)


def _rope_2d(x, row, col):
    h = x.shape[-1] // 2
    return jnp.concatenate([_rope(x[..., :h], row, ROPE_BASE), _rope(x[..., h:], col, ROPE_BASE)], -1)


def _split_in(p):
    bounds = []
    acc = 0
    for w in IN_SPLITS[:-1]:
        acc += w
        bounds.append(acc)
    return jnp.split(p, bounds, axis=-1)


def _heads(a, n):
    return a.reshape(a.shape[0], a.shape[1], n, a.shape[-1] // n)


def _retention_dir(q, k, v, log_g, s0):
    B, L, H, _ = q.shape
    dv = v.shape[-1]
    C = RET_CHUNK
    N = L // C

    def chunks(a):
        return a.astype(jnp.float32).reshape(B, N, C, H, a.shape[-1]).transpose(1, 0, 3, 2, 4)

    idx = jnp.arange(C, dtype=jnp.float32)
    rel = idx[:, None] - idx[None, :]
    inner = jnp.where(rel[None] >= 0, jnp.exp(jnp.maximum(rel, 0.0)[None] * log_g[:, None, None]), 0.0)
    q_dec = jnp.exp((idx + 1.0)[None, :] * log_g[:, None])[None, :, :, None]
    k_dec = jnp.exp((C - 1.0 - idx)[None, :] * log_g[:, None])[None, :, :, None]
    c_dec = jnp.exp(C * log_g)[None, :, None, None]

    def step(s, blk):
        qc, kc, vc = blk
        a = jnp.einsum('bhid,bhjd->bhij', qc, kc) * inner
        o = jnp.einsum('bhij,bhjv->bhiv', a, vc) + jnp.einsum('bhid,bhdv->bhiv', qc, s) * q_dec
        s = s * c_dec + jnp.einsum('bhjd,bhjv->bhdv', kc * k_dec, vc)
        return s, o

    s, o = lax.scan(step, s0, (chunks(q), chunks(k), chunks(v)))
    out = o.transpose(1, 0, 3, 2, 4).reshape(B, L, H, dv)
    return out.astype(v.dtype), s


def _attend(q, k, v):
    s = jnp.einsum('bqhd,bkhd->bhqk', q, k).astype(jnp.float32) * (q.shape[-1] ** -0.5)
    p = jax.nn.softmax(s, axis=-1).astype(v.dtype)
    return jnp.einsum('bhqk,bkhv->bqhv', p, v)


def _blocked_attend(q, k, v):
    B, L, H, d = q.shape
    nb = L // Q_BLOCK
    qb = q.reshape(B, nb, Q_BLOCK, H, d).transpose(1, 0, 2, 3, 4)
    ob = lax.map(lambda qi: _attend(qi, k, v), qb)
    return ob.transpose(1, 0, 2, 3, 4).reshape(B, L, H, ob.shape[-1])


def _mla_qkv(dq, dkv, kr, q_norm, w_uq, kv_norm, w_ukv, pos2d):
    B, L = dq.shape[0], dq.shape[1]
    q = _heads(_rms_norm(dq, q_norm) @ w_uq, MLA_HEADS)
    kv = _heads(_rms_norm(dkv, kv_norm) @ w_ukv, MLA_HEADS)
    q_nope, q_rope = q[..., :MLA_NOPE], q[..., MLA_NOPE:]
    k_nope, v = kv[..., :MLA_NOPE], kv[..., MLA_NOPE:]
    k_rope = kr[:, :, None, :]
    if pos2d is not None:
        q_rope = _rope_2d(q_rope, pos2d[0], pos2d[1])
        k_rope = _rope_2d(k_rope, pos2d[0], pos2d[1])
    k_rope = jnp.broadcast_to(k_rope, (B, L, MLA_HEADS, MLA_ROPE))
    return (jnp.concatenate([q_nope, q_rope], -1), jnp.concatenate([k_nope, k_rope], -1), v)


def _swiglu(h, w_gu, w_down):
    a, b = jnp.split(h @ w_gu, 2, axis=-1)
    return (jax.nn.silu(a) * b) @ w_down


def _token_mixer(hc, hl, row, col, tpos, w_in, dec_f, dec_b, w_ret_o, q_norm, w_uq, kv_norm, w_ukv,
                 w_mla_o, w_out, need_ctx):
    B, L, _ = hl.shape
    rq_c, rk_c, rv_c, rg_c, dq_c, dkv_c, kr_c, gr_c, gm_c = _split_in(hc @ w_in)
    rq_l, rk_l, rv_l, rg_l, dq_l, dkv_l, kr_l, gr_l, gm_l = _split_in(hl @ w_in)

    k_scale = RET_QK ** -0.5
    qc = _heads(rq_c, RET_HEADS)
    kc = _heads(rk_c, RET_HEADS) * k_scale
    vc = _heads(rv_c, RET_HEADS)
    ql = _rope(_heads(rq_l, RET_HEADS), tpos, RET_ROPE_BASE)
    kl = _rope(_heads(rk_l, RET_HEADS), tpos, RET_ROPE_BASE) * k_scale
    vl = _heads(rv_l, RET_HEADS)
    lg_f = jax.nn.log_sigmoid(dec_f.astype(jnp.float32))
    lg_b = jax.nn.log_sigmoid(dec_b.astype(jnp.float32))
    s0 = jnp.zeros((B, RET_HEADS, RET_QK, RET_V), jnp.float32)
    flip = lambda a: jnp.flip(a, axis=1)
    oc_f, s_f = _retention_dir(qc, kc, vc, lg_f, s0)
    oc_b, s_b = _retention_dir(flip(qc), flip(kc), flip(vc), lg_b, s0)
    ol_f, _ = _retention_dir(ql, kl, vl, lg_f, s_f)
    ol_b, _ = _retention_dir(flip(ql), flip(kl), flip(vl), lg_b, s_b)
    ret_l = (jax.nn.silu(rg_l) * _head_norm(ol_f + flip(ol_b)).reshape(B, L, -1)) @ w_ret_o

    q_c, k_c, v_c = _mla_qkv(dq_c, dkv_c, kr_c, q_norm, w_uq, kv_norm, w_ukv, None)
    q_l, k_l, v_l = _mla_qkv(dq_l, dkv_l, kr_l, q_norm, w_uq, kv_norm, w_ukv, (row, col))
    k_all = jnp.concatenate([k_c, k_l], axis=1)
    v_all = jnp.concatenate([v_c, v_l], axis=1)
    mla_l = _blocked_attend(q_l, k_all, v_all).reshape(B, L, -1) @ w_mla_o

    y_l = (jax.nn.sigmoid(gr_l) * ret_l + jax.nn.sigmoid(gm_l) * mla_l) @ w_out

    y_c = None
    if need_ctx:
        Lc = hc.shape[1]
        ret_c = (jax.nn.silu(rg_c) * _head_norm(oc_f + flip(oc_b)).reshape(B, Lc, -1)) @ w_ret_o
        mla_c = _attend(q_c, k_c, v_c).reshape(B, Lc, -1) @ w_mla_o
        y_c = (jax.nn.sigmoid(gr_c) * ret_c + jax.nn.sigmoid(gm_c) * mla_c) @ w_out
    return y_c, y_l


def setup_inputs(seed: int = 0) -> dict:
    key = jax.random.key(seed)
    ks = jax.random.split(key, 22)
    f32 = jnp.float32
    beta = (8.0 * DEPTH) ** -0.25

    def nrm(k, shape, scale):
        return jax.random.normal(k, shape, f32) * scale

    dec0 = jnp.log(jnp.exp2(5.0 + jnp.arange(RET_HEADS, dtype=f32)) - 1.0)
    return {
        'x': nrm(ks[0], (BATCH, SEQ, D_MODEL), 1.0),
        'c': nrm(ks[1], (BATCH, D_MODEL), 1.0),
        'ctx': nrm(ks[2], (BATCH, CTX_LEN, D_MODEL), 1.0),
        'c_ctx': nrm(ks[3], (D_MODEL,), 1.0),
        'w_ada': nrm(ks[4], (DEPTH, D_MODEL, 6 * D_MODEL), D_MODEL ** -0.5),
        'b_ada': nrm(ks[5], (DEPTH, 6 * D_MODEL), 0.02),
        'w_in': nrm(ks[6], (DEPTH, D_MODEL, IN_WIDTH), D_MODEL ** -0.5),
        'ret_decay_f': dec0 + nrm(ks[7], (DEPTH, RET_HEADS), 0.1),
        'ret_decay_b': dec0 + nrm(ks[8], (DEPTH, RET_HEADS), 0.1),
        'w_ret_o': nrm(ks[9], (DEPTH, RET_HEADS * RET_V, D_MODEL), beta * (RET_HEADS * RET_V) ** -0.5),
        'mla_q_norm': 1.0 + nrm(ks[10], (DEPTH, MLA_Q_RANK), 0.02),
        'w_uq': nrm(ks[11], (DEPTH, MLA_Q_RANK, MLA_HEADS * (MLA_NOPE + MLA_ROPE)), MLA_Q_RANK ** -0.5),
        'mla_kv_norm': 1.0 + nrm(ks[12], (DEPTH, MLA_KV_RANK), 0.02),
        'w_ukv': nrm(ks[13], (DEPTH, MLA_KV_RANK, MLA_HEADS * (MLA_NOPE + MLA_V)), MLA_KV_RANK ** -0.5),
        'w_mla_o': nrm(ks[14], (DEPTH, MLA_HEADS * MLA_V, D_MODEL), beta * (MLA_HEADS * MLA_V) ** -0.5),
        'w_out': nrm(ks[15], (DEPTH, D_MODEL, D_MODEL), beta * D_MODEL ** -0.5),
        'ln1_g': 1.0 + nrm(ks[16], (DEPTH, D_MODEL), 0.02),
        'ln1_b': nrm(ks[17], (DEPTH, D_MODEL), 0.02),
        'w_gu': nrm(ks[18], (DEPTH, D_MODEL, 2 * D_FF), D_MODEL ** -0.5),
        'w_down': nrm(ks[19], (DEPTH, D_FF, D_MODEL), beta * D_FF ** -0.5),
        'ln2_g': 1.0 + nrm(ks[20], (DEPTH, D_MODEL), 0.02),
        'ln2_b': nrm(ks[21], (DEPTH, D_MODEL), 0.02),
    }


def reference(x, c, ctx, c_ctx, w_ada, b_ada, w_in, ret_decay_f, ret_decay_b, w_ret_o, mla_q_norm, w_uq,
              mla_kv_norm, w_ukv, w_mla_o, w_out, ln1_g, ln1_b, w_gu, w_down, ln2_g, ln2_b):
    L = x.shape[1]
    rows = L // GRID_W
    row = jnp.repeat(jnp.arange(rows, dtype=jnp.float32), GRID_W)
    col = jnp.tile(jnp.arange(GRID_W, dtype=jnp.float32), rows)
    tpos = jnp.arange(L, dtype=jnp.float32)
    alpha = (2.0 * DEPTH) ** 0.25
    sc = jax.nn.silu(c)
    scc = jax.nn.silu(c_ctx)
    for i in range(DEPTH):
        last = i == DEPTH - 1
        sh1, s1, g1, sh2, s2, g2 = jnp.split((sc @ w_ada[i] + b_ada[i])[:, None, :], 6, axis=-1)
        csh1, cs1, cg1, csh2, cs2, cg2 = jnp.split(scc @ w_ada[i] + b_ada[i], 6, axis=-1)
        hl = x * (1.0 + s1) + sh1
        hc = ctx * (1.0 + cs1) + csh1
        y_c, y_l = _token_mixer(hc, hl, row, col, tpos, w_in[i], ret_decay_f[i], ret_decay_b[i], w_ret_o[i],
                                mla_q_norm[i], w_uq[i], mla_kv_norm[i], w_ukv[i], w_mla_o[i], w_out[i],
                                not last)
        x = _layer_norm(alpha * x + g1 * y_l, ln1_g[i], ln1_b[i])
        x = _layer_norm(alpha * x + g2 * _swiglu(x * (1.0 + s2) + sh2, w_gu[i], w_down[i]), ln2_g[i], ln2_b[i])
        if not last:
            ctx = _layer_norm(alpha * ctx + cg1 * y_c, ln1_g[i], ln1_b[i])
            ctx = _layer_norm(alpha * ctx + cg2 * _swiglu(ctx * (1.0 + cs2) + csh2, w_gu[i], w_down[i]),
                              ln2_g[i], ln2_b[i])
    return x
```

```python
import math
from contextlib import ExitStack

import numpy as np
import concourse.bass as bass
import concourse.mybir as mybir
from concourse.bass_utils import run_bass_kernel_spmd

F32 = mybir.dt.float32
BF16 = mybir.dt.bfloat16
AF = mybir.ActivationFunctionType
ALU = mybir.AluOpType
AX = mybir.AxisListType

D = 1024
KC = 8
LH = 2048
NCTX = 256
NKEY = 4352
NKB = 34
IN_W = 5792
RQ, RK, RV, RG, DQ, DKV, KR, GR, GM = 0, 512, 1024, 2048, 3072, 3456, 3712, 3744, 4768
DFF = 2816
LN_EPS = 1e-5
RMS_EPS = 1e-6
ALPHA = 2.0 ** 0.25
KSCALE = 128.0 ** -0.5
ASCALE = 96.0 ** -0.5
ARN = 35968

ENGS = ("pe", "act", "dve", "pool", "sp")
WKEYS = ("out", "accum_out", "ap")


def _isap(v):
    return hasattr(v, "tensor") and hasattr(v, "offset") and hasattr(v, "ap")


class Prog:
    def __init__(self, nc, esem, dsem):
        self.nc = nc
        self.esem = esem
        self.dsem = dsem
        self.dcnt = [0] * len(dsem)
        self.dnext = {"sp": 0, "pool": 0, "act": 0}
        self.dper = len(dsem) // 2
        self.ops = {e: [] for e in ENGS}
        self.cnt = {e: 0 for e in ENGS}
        self.seen = {e: {} for e in ENGS}
        self.recs = {}

    @staticmethod
    def _range(ap):
        name = ap.tensor.name
        pairs = ap.ap
        ds = mybir.dt.size(ap.dtype)
        if str(ap.space) == "PSUM":
            return name, 0, 2048
        if str(ap.space) == "DRAM":
            off = ap.offset
            dims = pairs
        else:
            pitch = pairs[0][0]
            off = ap.offset % pitch if pitch > 0 else ap.offset
            dims = pairs[1:]
        ext = 0
        for s, c in dims:
            ext += (c - 1) * abs(s)
        return name, off * ds, (off + ext + 1) * ds

    def _deps(self, eng, reads, writes, ev):
        need = {}

        def want(k, v):
            if k == ev[0] and v >= ev[1]:
                return
            if need.get(k, 0) < v:
                need[k] = v

        rr = [self._range(a) for a in reads]
        wr = [self._range(a) for a in writes]
        for name, lo, hi in rr:
            for rec in self.recs.get(name, ()):
                if rec[0] < hi and lo < rec[1] and rec[2] is not None:
                    want(*rec[2])
        for name, lo, hi in wr:
            for rec in self.recs.get(name, ()):
                if rec[0] < hi and lo < rec[1]:
                    if rec[2] is not None:
                        want(*rec[2])
                    for k, v in rec[3].items():
                        want(k, v)
        for name, lo, hi in rr:
            lst = self.recs.setdefault(name, [])
            hit = False
            for rec in lst:
                if rec[0] < hi and lo < rec[1]:
                    hit = True
                    if rec[3].get(ev[0], 0) < ev[1]:
                        rec[3][ev[0]] = ev[1]
            if not hit:
                lst.append([lo, hi, None, {ev[0]: ev[1]}])
        for name, lo, hi in wr:
            lst = self.recs.setdefault(name, [])
            new = []
            for rec in lst:
                if rec[0] < hi and lo < rec[1]:
                    if rec[0] < lo:
                        new.append([rec[0], lo, rec[2], dict(rec[3])])
                    if hi < rec[1]:
                        new.append([hi, rec[1], rec[2], dict(rec[3])])
                else:
                    new.append(rec)
            new.append([lo, hi, ev, {}])
            self.recs[name] = new
        waits = []
        for k, v in need.items():
            if k == ("e", "pe") and eng == "pe":
                continue
            if self.seen[eng].get(k, 0) >= v:
                continue
            self.seen[eng][k] = v
            waits.append((k, v))
        return waits

    def I(self, eng, meth, inc=True, **kw):
        reads, writes = [], []
        for k, v in kw.items():
            if _isap(v):
                (writes if k in WKEYS else reads).append(v)
        writes = writes + [a for a in reads if str(a.space) == "PSUM"]
        ev = (("e", eng), self.cnt[eng] + 1)
        waits = self._deps(eng, reads, writes, ev)
        if inc:
            self.cnt[eng] += 1
        self.ops[eng].append((waits, meth, kw, self.esem[eng] if inc else None, 1))

    def dma(self, q, out, in_, **kw):
        base = 0 if q == "sp" else self.dper
        idx = base + self.dnext[q]
        self.dnext[q] = (self.dnext[q] + 1) % self.dper
        prev = self.dcnt[idx]
        self.dcnt[idx] += 16
        ev = (("d", idx), self.dcnt[idx])
        waits = self._deps(q, [in_], [out], ev)
        if prev > 0 and self.seen[q].get(("d", idx), 0) < prev:
            self.seen[q][("d", idx)] = prev
            waits.append((("d", idx), prev))
        kw = dict(kw)
        kw["out"] = out
        kw["in_"] = in_
        self.ops[q].append((waits, "dma_start", kw, self.dsem[idx], 16))

    def finish(self):
        waits = [(("d", i), c) for i, c in enumerate(self.dcnt) if c > 0]
        self.ops["sp"].append((waits, None, None, None, 0))

    def emit(self, eng, e):
        for waits, meth, kw, sem, amt in self.ops[eng]:
            for (kind, key), val in waits:
                s = self.esem[key] if kind == "e" else self.dsem[key]
                e.wait_ge(s, val)
            if meth is None:
                continue
            ins = getattr(e, meth)(**kw)
            if sem is not None:
                ins.then_inc(sem, amt)

    def mm(self, out, lhsT, rhs, start=True, stop=True, inc=True):
        self.I("pe", "matmul", inc=inc, out=out, lhsT=lhsT, rhs=rhs, start=start, stop=stop)

    def tr(self, out, in_, ident, inc=True):
        self.I("pe", "transpose", inc=inc, out=out, in_=in_, identity=ident)

    def act(self, out, in_, func, **kw):
        self.I("act", "activation", out=out, in_=in_, func=func, **kw)

    def tt(self, eng, out, in0, in1, op):
        self.I(eng, "tensor_tensor", out=out, in0=in0, in1=in1, op=op)

    def ts(self, eng, out, in0, s1, op0, s2=None, op1=None):
        if op1 is None:
            self.I(eng, "tensor_scalar", out=out, in0=in0, scalar1=s1, scalar2=None, op0=op0)
        else:
            self.I(eng, "tensor_scalar", out=out, in0=in0, scalar1=s1, scalar2=s2, op0=op0, op1=op1)

    def stt(self, eng, out, in0, scalar, in1, op0, op1):
        self.I(eng, "scalar_tensor_tensor", out=out, in0=in0, scalar=scalar, in1=in1, op0=op0, op1=op1)

    def cp(self, eng, out, in_):
        if eng == "act":
            self.I("act", "activation", out=out, in_=in_, func=AF.Copy)
        else:
            self.I(eng, "tensor_copy", out=out, in_=in_)

    def memset(self, eng, ap, val):
        self.I(eng, "memset", ap=ap, constant=val)


def bc(ap, shape, axis):
    return ap.unsqueeze(axis).to_broadcast(list(shape))


class _Stop(Exception):
    pass


def build(debug=None, stop=99):
    debug = debug or {}

    def ck(k):
        if stop == k:
            raise _Stop()
    nc = bass.Bass("TRN2", target_bir_lowering=False)

    def din(name, shape, dt=F32):
        return nc.dram_tensor(name, list(shape), dt, kind="ExternalInput").ap()

    def dscr(name, shape, dt=BF16):
        return nc.dram_tensor(name, list(shape), dt, kind="Internal").ap()

    x_own = din("x_own", [LH, D])
    x_oth = din("x_oth", [LH, D])
    ctx_d = din("ctx", [NCTX, D])
    ccol_d = din("ccol", [128, 8, 2])
    w_ada_d = din("w_ada", [D, 6 * D])
    bada_col_d = din("bada_col", [128, 48])
    bada_row_d = din("bada_row", [1, 6 * D])
    w_in_d = din("w_in", [D, IN_W])
    w_ret_o_d = din("w_ret_o", [D, D])
    w_uq_d = din("w_uq", [384, 768])
    w_ukv_d = din("w_ukv", [256, 1024])
    w_mla_o_d = din("w_mla_o", [512, D])
    w_out_d = din("w_out", [D, D])
    w_gu_d = din("w_gu", [D, 2 * DFF])
    w_down_d = din("w_down", [DFF, D])
    qn_col_d = din("qn_col", [128, 3])
    kvn_col_d = din("kvn_col", [128, 2])
    ln1_col_d = din("ln1_col", [128, 8, 2])
    lnrows_d = din("lnrows", [4, D])
    dec_d = din("dec", [1, 8])
    flags_d = din("flags", [128, 4])
    rope_own_d = din("rope_own", [LH, 2, 128])
    rope_oth_d = din("rope_oth", [LH, 2, 128])
    mrope_own_d = din("mrope_own", [LH, 2, 32])
    mrope_oth_d = din("mrope_oth", [LH, 2, 32])
    dexp_d = din("dexp", [128, NKB, 2])
    iot_d = din("iot", [128, 4])
    masktab_d = din("masktab", [128, 4, 128])
    rowt_d = din("rowt", [128, 2, 128])
    ident_d = din("ident", [128, 128])
    out_d = nc.dram_tensor("out", [LH, D], F32, kind="ExternalOutput").ap()
    dbg_d = {k: nc.dram_tensor("dbg_" + k, list(shp), F32, kind="ExternalOutput").ap()
             for k, shp in debug.items()}

    w_in_s0 = dscr("w_in_s", [D, IN_W])
    w_in_s = w_in_s0.rearrange("(kc p) n -> p kc n", p=128)
    w_gu_s0 = dscr("w_gu_s", [D, 2 * DFF])
    w_gu_s = w_gu_s0.rearrange("(kc p) n -> p kc n", p=128)
    w_down_s0 = dscr("w_down_s", [DFF, D])
    w_down_s = w_down_s0.rearrange("(kc p) n -> p kc n", p=128)
    w_ret_o_s0 = dscr("w_ret_o_s", [D, D])
    w_ret_o_s = w_ret_o_s0.rearrange("(kc p) n -> p kc n", p=128)
    w_out_s0 = dscr("w_out_s", [D, D])
    w_out_s = w_out_s0.rearrange("(kc p) n -> p kc n", p=128)
    w_mla_o_s0 = dscr("w_mla_o_s", [512, D])
    w_mla_o_s = w_mla_o_s0.rearrange("(h p) n -> p h n", p=64)
    ktaug_s = dscr("ktaug_s", [8, 97, NKEY])
    kret_s = dscr("kret_s", [LH, 512])
    vret_s = dscr("vret_s", [LH, 1024])
    vaug_s = dscr("vaug_s", [8, 128, NKB, 65])

    with ExitStack() as es:
        def sb(name, shape, dt):
            return es.enter_context(nc.sbuf_tensor("s_" + name, list(shape), dt))

        ps = [es.enter_context(nc.psum_tensor("ps%d" % i, [128, 512], F32)) for i in range(8)]
        psb = [p[:].bitcast(BF16) for p in ps]
        esem = {e: es.enter_context(nc.semaphore("sem_" + e)) for e in ENGS}
        dsem = [es.enter_context(nc.semaphore("dsem%d" % i)) for i in range(24)]
        P = Prog(nc, esem, dsem)

        ident_f = sb("ident_f", [128, 128], F32)
        ident = sb("ident", [128, 128], BF16)
        ones_f = sb("ones_f", [128, 128], F32)
        lnbc = sb("lnbc", [128, 4, D], F32)
        g12 = sb("g12", [128, 2, D], F32)
        adaT = sb("adaT", [128, 48, 2], F32)
        m1 = sb("m1", [128, 8, 4], F32)
        a2b2 = sb("a2b2", [128, 8, 2], F32)
        lg = sb("lg", [128, 8], F32)
        cdec = sb("cdec", [128, 8], F32)
        c512 = sb("c512", [128, 8], F32)
        c2048 = sb("c2048", [128, 8], F32)
        kd = sb("kd", [128, 2, 4], F32)
        Dtab = sb("Dtab", [128, NKB, 2, 4], F32)
        qd = sb("qd", [128, 2, 4, 128], F32)
        Mcomb = sb("Mcomb", [128, 4, 128], F32)
        flags = sb("flags", [128, 4], F32)
        wfb = sb("wfb", [128, 8], F32)
        wuq = sb("wuq", [128, 3, 768], BF16)
        wukv = sb("wukv", [128, 2, 1024], BF16)
        Sf = sb("Sf", [128, 4, 256], F32)
        Sf_bf = sb("Sf_bf", [128, 4, 256], BF16)
        SB = sb("SB", [128, 4, 4, 256], F32)
        small = sb("small", [128, 64], F32)
        wpool = []
        arena = sb("arena", [128, ARN], F32)
        arena_b = arena[:].bitcast(BF16)

        class Arena:
            def __init__(self):
                self.top = 0

            def alloc(self, shape, dt):
                n = 1
                for s in shape[1:]:
                    n *= s
                ds = mybir.dt.size(dt)
                nb = (n * ds + 31) // 32 * 32
                lo = self.top
                self.top += nb
                assert self.top <= ARN * 4, "arena overflow %d" % self.top
                if dt == F32:
                    v = arena[0:shape[0], lo // 4: lo // 4 + n]
                else:
                    v = arena_b[0:shape[0], lo // 2: lo // 2 + n]
                if len(shape) == 2:
                    return v
                if len(shape) == 3:
                    return v.rearrange("p (a b) -> p a b", b=shape[2])
                if len(shape) == 4:
                    return v.rearrange("p (a b c) -> p a b c", b=shape[2], c=shape[3])
                raise ValueError

        A = Arena()
        wctr = [0]

        def wload(src, rows=128):
            b = wpool[wctr[0] % 3]
            wctr[0] += 1
            ncols = src.shape[-1]
            nk = src.shape[1]
            dst = b[0:rows, 0:nk, 0:ncols]
            P.dma("sp", out=dst, in_=src)
            return b

        def tap(name, view):
            if name in dbg_d:
                P.dma("sp" if view.dtype == F32 else "pool", out=dbg_d[name], in_=view)

        try:
            P.dma("sp", out=ident_f[:], in_=ident_d)
            P.cp("dve", ident[:], ident_f[:])
            P.memset("pool", ones_f[:], 1.0)
            P.dma("sp", out=flags[:], in_=flags_d)
            for i in range(4):
                P.dma("sp", out=lnbc[:, i, :], in_=lnrows_d[i, :].partition_broadcast(128))
            def cast_w(dst, src, nsplit):
                n = src.shape[0] // nsplit
                for i in range(nsplit):
                    P.dma("pool", out=dst[i * n:(i + 1) * n, :], in_=src[i * n:(i + 1) * n, :])

            for i in range(4):
                P.dma("pool", out=w_in_s0[i * 256:(i + 1) * 256, RK:RV + 1024], in_=w_in_d[i * 256:(i + 1) * 256, RK:RV + 1024])
            P.dma("pool", out=w_in_s0[:, DKV:GR], in_=w_in_d[:, DKV:GR])

            dect = small[:, 0:8]
            P.dma("sp", out=dect, in_=dec_d[0, :].partition_broadcast(128))
            P.act(small[:, 8:16], dect, AF.Exp, scale=-1.0)
            P.act(small[:, 16:24], small[:, 8:16], AF.Ln, bias=1.0)
            P.ts("dve", lg[:], small[:, 16:24], -1.0, ALU.mult)
            P.act(cdec[:], lg[:], AF.Exp, scale=128.0)
            P.act(c512[:], lg[:], AF.Exp, scale=512.0)
            P.act(c2048[:], lg[:], AF.Exp, scale=2048.0)
            iot = small[:, 24:28]
            P.dma("sp", out=iot, in_=iot_d)
            s0 = A.top
            dexp = A.alloc([128, NKB, 2], F32)
            P.dma("sp", out=dexp, in_=dexp_d)
            masktab = A.alloc([128, 4, 128], F32)
            P.dma("sp", out=masktab, in_=masktab_d)
            rowt = A.alloc([128, 2, 128], F32)
            P.dma("sp", out=rowt, in_=rowt_d)
            etmp = A.alloc([128, 2, 128], F32)
            for h in range(4):
                for d in range(2):
                    lcol = lg[:, d * 4 + h: d * 4 + h + 1]
                    P.act(Dtab[:, :, d, h], dexp[:, :, d], AF.Exp, scale=lcol)
                    P.act(qd[:, d, h, :], rowt[:, d, :], AF.Exp, scale=lcol)
                    P.act(kd[:, d, h: h + 1], iot[:, 1 - d: 2 - d], AF.Exp, scale=lcol)
                    P.act(etmp[:, d, :], masktab[:, d, :], AF.Exp, scale=lcol)
                    P.tt("dve", etmp[:, d, :], etmp[:, d, :], masktab[:, 2 + d, :], ALU.mult)
                P.tt("dve", etmp[:, 0, :], etmp[:, 0, :], etmp[:, 1, :], ALU.add)
                P.ts("dve", Mcomb[:, h, :], etmp[:, 0, :], KSCALE, ALU.mult)
            lgsel = small[:, 52:56]
            P.ts("dve", lgsel, lg[:, 0:4], flags[:, 0:1], ALU.mult)
            P.stt("dve", lgsel, lg[:, 4:8], flags[:, 1:2], lgsel, ALU.mult, ALU.add)
            for h in range(4):
                P.act(Dtab[:, 2:18, 1, h], dexp[:, 2:18, 1], AF.Exp, scale=lgsel[:, h:h + 1])
            P.ts("dve", Dtab[:], Dtab[:], KSCALE, ALU.mult)
            P.ts("dve", kd[:], kd[:], KSCALE, ALU.mult)
            for d in range(2):
                P.stt("dve", wfb[:, d * 4:(d + 1) * 4], c2048[:, d * 4:(d + 1) * 4], flags[:, d:d + 1],
                      flags[:, 2 + d:3 + d].to_broadcast([128, 4]), ALU.mult, ALU.add)

            ccol = A.alloc([128, 8, 2], F32)
            P.dma("sp", out=ccol, in_=ccol_d)
            scb = A.alloc([128, 8, 2], BF16)
            P.act(scb, ccol, AF.Silu)
            screp = A.alloc([128, 8, 128], BF16)
            P.cp("dve", screp, scb[:, :, 0:1].to_broadcast([128, 8, 128]))
            bcol = A.alloc([128, 48], F32)
            P.dma("sp", out=bcol, in_=bada_col_d)
            brow = A.alloc([128, 2, D], F32)
            P.dma("sp", out=brow[:, 0, :], in_=bada_row_d[0, 2048:3072].partition_broadcast(128))
            P.dma("sp", out=brow[:, 1, :], in_=bada_row_d[0, 5120:6144].partition_broadcast(128))
            ln1col = A.alloc([128, 8, 2], F32)
            P.dma("sp", out=ln1col, in_=ln1_col_d)
            wa = [A.alloc([128, 8, 1536], BF16) for _ in range(2)]
            w_ada_v = w_ada_d.rearrange("(kc p) n -> p kc n", p=128)
            ps_ada = ps[0][:, 0:96]
            for pc in range(4):
                w = wa[pc % 2]
                for kc in range(KC):
                    P.dma("pool", out=w[:, kc, :], in_=w_ada_v[:, kc, pc * 1536:(pc + 1) * 1536])
                for ch in range(12):
                    CH = pc * 12 + ch
                    for kc in range(KC):
                        P.mm(ps_ada[:, 2 * CH:2 * CH + 2], w[:, kc, ch * 128:(ch + 1) * 128], scb[:, kc, :],
                             start=(kc == 0), stop=(kc == KC - 1), inc=(kc == KC - 1))
                if pc in (1, 3):
                    gi = 0 if pc == 1 else 1
                    for hf in range(2):
                        pg = ps[1 + hf]
                        for kc in range(KC):
                            P.mm(pg[:], screp[:, kc, :], w[:, kc, 512 + hf * 512: 1024 + hf * 512],
                                 start=(kc == 0), stop=(kc == KC - 1), inc=(kc == KC - 1))
                        P.tt("dve", g12[:, gi, hf * 512:(hf + 1) * 512], pg[:], brow[:, gi, hf * 512:(hf + 1) * 512], ALU.add)
            P.tt("dve", adaT[:], ps_ada.rearrange("p (c t) -> p c t", t=2),
                 bcol.unsqueeze(2).to_broadcast([128, 48, 2]), ALU.add)
            P.ts("dve", m1[:, :, 0], adaT[:, 8:16, 0], 1.0, ALU.add)
            P.cp("dve", m1[:, :, 1], adaT[:, 0:8, 0])
            P.ts("dve", m1[:, :, 2], adaT[:, 8:16, 1], 1.0, ALU.add)
            P.cp("dve", m1[:, :, 3], adaT[:, 0:8, 1])
            s2p = small[:, 32:40]
            P.ts("dve", s2p, adaT[:, 32:40, 0], 1.0, ALU.add)
            P.tt("dve", a2b2[:, :, 0], ln1col[:, :, 0], s2p, ALU.mult)
            P.tt("dve", a2b2[:, :, 1], ln1col[:, :, 1], s2p, ALU.mult)
            P.tt("dve", a2b2[:, :, 1], a2b2[:, :, 1], adaT[:, 24:32, 0], ALU.add)
            tap("adaT", adaT[:].rearrange("p c t -> p (c t)"))
            tap("g12", g12[:, 0, :])

            wtmp = A.alloc([128, 3, 1024], F32)
            qncol = small[:, 40:43]
            kvncol = small[:, 44:46]
            P.dma("sp", out=qncol, in_=qn_col_d)
            P.dma("sp", out=kvncol, in_=kvn_col_d)
            P.dma("sp", out=wtmp[:, :, 0:768], in_=w_uq_d.rearrange("(kc p) n -> p kc n", p=128))
            for kc in range(3):
                P.ts("dve", wuq[:, kc, :], wtmp[:, kc, 0:768], qncol[:, kc:kc + 1], ALU.mult)
            w_ukv_v = w_ukv_d.rearrange("(kc p) (h t d) -> p kc t h d", p=128, t=2, d=64)
            wtmp2 = A.alloc([128, 2, 1024], F32)
            for kc in range(2):
                for t in range(2):
                    P.dma("sp", out=wtmp2[:, kc, t * 512:(t + 1) * 512].rearrange("p (h d) -> p h d", d=64),
                          in_=w_ukv_v[:, kc, t, :, :])
            for kc in range(2):
                P.ts("dve", wukv[:, kc, :], wtmp2[:, kc, :], kvncol[:, kc:kc + 1], ALU.mult)

            A.top = s0
            ck(0)

            def load_hlT(src, nblk, mcol, hlT, xbufs):
                for j in range(nblk):
                    xb = xbufs[j % 2]
                    P.dma("pool", out=xb, in_=src[j * 128:(j + 1) * 128, :])
                    for kc in range(KC):
                        P.tr(psb[kc // 2][:, (kc % 2) * 512 + j * 128:(kc % 2) * 512 + (j + 1) * 128],
                             xb[:, kc * 128:(kc + 1) * 128], ident[:])
                n = nblk * 128
                for kc in range(KC):
                    P.act(hlT[:, kc, 0:n], psb[kc // 2][:, (kc % 2) * 512:(kc % 2) * 512 + n], AF.Identity,
                          scale=m1[:, kc, mcol:mcol + 1], bias=m1[:, kc, mcol + 1:mcol + 2])

            def rope_tm(src, nh, RC, RS, t1, t2, outv):
                P.tt("dve", t1, src, bc(RC, [128, nh, 128], 1), ALU.mult)
                P.tt("dve", t2[:, :, 0:64], src[:, :, 64:128], bc(RS[:, 0:64], [128, nh, 64], 1), ALU.mult)
                P.tt("dve", t2[:, :, 64:128], src[:, :, 0:64], bc(RS[:, 64:128], [128, nh, 64], 1), ALU.mult)
                P.tt("pool", outv, t1, t2, ALU.add)

            def mrope(src, nh, C, S, t1, t2, outv):
                P.tt("dve", t1, src, bc(C, [128, nh, 32], 1), ALU.mult)
                for b in range(2):
                    for hf in range(2):
                        o = b * 16 + hf * 8
                        i = b * 16 + (1 - hf) * 8
                        P.tt("dve", t2[:, :, o:o + 8], src[:, :, i:i + 8], bc(S[:, o:o + 8], [128, nh, 8], 1), ALU.mult)
                P.tt("pool", outv, t1, t2, ALU.add)

            s1 = A.top
            w1 = A.alloc([128, 8, 1824], BF16)
            P.dma("sp", out=w1[:, :, 0:1536], in_=w_in_s[:, :, RK:RV + 1024])
            P.dma("sp", out=w1[:, :, 1536:1824], in_=w_in_s[:, :, DKV:GR])
            hlTs = [A.alloc([128, 8, 512], BF16) for _ in range(2)]
            xsets = [[A.alloc([128, D], BF16) for _ in range(4)] for _ in range(2)]
            ropeTs = [A.alloc([128, 4, 2, 128], F32) for _ in range(2)]
            mropeTs = [A.alloc([128, 4, 2, 32], F32) for _ in range(2)]

            def mkset():
                d = {}
                d["t1"] = A.alloc([128, 4, 128], F32)
                d["t2"] = A.alloc([128, 4, 128], F32)
                d["kr32"] = A.alloc([128, 4, 128], F32)
                d["Kf"] = A.alloc([128, 4, 128], BF16)
                d["Kb"] = A.alloc([128, 4, 128], BF16)
                d["Vr"] = A.alloc([128, 1024], BF16)
                d["junk"] = A.alloc([128, 256], F32)
                d["dkvn"] = A.alloc([128, 256], BF16)
                d["dkvnT"] = A.alloc([128, 2, 128], BF16)
                d["mt1"] = A.alloc([128, 1, 32], F32)
                d["mt2"] = A.alloc([128, 1, 32], F32)
                d["krr"] = A.alloc([128, 1, 32], F32)
                d["kfull"] = A.alloc([128, 8, 97], BF16)
                d["vst"] = A.alloc([128, 8, 65], BF16)
                d["kst"] = A.alloc([128, 8, 128], BF16)
                d["ssq"] = A.alloc([128, 2], F32)
                d["kbf"] = A.alloc([128, 512], BF16)
                return d

            tsets = [mkset(), mkset()]
            Sfc = A.alloc([128, 4, 256], F32)
            Sbc = A.alloc([128, 4, 256], F32)
            Bo = A.alloc([128, 4, 256], F32)
            Bm = A.alloc([128, 4, 4, 256], F32)
            for i in range(2):
                P.memset("pool", tsets[i]["kfull"][:, :, 96:97], 1.0)
                P.memset("pool", tsets[i]["vst"][:, :, 64:65], 1.0)

            cast_q = []
            for (c0, c1) in ((RQ, RK), (RG, DKV), (GR, IN_W)):
                for i in range(4):
                    cast_q.append((w_in_s0[i * 256:(i + 1) * 256, c0:c1], w_in_d[i * 256:(i + 1) * 256, c0:c1]))
            for (dst, src, ns) in ((w_ret_o_s0, w_ret_o_d, 2), (w_mla_o_s0, w_mla_o_d, 1), (w_out_s0, w_out_d, 2),
                                   (w_gu_s0, w_gu_d, 8), (w_down_s0, w_down_d, 5)):
                n = src.shape[0] // ns
                for i in range(ns):
                    r1 = src.shape[0] if i == ns - 1 else (i + 1) * n
                    cast_q.append((dst[i * n:r1, :], src[i * n:r1, :]))

            segs = [("ctx", ctx_d, 2, None, None), ("oth", x_oth, 16, rope_oth_d, mrope_oth_d),
                    ("own", x_own, 16, rope_own_d, mrope_own_d)]
            groups = []
            blocks = []
            gb = 0
            for sname, src, nblk_seg, rope_d, mrope_d in segs:
                for g in range((nblk_seg + 3) // 4):
                    nb = min(4, nblk_seg - g * 4)
                    gi = len(groups)
                    groups.append(dict(sname=sname, src=src[g * 512:g * 512 + nb * 128, :], nb=nb, g=g,
                                       rope=None if rope_d is None else rope_d[g * 512:g * 512 + nb * 128],
                                       mrope=None if mrope_d is None else mrope_d[g * 512:g * 512 + nb * 128],
                                       gb0=gb))
                    for j in range(nb):
                        blk = g * 4 + j
                        first = (blk % 4 == 0) if sname == "own" else (blk == 0)
                        last = (blk % 4 == 3) if sname == "own" else (blk == nblk_seg - 1)
                        blocks.append(dict(gi=gi, j=j, sname=sname, g=g, first=first, last=last, gblk=gb))
                        gb += 1

            def issue_x(gi):
                G = groups[gi]
                for j in range(G["nb"]):
                    P.dma("pool", out=xsets[gi % 2][j], in_=G["src"][j * 128:(j + 1) * 128, :])

            def prep(gi):
                G = groups[gi]
                bs = (G["gb0"] % 2) * 4
                nb = G["nb"]
                mcol = 2 if G["sname"] == "ctx" else 0
                hl = hlTs[gi % 2]
                for j in range(nb):
                    for kc in range(KC):
                        P.tr(psb[bs + kc // 2][:, (kc % 2) * 512 + j * 128:(kc % 2) * 512 + (j + 1) * 128],
                             xsets[gi % 2][j][:, kc * 128:(kc + 1) * 128], ident[:], inc=(kc == KC - 1))
                if gi + 1 < len(groups):
                    issue_x(gi + 1)
                n = nb * 128
                for kc in range(KC):
                    srcp = psb[bs + kc // 2][:, (kc % 2) * 512:(kc % 2) * 512 + n]
                    if kc % 2 == 0:
                        P.act(hl[:, kc, 0:n], srcp, AF.Identity, scale=m1[:, kc, mcol:mcol + 1], bias=m1[:, kc, mcol + 1:mcol + 2])
                    else:
                        P.I("dve", "tensor_scalar", out=hl[:, kc, 0:n], in0=srcp, scalar1=m1[:, kc, mcol:mcol + 1],
                            scalar2=m1[:, kc, mcol + 1:mcol + 2], op0=ALU.mult, op1=ALU.add)
                rT, mT = ropeTs[gi % 2], mropeTs[gi % 2]
                if G["rope"] is None:
                    P.memset("pool", rT[:, :, 0, :], 1.0)
                    P.memset("pool", rT[:, :, 1, :], 0.0)
                    P.memset("pool", mT[:, :, 0, :], 1.0)
                    P.memset("pool", mT[:, :, 1, :], 0.0)
                else:
                    P.dma("sp", out=rT[:, 0:nb], in_=G["rope"].rearrange("(j p) t d -> p j t d", p=128))
                    P.dma("sp", out=mT[:, 0:nb], in_=G["mrope"].rearrange("(j p) t d -> p j t d", p=128))

            IPC = ((0, 0, 512), (1, 512, 512), (2, 1024, 512), (3, 1536, 288))

            def ip_piece(B, k):
                bs = (B["gblk"] % 2) * 4
                bank, c0, cn = IPC[k]
                hl = hlTs[B["gi"] % 2]
                tok = slice(B["j"] * 128, (B["j"] + 1) * 128)
                for kc in range(KC):
                    P.mm(ps[bs + bank][:, 0:cn], hl[:, kc, tok], w1[:, kc, c0:c0 + cn],
                         start=(kc == 0), stop=(kc == KC - 1), inc=(kc == KC - 1))

            def head_ops(B):
                bs = (B["gblk"] % 2) * 4
                T = tsets[B["gblk"] % 2]
                rT, mT = ropeTs[B["gi"] % 2], mropeTs[B["gi"] % 2]
                j = B["j"]
                pk, pv0, pv1, pd = ps[bs], ps[bs + 1], ps[bs + 2], ps[bs + 3]
                rope_tm(pk[:].rearrange("p (h d) -> p h d", d=128), 4, rT[:, j, 0, :], rT[:, j, 1, :], T["t1"], T["t2"], T["kr32"])
                P.cp("act", T["Vr"][:, 0:512], pv0[:])
                P.cp("act", T["Vr"][:, 512:1024], pv1[:])
                dirs = (0, 1) if B["sname"] == "ctx" else (1,)
                for d in dirs:
                    Kx = T["Kf"] if d == 0 else T["Kb"]
                    P.tt("pool", Kx, T["kr32"], Dtab[:, B["gblk"], d, :].unsqueeze(2).to_broadcast([128, 4, 128]), ALU.mult)
                if B["sname"] == "own":
                    r0 = (B["g"] * 4 + j) * 128
                    P.cp("pool", T["kbf"], T["kr32"][:].rearrange("p h d -> p (h d)"))
                    P.dma("sp", out=kret_s[r0:r0 + 128, :], in_=T["kbf"])
                    P.dma("sp", out=vret_s[r0:r0 + 128, :], in_=T["Vr"])
                ssq = T["ssq"][:, 0:1]
                rstd = T["ssq"][:, 1:2]
                P.memset("pool", ssq, 0.0)
                P.act(T["junk"], pd[:, 0:256], AF.Square, accum_out=ssq)
                P.act(rstd, ssq, AF.Sqrt, scale=1.0 / 256.0, bias=RMS_EPS)
                P.I("dve", "reciprocal", out=rstd, in_=rstd)
                P.ts("dve", T["dkvn"], pd[:, 0:256], rstd, ALU.mult)
                mrope(pd[:, 256:288].unsqueeze(1), 1, mT[:, j, 0, :], mT[:, j, 1, :], T["mt1"], T["mt2"], T["krr"])

            def state_part(B, heads, bank):
                T = tsets[B["gblk"] % 2]
                dirs = (0, 1) if B["sname"] == "ctx" else (1,)
                for d in dirs:
                    Kx = T["Kf"] if d == 0 else T["Kb"]
                    if B["sname"] == "ctx":
                        dst = Sfc if d == 0 else Sbc
                    elif B["sname"] == "oth":
                        dst = Bo
                    else:
                        dst = Bm[:, B["g"]]
                    for h in heads:
                        reg = bank[:, (h % 2) * 256:(h % 2) * 256 + 256]
                        P.mm(reg, Kx[:, h, :], T["Vr"][:, h * 256:(h + 1) * 256])
                        if B["first"]:
                            P.cp("dve", dst[:, h, :], reg)
                        else:
                            P.tt("dve", dst[:, h, :], reg, dst[:, h, :], ALU.add)

            def chain_steps(B):
                bs = (B["gblk"] % 2) * 4
                T = tsets[B["gblk"] % 2]
                gblk = B["gblk"]

                def c1():
                    state_part(B, (0, 1), ps[bs])

                def c2():
                    for kc in range(2):
                        P.tr(psb[bs + 3][:, kc * 128:(kc + 1) * 128], T["dkvn"][:, kc * 128:(kc + 1) * 128], ident[:], inc=(kc == 1))
                    P.cp("dve", T["dkvnT"], psb[bs + 3][:, 0:256].rearrange("p (k t) -> p k t", t=128))

                def c3():
                    for t in range(2):
                        for kc in range(2):
                            P.mm(ps[bs + 1 + t][:], T["dkvnT"][:, kc, :], wukv[:, kc, t * 512:(t + 1) * 512],
                                 start=(kc == 0), stop=(kc == 1), inc=(kc == 1))
                    P.cp("act", T["kfull"][:, :, 0:64], ps[bs + 1][:].rearrange("p (h d) -> p h d", d=64))
                    P.cp("pool", T["kfull"][:, :, 64:96], T["krr"].to_broadcast([128, 8, 32]))
                    P.cp("act", T["vst"][:, :, 0:64], ps[bs + 2][:].rearrange("p (h d) -> p h d", d=64))

                def c4():
                    state_part(B, (2, 3), ps[bs + 3])

                def c5():
                    for h in range(8):
                        P.tr(psb[bs][0:97, h * 128:(h + 1) * 128], T["kfull"][:, h, :], ident[:], inc=(h == 7))
                    P.cp("dve", T["kst"][0:97], psb[bs][0:97, :].rearrange("p (h t) -> p h t", t=128))
                    ktv = ktaug_s.rearrange("h r t -> r h t")
                    P.dma("sp", out=ktv[0:96, :, gblk * 128:(gblk + 1) * 128], in_=T["kst"][0:96])
                    P.dma("sp", out=ktv[96:97, :, gblk * 128:(gblk + 1) * 128], in_=T["kst"][96:97])
                    P.dma("sp", out=vaug_s.rearrange("h p c e -> p h c e")[:, :, gblk, :], in_=T["vst"])

                return [c1, c2, c3, c4, c5]

            issue_x(0)
            prep(0)
            for k in range(4):
                ip_piece(blocks[0], k)
            for bi, B in enumerate(blocks):
                head_ops(B)
                nxt = []
                if bi + 1 < len(blocks):
                    NB = blocks[bi + 1]
                    if NB["gi"] != B["gi"]:
                        nxt.append(lambda NB=NB: prep(NB["gi"]))
                    for k in range(4):
                        nxt.append(lambda NB=NB, k=k: ip_piece(NB, k))
                ch = chain_steps(B)
                for st in nxt:
                    st()
                for st in ch:
                    st()
                if B["sname"] != "ctx" and len(cast_q) > 13:
                    d_, s_ = cast_q.pop(0)
                    P.dma("pool", out=d_, in_=s_)
            while cast_q:
                d_, s_ = cast_q.pop(0)
                P.dma("pool", out=d_, in_=s_)
            for h in range(4):
                P.ts("pool", Sf[:, h, :], Sfc[:, h, :], wfb[:, h:h + 1], ALU.mult)
                P.stt("dve", Sf[:, h, :], Bo[:, h, :], flags[:, 0:1], Sf[:, h, :], ALU.mult, ALU.add)
                P.ts("pool", SB[:, 3, h, :], Sbc[:, h, :], wfb[:, 4 + h:5 + h], ALU.mult)
                P.stt("dve", SB[:, 3, h, :], Bo[:, h, :], flags[:, 1:2], SB[:, 3, h, :], ALU.mult, ALU.add)
                for m in (3, 2, 1):
                    P.stt("dve", SB[:, m - 1, h, :], SB[:, m, h, :], c512[:, 4 + h:5 + h], Bm[:, m, h, :], ALU.mult, ALU.add)
            P.cp("act", Sf_bf[:], Sf[:])
            tap("Sf", Sf[:].rearrange("p h d -> p (h d)"))
            tap("SB0", SB[:, 0].rearrange("p h d -> p (h d)"))
            ck(1)
            A.top = s1

            wp0 = A.top
            wpool[:] = [A.alloc([128, 8, 512], BF16) for _ in range(3)]
            _sv = A.top
            A.top = wp0
            wd2 = A.alloc([128, 22, 512], BF16)
            assert A.top <= _sv
            A.top = _sv
            hlT = A.alloc([128, 8, 512], BF16)
            xbufs = [A.alloc([128, D], BF16) for _ in range(4)]
            ropeT = A.alloc([128, 4, 2, 128], F32)
            mropeT = A.alloc([128, 4, 2, 32], F32)
            sgrT = A.alloc([128, 8, 512], BF16)

            def issue_x2(m):
                for j in range(4):
                    P.dma("pool", out=xbufs[j], in_=x_own[m * 512 + j * 128:m * 512 + (j + 1) * 128, :])

            def evac_T(dst, mtab, c0):
                for kc in range(KC):
                    srcp = psb[kc // 2][:, (kc % 2) * 512:(kc % 2) * 512 + 512]
                    if kc % 2 == 0:
                        P.act(dst[:, kc, :], srcp, AF.Identity, scale=mtab[:, kc, c0:c0 + 1], bias=mtab[:, kc, c0 + 1:c0 + 2])
                    else:
                        P.I("dve", "tensor_scalar", out=dst[:, kc, :], in0=srcp, scalar1=mtab[:, kc, c0:c0 + 1],
                            scalar2=mtab[:, kc, c0 + 1:c0 + 2], op0=ALU.mult, op1=ALU.add)

            issue_x2(0)
            sgmT = A.alloc([128, 8, 512], BF16)
            dqnT = A.alloc([128, 3, 512], BF16)
            s2 = A.top
            for m in range(4):
                A.top = s2
                rows = slice(m * 512, (m + 1) * 512)
                for j in range(4):
                    for kc in range(KC):
                        P.tr(psb[kc // 2][:, (kc % 2) * 512 + j * 128:(kc % 2) * 512 + (j + 1) * 128],
                             xbufs[j][:, kc * 128:(kc + 1) * 128], ident[:], inc=(kc == KC - 1))
                if m + 1 < 4:
                    issue_x2(m + 1)
                evac_T(hlT, m1, 0)
                P.dma("sp", out=ropeT[:], in_=rope_own_d[rows].rearrange("(j p) t d -> p j t d", p=128))
                P.dma("sp", out=mropeT[:], in_=mrope_own_d[rows].rearrange("(j p) t d -> p j t d", p=128))
                pa = A.top
                q_r = A.alloc([128, 4, 4, 128], BF16)
                k_r = A.alloc([128, 4, 4, 128], BF16)
                Kf = A.alloc([128, 4, 4, 128], BF16)
                Kb = A.alloc([128, 4, 4, 128], BF16)
                V = A.alloc([128, 4, 1024], BF16)
                sg = A.alloc([128, 4, 1024], BF16)
                t1 = A.alloc([128, 4, 128], F32)
                t2 = A.alloc([128, 4, 128], F32)
                r32 = A.alloc([128, 4, 128], F32)
                junk = A.alloc([128, 512], F32)
                dqn = A.alloc([128, 384], BF16)
                QT = A.alloc([128, 4, 128], BF16)
                KT = A.alloc([128, 4, 128], BF16)
                QdfT = A.alloc([128, 4, 128], BF16)
                QdbT = A.alloc([128, 4, 128], BF16)
                AT = A.alloc([128, 4, 128], BF16)
                on = A.alloc([128, 1024], F32)
                gated = A.alloc([128, 1024], BF16)
                retgT = A.alloc([128, 8, 512], BF16)
                Sb_ch = A.alloc([128, 4, 4, 256], BF16)
                Sbw = A.alloc([128, 4, 256], F32)
                stt6 = A.alloc([128, 4, 6], F32)
                mv = A.alloc([128, 4, 2], F32)
                rs4 = A.alloc([128, 4], F32)
                nb4 = A.alloc([128, 4], F32)
                ssq = small[:, 48:49]
                rstd = small[:, 49:50]
                wk = [ps[0], ps[1], ps[2], ps[3]]
                wi = [0]

                def nextbank():
                    b = wk[wi[0] % 4]
                    wi[0] += 1
                    return b

                W = wload(w_in_s[:, :, RQ:RQ + 512])
                for j in range(4):
                    pt = nextbank()
                    for kc in range(KC):
                        P.mm(pt[:], hlT[:, kc, j * 128:(j + 1) * 128], W[:, kc, :], start=(kc == 0), stop=(kc == KC - 1), inc=(kc == KC - 1))
                    rope_tm(pt[:].rearrange("p (h d) -> p h d", d=128), 4, ropeT[:, j, 0, :], ropeT[:, j, 1, :], t1, t2, r32)
                    P.cp("act", q_r[:, j], r32)
                P.dma("sp", out=k_r[:].rearrange("p j h d -> p j (h d)"), in_=kret_s[rows, :].rearrange("(j p) d -> p j d", p=128))
                P.dma("sp", out=V[:], in_=vret_s[rows, :].rearrange("(j p) d -> p j d", p=128))
                for j in range(4):
                    P.tt("pool", Kf[:, j], k_r[:, j], kd[:, 0, :].unsqueeze(2).to_broadcast([128, 4, 128]), ALU.mult)
                    P.tt("pool", Kb[:, j], k_r[:, j], kd[:, 1, :].unsqueeze(2).to_broadcast([128, 4, 128]), ALU.mult)
                ck(20)
                for hf in range(2):
                    W = wload(w_in_s[:, :, RG + hf * 512:RG + (hf + 1) * 512])
                    for j in range(4):
                        pt = nextbank()
                        for kc in range(KC):
                            P.mm(pt[:], hlT[:, kc, j * 128:(j + 1) * 128], W[:, kc, :], start=(kc == 0), stop=(kc == KC - 1), inc=(kc == KC - 1))
                        P.act(sg[:, j, hf * 512:(hf + 1) * 512], pt[:], AF.Silu)
                ck(21)
                W = wload(w_in_s[:, :, DQ:DQ + 384])
                for j in range(4):
                    pt = nextbank()
                    for kc in range(KC):
                        P.mm(pt[:, 0:384], hlT[:, kc, j * 128:(j + 1) * 128], W[:, kc, 0:384], start=(kc == 0), stop=(kc == KC - 1), inc=(kc == KC - 1))
                    P.memset("pool", ssq, 0.0)
                    P.act(junk[:, 0:384], pt[:, 0:384], AF.Square, accum_out=ssq)
                    P.act(rstd, ssq, AF.Sqrt, scale=1.0 / 384.0, bias=RMS_EPS)
                    P.I("dve", "reciprocal", out=rstd, in_=rstd)
                    P.ts("dve", dqn, pt[:, 0:384], rstd, ALU.mult)
                    for kc in range(3):
                        P.tr(psb[7][:, kc * 128:(kc + 1) * 128], dqn[:, kc * 128:(kc + 1) * 128], ident[:], inc=(kc == 2))
                    P.cp("dve", dqnT[:, :, j * 128:(j + 1) * 128], psb[7][:, 0:384].rearrange("p (k t) -> p k t", t=128))
                ck(22)
                for (dst, c0) in ((sgrT, GR), (sgmT, GM)):
                    for hf in range(2):
                        W = wload(w_in_s[:, :, c0 + hf * 512:c0 + (hf + 1) * 512])
                        for c4 in range(4):
                            pt = nextbank()
                            for kc in range(KC):
                                P.mm(pt[:], W[:, kc, c4 * 128:(c4 + 1) * 128], hlT[:, kc, :], start=(kc == 0), stop=(kc == KC - 1), inc=(kc == KC - 1))
                            P.act(dst[:, hf * 4 + c4, :], pt[:], AF.Sigmoid)

                ck(2)
                accA = [ps[0 + h // 2][:, (h % 2) * 256:(h % 2) * 256 + 256] for h in range(4)]
                oacc = [ps[4 + h // 2][:, (h % 2) * 256:(h % 2) * 256 + 256] for h in range(4)]
                P.cp("pool", Sbw[:], SB[:, m])
                P.cp("act", Sb_ch[:, 3], SB[:, m])
                ck(23)
                for c in (3, 2, 1):
                    for h in range(4):
                        P.mm(accA[h], Kb[:, c, h, :], V[:, c, h * 256:(h + 1) * 256])
                        P.ts("dve", Sbw[:, h, :], Sbw[:, h, :], cdec[:, 4 + h:5 + h], ALU.mult)
                        P.tt("dve", Sbw[:, h, :], accA[h], Sbw[:, h, :], ALU.add)
                        if c == 3 and h == 0:
                            ck(24)
                        if c == 3 and h == 3:
                            ck(28)
                        if c == 2 and h == 0:
                            ck(29)
                    P.cp("dve", Sb_ch[:, c - 1], Sbw[:])
                ck(25)
                for c in range(4):
                    if c == 1:
                        ck(26)
                    for h in range(4):
                        P.tr(psb[7][:, h * 128:(h + 1) * 128], q_r[:, c, h, :], ident[:], inc=False)
                    for h in range(4):
                        P.tr(psb[7][:, (4 + h) * 128:(5 + h) * 128], k_r[:, c, h, :], ident[:], inc=(h == 3))
                    p7q = psb[7][:, 0:512].rearrange("p (h t) -> p h t", t=128)
                    p7k = psb[7][:, 512:1024].rearrange("p (h t) -> p h t", t=128)
                    P.cp("act", QT, p7q)
                    P.cp("act", KT, p7k)
                    P.tt("dve", QdfT, p7q, qd[:, 0], ALU.mult)
                    P.tt("dve", QdbT, p7q, qd[:, 1], ALU.mult)
                    for h in range(4):
                        P.mm(ps[6][:, h * 128:(h + 1) * 128], KT[:, h, :], QT[:, h, :], inc=(h == 3))
                    P.tt("dve", AT, ps[6][:].rearrange("p (h t) -> p h t", t=128), Mcomb[:], ALU.mult)
                    for h in range(4):
                        P.mm(oacc[h], AT[:, h, :], V[:, c, h * 256:(h + 1) * 256], start=True, stop=False, inc=False)
                        P.mm(oacc[h], QdfT[:, h, :], Sf_bf[:, h, :], start=False, stop=False, inc=False)
                        P.mm(oacc[h], QdbT[:, h, :], Sb_ch[:, c, h, :], start=False, stop=True, inc=True)
                    for h in range(4):
                        P.I("dve", "bn_stats", out=stt6[:, h, :], in_=oacc[h])
                        P.I("dve", "bn_aggr", out=mv[:, h, :], in_=stt6[:, h, :])
                    P.act(rs4, mv[:, :, 1], AF.Sqrt, bias=LN_EPS)
                    P.I("dve", "reciprocal", out=rs4, in_=rs4)
                    P.stt("dve", nb4, mv[:, :, 0], -1.0, rs4, ALU.mult, ALU.mult)
                    for h in range(4):
                        P.act(on[:, h * 256:(h + 1) * 256], oacc[h], AF.Identity, scale=rs4[:, h:h + 1], bias=nb4[:, h:h + 1])
                    P.tt("pool", gated, on, sg[:, c, :], ALU.mult)
                    for kc in range(KC):
                        P.tr(psb[7][:, kc * 128:(kc + 1) * 128], gated[:, kc * 128:(kc + 1) * 128], ident[:], inc=(kc == KC - 1))
                    P.cp("act", retgT[:, :, c * 128:(c + 1) * 128], psb[7][:].rearrange("p (k t) -> p k t", t=128))
                    for h in range(4):
                        P.mm(accA[h], Kf[:, c, h, :], V[:, c, h * 256:(h + 1) * 256])
                        P.ts("dve", Sf[:, h, :], Sf[:, h, :], cdec[:, h:h + 1], ALU.mult)
                        P.tt("dve", Sf[:, h, :], accA[h], Sf[:, h, :], ALU.add)
                    P.cp("act", Sf_bf[:], Sf[:])
                ck(27)
                for hf in range(2):
                    W = wload(w_ret_o_s[:, :, hf * 512:(hf + 1) * 512])
                    for c4 in range(4):
                        pt = nextbank()
                        for kc in range(KC):
                            P.mm(pt[:], W[:, kc, c4 * 128:(c4 + 1) * 128], retgT[:, kc, :], start=(kc == 0), stop=(kc == KC - 1), inc=(kc == KC - 1))
                        P.tt("dve", sgrT[:, hf * 4 + c4, :], pt[:], sgrT[:, hf * 4 + c4, :], ALU.mult)
                if m == 0:
                    tap("m1T", sgrT[:, 0, :])

                ck(3)
                A.top = pa
                q_sb = A.alloc([128, 8, 96], F32)
                q_bf = A.alloc([128, 8, 97], BF16)
                qt1 = A.alloc([128, 8, 32], F32)
                qt2 = A.alloc([128, 8, 32], F32)
                QTa = A.alloc([128, 8, 512], BF16)
                pT = [A.alloc([128, 512], BF16) for _ in range(5)]
                o_sb = A.alloc([128, 512], F32)
                rrow = A.alloc([128, 512], F32)
                OTn = A.alloc([128, 8, 512], BF16)
                mtmp = A.alloc([128, 512], F32)
                kbuf = [A.alloc([128, NKEY], BF16) for _ in range(2)]
                vbuf = [A.alloc([128, NKB, 65], BF16) for _ in range(2)]
                P.memset("pool", q_bf[:, :, 96:97], 0.0)
                for j in range(4):
                    for hf in range(2):
                        pt = ps[hf]
                        for kc in range(3):
                            P.mm(pt[:, 0:384], dqnT[:, kc, j * 128:(j + 1) * 128], wuq[:, kc, hf * 384:(hf + 1) * 384],
                                 start=(kc == 0), stop=(kc == 2), inc=(kc == 2))
                        P.cp("act", q_sb[:, hf * 4:(hf + 1) * 4, :], pt[:, 0:384].rearrange("p (h d) -> p h d", d=96))
                    mrope(q_sb[:, :, 64:96], 8, mropeT[:, j, 0, :], mropeT[:, j, 1, :], qt1, qt2, q_bf[:, :, 64:96])
                    P.cp("pool", q_bf[:, :, 0:64], q_sb[:, :, 0:64])
                    for h in range(8):
                        P.tr(psb[7][0:97, h * 128:(h + 1) * 128], q_bf[:, h, :], ident[:], inc=(h == 7))
                    P.cp("dve", QTa[0:97, :, j * 128:(j + 1) * 128], psb[7][0:97, :].rearrange("p (h t) -> p h t", t=128))
                pend = [None]
                psc = [ps[5], ps[6], ps[0], ps[1], ps[2]]
                LA = 3
                for h in range(8):
                    KTh = kbuf[h % 2]
                    Vh = vbuf[h % 2]
                    P.dma("sp", out=KTh[0:96, :], in_=ktaug_s[h, 0:96, :])
                    P.dma("sp", out=KTh[96:97, :], in_=ktaug_s[h, 96:97, :])
                    P.dma("sp", out=Vh[:], in_=vaug_s[h])
                    po = (ps[4] if h % 2 == 0 else ps[7])[0:65, :]

                    def S(c):
                        P.mm(psc[c % 5][:], KTh[0:97, c * 128:(c + 1) * 128], QTa[0:97, h, :])

                    for c in range(LA):
                        S(c)
                    for c in range(NKB):
                        if c + LA < NKB:
                            S(c + LA)
                        P.act(pT[c % 5], psc[c % 5][:], AF.Exp, scale=ASCALE)
                        P.mm(po, Vh[:, c, :], pT[c % 5], start=(c == 0), stop=(c == NKB - 1), inc=(c == NKB - 1))
                        if c == 4 and pend[0] is not None:
                            pend[0]()
                            pend[0] = None
                    P.I("dve", "reciprocal", out=rrow[64:65, :], in_=po[64:65, :])
                    P.cp("dve", o_sb[0:64, :], po[0:64, :])

                    def epi(h=h):
                        P.mm(ps[3][0:64, :], ones_f[64:65, 0:64], rrow[64:65, :])
                        P.tt("dve", OTn[0:64, h, :], ps[3][0:64, :], o_sb[0:64, :], ALU.mult)

                    pend[0] = epi
                pend[0]()
                for hf in range(2):
                    W = wload(w_mla_o_s[:, :, hf * 512:(hf + 1) * 512], rows=64)
                    for c4 in range(4):
                        pt = nextbank()
                        for h in range(8):
                            P.mm(pt[:], W[0:64, h, c4 * 128:(c4 + 1) * 128], OTn[0:64, h, :], start=(h == 0), stop=(h == 7), inc=(h == 7))
                        oc = hf * 4 + c4
                        P.tt("dve", mtmp, pt[:], sgmT[:, oc, :], ALU.mult)
                        P.tt("pool", sgrT[:, oc, :], mtmp, sgrT[:, oc, :], ALU.add)
                if m == 0:
                    tap("mergedT", sgrT[:, 0, :])

                ck(4)
                A.top = pa
                ty = A.alloc([128, 4, D], F32)
                xr = A.alloc([128, D], F32)
                xn = A.alloc([128, D], F32)
                xnb = [A.alloc([128, D], BF16) for _ in range(2)]
                actT = A.alloc([128, 22, 512], BF16)
                sa = [A.alloc([128, 512], F32) for _ in range(2)]
                wd = A.alloc([128, 22, 512], BF16)
                st2 = A.alloc([128, 2, 6], F32)
                mv2 = A.alloc([128, 2], F32)
                rs1 = small[:, 50:51]
                nb1 = small[:, 51:52]
                for hf in range(2):
                    W = wload(w_out_s[:, :, hf * 512:(hf + 1) * 512])
                    for j in range(4):
                        pt = nextbank()
                        for kc in range(KC):
                            P.mm(pt[:], sgrT[:, kc, j * 128:(j + 1) * 128], W[:, kc, :], start=(kc == 0), stop=(kc == KC - 1), inc=(kc == KC - 1))
                        P.tt("dve", ty[:, j, hf * 512:(hf + 1) * 512], pt[:], g12[:, 0, hf * 512:(hf + 1) * 512], ALU.mult)

                def layer_norm(buf, gi, xn_out):
                    for hf in range(2):
                        P.I("dve", "bn_stats", out=st2[:, hf, :], in_=buf[:, hf * 512:(hf + 1) * 512])
                    P.I("dve", "bn_aggr", out=mv2, in_=st2[:].rearrange("p a b -> p (a b)"))
                    P.act(rs1, mv2[:, 1:2], AF.Sqrt, bias=LN_EPS)
                    P.I("dve", "reciprocal", out=rs1, in_=rs1)
                    P.stt("dve", nb1, mv2[:, 0:1], -1.0, rs1, ALU.mult, ALU.mult)
                    P.act(xn_out, buf, AF.Identity, scale=rs1, bias=nb1)
                    P.tt("pool", buf, xn_out, lnbc[:, gi, :], ALU.mult)
                    P.tt("pool", buf, buf, lnbc[:, gi + 1, :], ALU.add)

                for j in range(4):
                    P.dma("sp", out=xr, in_=x_own[m * 512 + j * 128: m * 512 + (j + 1) * 128, :])
                    P.stt("dve", ty[:, j, :], xr, ALPHA, ty[:, j, :], ALU.mult, ALU.add)
                    layer_norm(ty[:, j, :], 0, xn)
                    xb = xnb[j % 2]
                    P.cp("act", xb, xn)
                    for kc in range(KC):
                        P.tr(psb[kc // 2][:, (kc % 2) * 512 + j * 128:(kc % 2) * 512 + (j + 1) * 128],
                             xb[:, kc * 128:(kc + 1) * 128], ident[:])
                evac_T(hlT, a2b2, 0)
                if m == 0:
                    tap("x1", ty[:, 0, :])
                for i in range(6):
                    ncol = 512 if i < 5 else 256
                    Wa = wload(w_gu_s[:, :, i * 512:i * 512 + ncol])
                    Wb = wload(w_gu_s[:, :, DFF + i * 512:DFF + i * 512 + ncol])
                    for c4 in range(ncol // 128):
                        c = i * 4 + c4
                        pa_, pb_ = (ps[0], ps[1]) if c % 2 == 0 else (ps[2], ps[3])
                        for kc in range(KC):
                            P.mm(pa_[:], Wa[:, kc, c4 * 128:(c4 + 1) * 128], hlT[:, kc, :], start=(kc == 0), stop=(kc == KC - 1), inc=(kc == KC - 1))
                        for kc in range(KC):
                            P.mm(pb_[:], Wb[:, kc, c4 * 128:(c4 + 1) * 128], hlT[:, kc, :], start=(kc == 0), stop=(kc == KC - 1), inc=(kc == KC - 1))
                        P.act(sa[c % 2], pa_[:], AF.Silu)
                        P.tt("dve", actT[:, c, :], pb_[:], sa[c % 2], ALU.mult)
                for hf in range(2):
                    wdx = wd if hf == 0 else wd2
                    P.dma("sp", out=wdx[:], in_=w_down_s[:, :, hf * 512:(hf + 1) * 512])
                    for j in range(4):
                        pt = ps[4 + (j % 2)]
                        for kc in range(22):
                            P.mm(pt[:], actT[:, kc, j * 128:(j + 1) * 128], wdx[:, kc, :], start=(kc == 0), stop=(kc == 21), inc=(kc == 21))
                        P.tt("dve", sa[j % 2], pt[:], g12[:, 1, hf * 512:(hf + 1) * 512], ALU.mult)
                        P.stt("dve", ty[:, j, hf * 512:(hf + 1) * 512], ty[:, j, hf * 512:(hf + 1) * 512], ALPHA, sa[j % 2], ALU.mult, ALU.add)
                for j in range(4):
                    layer_norm(ty[:, j, :], 2, xn)
                    P.dma("sp", out=out_d[m * 512 + j * 128: m * 512 + (j + 1) * 128, :], in_=ty[:, j, :])

        except _Stop:
            pass
        P.finish()
        with nc.Block() as block:
            @block.tensor
            def _(e):
                P.emit("pe", e)

            @block.scalar
            def _(e):
                P.emit("act", e)

            @block.vector
            def _(e):
                P.emit("dve", e)

            @block.gpsimd
            def _(e):
                P.emit("pool", e)

            @block.sync
            def _(e):
                P.emit("sp", e)
    return nc


def _rope_tab(pos):
    half = 64
    freqs = (10000.0 ** (-np.arange(half, dtype=np.float32) / half)).astype(np.float32)
    ang = pos[:, None].astype(np.float32) * freqs[None, :]
    c = np.cos(ang).astype(np.float32)
    s = np.sin(ang).astype(np.float32)
    out = np.zeros((pos.shape[0], 2, 128), np.float32)
    out[:, 0, :64] = c
    out[:, 0, 64:] = c
    out[:, 1, :64] = -s
    out[:, 1, 64:] = s
    return out


def _mrope_tab(pos):
    row = (pos // 64).astype(np.float32)
    col = (pos % 64).astype(np.float32)
    freqs = (10000.0 ** (-np.arange(8, dtype=np.float32) / 8)).astype(np.float32)
    out = np.zeros((pos.shape[0], 2, 32), np.float32)
    for b, p in enumerate((row, col)):
        ang = p[:, None] * freqs[None, :]
        c = np.cos(ang).astype(np.float32)
        s = np.sin(ang).astype(np.float32)
        out[:, 0, b * 16:b * 16 + 8] = c
        out[:, 0, b * 16 + 8:b * 16 + 16] = c
        out[:, 1, b * 16:b * 16 + 8] = -s
        out[:, 1, b * 16 + 8:b * 16 + 16] = s
    return out


def _dexp_core(dexp, h):
    d = dexp.copy()
    p = np.arange(128)
    for b in range(16):
        jj = b * 128 + p
        d[:, 2 + b, 1] = jj if h == 0 else 2047 - jj
    return d


def _col(v, n):
    return np.ascontiguousarray(np.asarray(v, np.float32).reshape(n, 128).T)


def make_in_maps(x, c, ctx, c_ctx, w_ada, b_ada, w_in, ret_decay_f, ret_decay_b, w_ret_o, mla_q_norm, w_uq,
                 mla_kv_norm, w_ukv, w_mla_o, w_out, ln1_g, ln1_b, w_gu, w_down, ln2_g, ln2_b):
    f = lambda a: np.ascontiguousarray(np.asarray(a, np.float32))
    x = f(x); c = f(c); ctx = f(ctx); c_ctx = f(c_ctx)
    p = np.arange(128)
    iot = np.stack([p, 127 - p, p + 1, 128 - p], 1).astype(np.float32)
    i = np.arange(128)[None, :]
    j = np.arange(128)[:, None]
    masktab = np.stack([np.maximum(i - j, 0), np.maximum(j - i, 0), (i >= j), (j >= i)], 1).astype(np.float32)
    rowt = np.zeros((128, 2, 128), np.float32)
    rowt[:, 0, :] = np.arange(128)[None, :] + 1
    rowt[:, 1, :] = 128 - np.arange(128)[None, :]
    dexp = np.zeros((128, NKB, 2), np.float32)
    for b in range(2):
        jj = b * 128 + p
        dexp[:, b, 0] = 255 - jj
        dexp[:, b, 1] = jj
    for b in range(16):
        jj = b * 128 + p
        dexp[:, 2 + b, 0] = 2047 - jj
        dexp[:, 2 + b, 1] = jj
        dexp[:, 18 + b, 0] = 0
        dexp[:, 18 + b, 1] = (b % 4) * 128 + p
    shared = {
        "w_ada": f(w_ada[0]), "bada_col": _col(b_ada[0], 48), "bada_row": f(b_ada[0]).reshape(1, -1),
        "w_in": f(w_in[0]), "w_ret_o": f(w_ret_o[0]), "w_uq": f(w_uq[0]), "w_ukv": f(w_ukv[0]),
        "w_mla_o": f(w_mla_o[0]), "w_out": f(w_out[0]), "w_gu": f(w_gu[0]), "w_down": f(w_down[0]),
        "qn_col": _col(mla_q_norm[0], 3), "kvn_col": _col(mla_kv_norm[0], 2),
        "ln1_col": np.ascontiguousarray(np.stack([_col(ln1_g[0], 8), _col(ln1_b[0], 8)], 2)),
        "lnrows": np.ascontiguousarray(np.stack([f(ln1_g[0]), f(ln1_b[0]), f(ln2_g[0]), f(ln2_b[0])], 0)),
        "dec": np.concatenate([f(ret_decay_f[0]), f(ret_decay_b[0])]).reshape(1, 8),
        "iot": iot, "masktab": masktab, "rowt": rowt, "ident": np.eye(128, dtype=np.float32),
    }
    maps = []
    for core in range(8):
        b, h = core // 2, core % 2
        own = np.arange(h * LH, (h + 1) * LH)
        oth = np.arange((1 - h) * LH, (2 - h) * LH)
        fl = np.zeros((128, 4), np.float32)
        fl[:, 0] = h
        fl[:, 1] = 1 - h
        fl[:, 2] = 1 - h
        fl[:, 3] = h
        mp = dict(shared)
        mp.update({
            "x_own": np.ascontiguousarray(x[b, own]), "x_oth": np.ascontiguousarray(x[b, oth]),
            "ctx": np.ascontiguousarray(ctx[b]),
            "ccol": np.ascontiguousarray(np.stack([_col(c[b], 8), _col(c_ctx, 8)], 2)),
            "flags": fl,
            "dexp": _dexp_core(dexp, h),
            "rope_own": _rope_tab(own), "rope_oth": _rope_tab(oth),
            "mrope_own": _mrope_tab(own), "mrope_oth": _mrope_tab(oth),
        })
        maps.append(mp)
    return maps


_NC_CACHE = {}


def kernel(**inputs):
    if "nc" not in _NC_CACHE:
        _NC_CACHE["nc"] = build()
    nc = _NC_CACHE["nc"]
    in_maps = make_in_maps(**inputs)
    res = run_bass_kernel_spmd(nc, in_maps, core_ids=list(range(8)))
    out = np.zeros((4, 4096, D), np.float32)
    for core in range(8):
        b, h = core // 2, core % 2
        out[b, h * LH:(h + 1) * LH] = np.asarray(res.results[core]["out"], np.float32)
    return out
```

```python
import math
from contextlib import ExitStack

import numpy as np
import concourse.bass as bass
import concourse.mybir as mybir
from concourse.bass_utils import run_bass_kernel_spmd

F32 = mybir.dt.float32
BF16 = mybir.dt.bfloat16
AF = mybir.ActivationFunctionType
ALU = mybir.AluOpType
AX = mybir.AxisListType

D = 1024
KC = 8
LH = 2048
NCTX = 256
NKEY = 4352
NKB = 34
IN_W = 5792
RQ, RK, RV, RG, DQ, DKV, KR, GR, GM = 0, 512, 1024, 2048, 3072, 3456, 3712, 3744, 4768
DFF = 2816
LN_EPS = 1e-5
RMS_EPS = 1e-6
ALPHA = 2.0 ** 0.25
KSCALE = 128.0 ** -0.5
ASCALE = 96.0 ** -0.5
ARN = 35968

ENGS = ("pe", "act", "dve", "pool", "sp")
WKEYS = ("out", "accum_out", "ap")


def _isap(v):
    return hasattr(v, "tensor") and hasattr(v, "offset") and hasattr(v, "ap")


class Prog:
    def __init__(self, nc, esem, dsem):
        self.nc = nc
        self.esem = esem
        self.dsem = dsem
        self.dcnt = [0] * len(dsem)
        self.dnext = {"sp": 0, "pool": 0, "act": 0}
        self.dper = len(dsem) // 2
        self.ops = {e: [] for e in ENGS}
        self.cnt = {e: 0 for e in ENGS}
        self.seen = {e: {} for e in ENGS}
        self.recs = {}

    @staticmethod
    def _range(ap):
        name = ap.tensor.name
        pairs = ap.ap
        ds = mybir.dt.size(ap.dtype)
        if str(ap.space) == "PSUM":
            return name, 0, 2048
        if str(ap.space) == "DRAM":
            off = ap.offset
            dims = pairs
        else:
            pitch = pairs[0][0]
            off = ap.offset % pitch if pitch > 0 else ap.offset
            dims = pairs[1:]
        ext = 0
        for s, c in dims:
            ext += (c - 1) * abs(s)
        return name, off * ds, (off + ext + 1) * ds

    def _deps(self, eng, reads, writes, ev):
        need = {}

        def want(k, v):
            if k == ev[0] and v >= ev[1]:
                return
            if need.get(k, 0) < v:
                need[k] = v

        rr = [self._range(a) for a in reads]
        wr = [self._range(a) for a in writes]
        for name, lo, hi in rr:
            for rec in self.recs.get(name, ()):
                if rec[0] < hi and lo < rec[1] and rec[2] is not None:
                    want(*rec[2])
        for name, lo, hi in wr:
            for rec in self.recs.get(name, ()):
                if rec[0] < hi and lo < rec[1]:
                    if rec[2] is not None:
                        want(*rec[2])
                    for k, v in rec[3].items():
                        want(k, v)
        for name, lo, hi in rr:
            lst = self.recs.setdefault(name, [])
            hit = False
            for rec in lst:
                if rec[0] < hi and lo < rec[1]:
                    hit = True
                    if rec[3].get(ev[0], 0) < ev[1]:
                        rec[3][ev[0]] = ev[1]
            if not hit:
                lst.append([lo, hi, None, {ev[0]: ev[1]}])
        for name, lo, hi in wr:
            lst = self.recs.setdefault(name, [])
            new = []
            for rec in lst:
                if rec[0] < hi and lo < rec[1]:
                    if rec[0] < lo:
                        new.append([rec[0], lo, rec[2], dict(rec[3])])
                    if hi < rec[1]:
                        new.append([hi, rec[1], rec[2], dict(rec[3])])
                else:
                    new.append(rec)
            new.append([lo, hi, ev, {}])
            self.recs[name] = new
        waits = []
        for k, v in need.items():
            if k == ("e", "pe") and eng == "pe":
                continue
            if self.seen[eng].get(k, 0) >= v:
                continue
            self.seen[eng][k] = v
            waits.append((k, v))
        return waits

    def I(self, eng, meth, inc=True, **kw):
        reads, writes = [], []
        for k, v in kw.items():
            if _isap(v):
                (writes if k in WKEYS else reads).append(v)
        writes = writes + [a for a in reads if str(a.space) == "PSUM"]
        ev = (("e", eng), self.cnt[eng] + 1)
        waits = self._deps(eng, reads, writes, ev)
        if inc:
            self.cnt[eng] += 1
        self.ops[eng].append((waits, meth, kw, self.esem[eng] if inc else None, 1))

    def dma(self, q, out, in_, **kw):
        base = 0 if q == "sp" else self.dper
        idx = base + self.dnext[q]
        self.dnext[q] = (self.dnext[q] + 1) % self.dper
        prev = self.dcnt[idx]
        self.dcnt[idx] += 16
        ev = (("d", idx), self.dcnt[idx])
        waits = self._deps(q, [in_], [out], ev)
        if prev > 0 and self.seen[q].get(("d", idx), 0) < prev:
            self.seen[q][("d", idx)] = prev
            waits.append((("d", idx), prev))
        kw = dict(kw)
        kw["out"] = out
        kw["in_"] = in_
        self.ops[q].append((waits, "dma_start", kw, self.dsem[idx], 16))

    def finish(self):
        waits = [(("d", i), c) for i, c in enumerate(self.dcnt) if c > 0]
        self.ops["sp"].append((waits, None, None, None, 0))

    def emit(self, eng, e):
        for waits, meth, kw, sem, amt in self.ops[eng]:
            for (kind, key), val in waits:
                s = self.esem[key] if kind == "e" else self.dsem[key]
                e.wait_ge(s, val)
            if meth is None:
                continue
            ins = getattr(e, meth)(**kw)
            if sem is not None:
                ins.then_inc(sem, amt)

    def mm(self, out, lhsT, rhs, start=True, stop=True, inc=True):
        self.I("pe", "matmul", inc=inc, out=out, lhsT=lhsT, rhs=rhs, start=start, stop=stop)

    def tr(self, out, in_, ident, inc=True):
        self.I("pe", "transpose", inc=inc, out=out, in_=in_, identity=ident)

    def act(self, out, in_, func, **kw):
        self.I("act", "activation", out=out, in_=in_, func=func, **kw)

    def tt(self, eng, out, in0, in1, op):
        self.I(eng, "tensor_tensor", out=out, in0=in0, in1=in1, op=op)

    def ts(self, eng, out, in0, s1, op0, s2=None, op1=None):
        if op1 is None:
            self.I(eng, "tensor_scalar", out=out, in0=in0, scalar1=s1, scalar2=None, op0=op0)
        else:
            self.I(eng, "tensor_scalar", out=out, in0=in0, scalar1=s1, scalar2=s2, op0=op0, op1=op1)

    def stt(self, eng, out, in0, scalar, in1, op0, op1):
        self.I(eng, "scalar_tensor_tensor", out=out, in0=in0, scalar=scalar, in1=in1, op0=op0, op1=op1)

    def cp(self, eng, out, in_):
        if eng == "act":
            self.I("act", "activation", out=out, in_=in_, func=AF.Copy)
        else:
            self.I(eng, "tensor_copy", out=out, in_=in_)

    def memset(self, eng, ap, val):
        self.I(eng, "memset", ap=ap, constant=val)


def bc(ap, shape, axis):
    return ap.unsqueeze(axis).to_broadcast(list(shape))


class _Stop(Exception):
    pass


def build(debug=None, stop=99):
    debug = debug or {}

    def ck(k):
        if stop == k:
            raise _Stop()
    nc = bass.Bass("TRN2", target_bir_lowering=False)

    def din(name, shape, dt=F32):
        return nc.dram_tensor(name, list(shape), dt, kind="ExternalInput").ap()

    def dscr(name, shape, dt=BF16):
        return nc.dram_tensor(name, list(shape), dt, kind="Internal").ap()

    x_own = din("x_own", [LH, D])
    x_oth = din("x_oth", [LH, D])
    ctx_d = din("ctx", [NCTX, D])
    ccol_d = din("ccol", [128, 8, 2])
    w_ada_d = din("w_ada", [D, 6 * D])
    bada_col_d = din("bada_col", [128, 48])
    bada_row_d = din("bada_row", [1, 6 * D])
    w_in_d = din("w_in", [D, IN_W])
    w_ret_o_d = din("w_ret_o", [D, D])
    w_uq_d = din("w_uq", [384, 768])
    w_ukv_d = din("w_ukv", [256, 1024])
    w_mla_o_d = din("w_mla_o", [512, D])
    w_out_d = din("w_out", [D, D])
    w_gu_d = din("w_gu", [D, 2 * DFF])
    w_down_d = din("w_down", [DFF, D])
    qn_col_d = din("qn_col", [128, 3])
    kvn_col_d = din("kvn_col", [128, 2])
    ln1_col_d = din("ln1_col", [128, 8, 2])
    lnrows_d = din("lnrows", [4, D])
    dec_d = din("dec", [1, 8])
    flags_d = din("flags", [128, 4])
    rope_own_d = din("rope_own", [LH, 2, 128])
    rope_oth_d = din("rope_oth", [LH, 2, 128])
    mrope_own_d = din("mrope_own", [LH, 2, 32])
    mrope_oth_d = din("mrope_oth", [LH, 2, 32])
    dexp_d = din("dexp", [128, NKB, 2])
    iot_d = din("iot", [128, 4])
    masktab_d = din("masktab", [128, 4, 128])
    rowt_d = din("rowt", [128, 2, 128])
    ident_d = din("ident", [128, 128])
    out_d = nc.dram_tensor("out", [LH, D], F32, kind="ExternalOutput").ap()
    dbg_d = {k: nc.dram_tensor("dbg_" + k, list(shp), F32, kind="ExternalOutput").ap()
             for k, shp in debug.items()}

    w_in_s0 = dscr("w_in_s", [D, IN_W])
    w_in_s = w_in_s0.rearrange("(kc p) n -> p kc n", p=128)
    w_gu_s0 = dscr("w_gu_s", [D, 2 * DFF])
    w_gu_s = w_gu_s0.rearrange("(kc p) n -> p kc n", p=128)
    w_down_s0 = dscr("w_down_s", [DFF, D])
    w_down_s = w_down_s0.rearrange("(kc p) n -> p kc n", p=128)
    w_ret_o_s0 = dscr("w_ret_o_s", [D, D])
    w_ret_o_s = w_ret_o_s0.rearrange("(kc p) n -> p kc n", p=128)
    w_out_s0 = dscr("w_out_s", [D, D])
    w_out_s = w_out_s0.rearrange("(kc p) n -> p kc n", p=128)
    w_mla_o_s0 = dscr("w_mla_o_s", [512, D])
    w_mla_o_s = w_mla_o_s0.rearrange("(h p) n -> p h n", p=64)
    ktaug_s = dscr("ktaug_s", [8, 97, NKEY])
    kret_s = dscr("kret_s", [LH, 512])
    vret_s = dscr("vret_s", [LH, 1024])
    vaug_s = dscr("vaug_s", [8, 128, NKB, 65])

    with ExitStack() as es:
        def sb(name, shape, dt):
            return es.enter_context(nc.sbuf_tensor("s_" + name, list(shape), dt))

        ps = [es.enter_context(nc.psum_tensor("ps%d" % i, [128, 512], F32)) for i in range(8)]
        psb = [p[:].bitcast(BF16) for p in ps]
        esem = {e: es.enter_context(nc.semaphore("sem_" + e)) for e in ENGS}
        dsem = [es.enter_context(nc.semaphore("dsem%d" % i)) for i in range(24)]
        P = Prog(nc, esem, dsem)

        ident_f = sb("ident_f", [128, 128], F32)
        ident = sb("ident", [128, 128], BF16)
        ones_f = sb("ones_f", [128, 128], F32)
        lnbc = sb("lnbc", [128, 4, D], F32)
        g12 = sb("g12", [128, 2, D], F32)
        adaT = sb("adaT", [128, 48, 2], F32)
        m1 = sb("m1", [128, 8, 4], F32)
        a2b2 = sb("a2b2", [128, 8, 2], F32)
        lg = sb("lg", [128, 8], F32)
        cdec = sb("cdec", [128, 8], F32)
        c512 = sb("c512", [128, 8], F32)
        c2048 = sb("c2048", [128, 8], F32)
        kd = sb("kd", [128, 2, 4], F32)
        Dtab = sb("Dtab", [128, NKB, 2, 4], F32)
        qd = sb("qd", [128, 2, 4, 128], F32)
        Mcomb = sb("Mcomb", [128, 4, 128], F32)
        flags = sb("flags", [128, 4], F32)
        wfb = sb("wfb", [128, 8], F32)
        wuq = sb("wuq", [128, 3, 768], BF16)
        wukv = sb("wukv", [128, 2, 1024], BF16)
        Sf = sb("Sf", [128, 4, 256], F32)
        Sf_bf = sb("Sf_bf", [128, 4, 256], BF16)
        SB = sb("SB", [128, 4, 4, 256], F32)
        small = sb("small", [128, 64], F32)
        wpool = []
        arena = sb("arena", [128, ARN], F32)
        arena_b = arena[:].bitcast(BF16)

        class Arena:
            def __init__(self):
                self.top = 0

            def alloc(self, shape, dt):
                n = 1
                for s in shape[1:]:
                    n *= s
                ds = mybir.dt.size(dt)
                nb = (n * ds + 31) // 32 * 32
                lo = self.top
                self.top += nb
                assert self.top <= ARN * 4, "arena overflow %d" % self.top
                if dt == F32:
                    v = arena[0:shape[0], lo // 4: lo // 4 + n]
                else:
                    v = arena_b[0:shape[0], lo // 2: lo // 2 + n]
                if len(shape) == 2:
                    return v
                if len(shape) == 3:
                    return v.rearrange("p (a b) -> p a b", b=shape[2])
                if len(shape) == 4:
                    return v.rearrange("p (a b c) -> p a b c", b=shape[2], c=shape[3])
                raise ValueError

        A = Arena()
        wctr = [0]

        def wload(src, rows=128):
            b = wpool[wctr[0] % 3]
            wctr[0] += 1
            ncols = src.shape[-1]
            nk = src.shape[1]
            dst = b[0:rows, 0:nk, 0:ncols]
            P.dma("sp", out=dst, in_=src)
            return b

        def tap(name, view):
            if name in dbg_d:
                P.dma("sp" if view.dtype == F32 else "pool", out=dbg_d[name], in_=view)

        try:
            P.dma("sp", out=ident_f[:], in_=ident_d)
            P.cp("dve", ident[:], ident_f[:])
            P.memset("pool", ones_f[:], 1.0)
            P.dma("sp", out=flags[:], in_=flags_d)
            for i in range(4):
                P.dma("sp", out=lnbc[:, i, :], in_=lnrows_d[i, :].partition_broadcast(128))
            def cast_w(dst, src, nsplit):
                n = src.shape[0] // nsplit
                for i in range(nsplit):
                    P.dma("pool", out=dst[i * n:(i + 1) * n, :], in_=src[i * n:(i + 1) * n, :])

            for i in range(4):
                P.dma("pool", out=w_in_s0[i * 256:(i + 1) * 256, RK:RV + 1024], in_=w_in_d[i * 256:(i + 1) * 256, RK:RV + 1024])
            P.dma("pool", out=w_in_s0[:, DKV:GR], in_=w_in_d[:, DKV:GR])

            dect = small[:, 0:8]
            P.dma("sp", out=dect, in_=dec_d[0, :].partition_broadcast(128))
            P.act(small[:, 8:16], dect, AF.Exp, scale=-1.0)
            P.act(small[:, 16:24], small[:, 8:16], AF.Ln, bias=1.0)
            P.ts("dve", lg[:], small[:, 16:24], -1.0, ALU.mult)
            P.act(cdec[:], lg[:], AF.Exp, scale=128.0)
            P.act(c512[:], lg[:], AF.Exp, scale=512.0)
            P.act(c2048[:], lg[:], AF.Exp, scale=2048.0)
            iot = small[:, 24:28]
            P.dma("sp", out=iot, in_=iot_d)
            s0 = A.top
            dexp = A.alloc([128, NKB, 2], F32)
            P.dma("sp", out=dexp, in_=dexp_d)
            masktab = A.alloc([128, 4, 128], F32)
            P.dma("sp", out=masktab, in_=masktab_d)
            rowt = A.alloc([128, 2, 128], F32)
            P.dma("sp", out=rowt, in_=rowt_d)
            etmp = A.alloc([128, 2, 128], F32)
            for h in range(4):
                for d in range(2):
                    lcol = lg[:, d * 4 + h: d * 4 + h + 1]
                    P.act(Dtab[:, :, d, h], dexp[:, :, d], AF.Exp, scale=lcol)
                    P.act(qd[:, d, h, :], rowt[:, d, :], AF.Exp, scale=lcol)
                    P.act(kd[:, d, h: h + 1], iot[:, 1 - d: 2 - d], AF.Exp, scale=lcol)
                    P.act(etmp[:, d, :], masktab[:, d, :], AF.Exp, scale=lcol)
                    P.tt("dve", etmp[:, d, :], etmp[:, d, :], masktab[:, 2 + d, :], ALU.mult)
                P.tt("dve", etmp[:, 0, :], etmp[:, 0, :], etmp[:, 1, :], ALU.add)
                P.ts("dve", Mcomb[:, h, :], etmp[:, 0, :], KSCALE, ALU.mult)
            lgsel = small[:, 52:56]
            P.ts("dve", lgsel, lg[:, 0:4], flags[:, 0:1], ALU.mult)
            P.stt("dve", lgsel, lg[:, 4:8], flags[:, 1:2], lgsel, ALU.mult, ALU.add)
            for h in range(4):
                P.act(Dtab[:, 2:18, 1, h], dexp[:, 2:18, 1], AF.Exp, scale=lgsel[:, h:h + 1])
            P.ts("dve", Dtab[:], Dtab[:], KSCALE, ALU.mult)
            P.ts("dve", kd[:], kd[:], KSCALE, ALU.mult)
            for d in range(2):
                P.stt("dve", wfb[:, d * 4:(d + 1) * 4], c2048[:, d * 4:(d + 1) * 4], flags[:, d:d + 1],
                      flags[:, 2 + d:3 + d].to_broadcast([128, 4]), ALU.mult, ALU.add)

            ccol = A.alloc([128, 8, 2], F32)
            P.dma("sp", out=ccol, in_=ccol_d)
            scb = A.alloc([128, 8, 2], BF16)
            P.act(scb, ccol, AF.Silu)
            screp = A.alloc([128, 8, 128], BF16)
            P.cp("dve", screp, scb[:, :, 0:1].to_broadcast([128, 8, 128]))
            bcol = A.alloc([128, 48], F32)
            P.dma("sp", out=bcol, in_=bada_col_d)
            brow = A.alloc([128, 2, D], F32)
            P.dma("sp", out=brow[:, 0, :], in_=bada_row_d[0, 2048:3072].partition_broadcast(128))
            P.dma("sp", out=brow[:, 1, :], in_=bada_row_d[0, 5120:6144].partition_broadcast(128))
            ln1col = A.alloc([128, 8, 2], F32)
            P.dma("sp", out=ln1col, in_=ln1_col_d)
            wa = [A.alloc([128, 8, 1536], BF16) for _ in range(2)]
            w_ada_v = w_ada_d.rearrange("(kc p) n -> p kc n", p=128)
            ps_ada = ps[0][:, 0:96]
            for pc in range(4):
                w = wa[pc % 2]
                for kc in range(KC):
                    P.dma("pool", out=w[:, kc, :], in_=w_ada_v[:, kc, pc * 1536:(pc + 1) * 1536])
                for ch in range(12):
                    CH = pc * 12 + ch
                    for kc in range(KC):
                        P.mm(ps_ada[:, 2 * CH:2 * CH + 2], w[:, kc, ch * 128:(ch + 1) * 128], scb[:, kc, :],
                             start=(kc == 0), stop=(kc == KC - 1), inc=(kc == KC - 1))
                if pc in (1, 3):
                    gi = 0 if pc == 1 else 1
                    for hf in range(2):
                        pg = ps[1 + hf]
                        for kc in range(KC):
                            P.mm(pg[:], screp[:, kc, :], w[:, kc, 512 + hf * 512: 1024 + hf * 512],
                                 start=(kc == 0), stop=(kc == KC - 1), inc=(kc == KC - 1))
                        P.tt("dve", g12[:, gi, hf * 512:(hf + 1) * 512], pg[:], brow[:, gi, hf * 512:(hf + 1) * 512], ALU.add)
            P.tt("dve", adaT[:], ps_ada.rearrange("p (c t) -> p c t", t=2),
                 bcol.unsqueeze(2).to_broadcast([128, 48, 2]), ALU.add)
            P.ts("dve", m1[:, :, 0], adaT[:, 8:16, 0], 1.0, ALU.add)
            P.cp("dve", m1[:, :, 1], adaT[:, 0:8, 0])
            P.ts("dve", m1[:, :, 2], adaT[:, 8:16, 1], 1.0, ALU.add)
            P.cp("dve", m1[:, :, 3], adaT[:, 0:8, 1])
            s2p = small[:, 32:40]
            P.ts("dve", s2p, adaT[:, 32:40, 0], 1.0, ALU.add)
            P.tt("dve", a2b2[:, :, 0], ln1col[:, :, 0], s2p, ALU.mult)
            P.tt("dve", a2b2[:, :, 1], ln1col[:, :, 1], s2p, ALU.mult)
            P.tt("dve", a2b2[:, :, 1], a2b2[:, :, 1], adaT[:, 24:32, 0], ALU.add)
            tap("adaT", adaT[:].rearrange("p c t -> p (c t)"))
            tap("g12", g12[:, 0, :])

            wtmp = A.alloc([128, 3, 1024], F32)
            qncol = small[:, 40:43]
            kvncol = small[:, 44:46]
            P.dma("sp", out=qncol, in_=qn_col_d)
            P.dma("sp", out=kvncol, in_=kvn_col_d)
            P.dma("sp", out=wtmp[:, :, 0:768], in_=w_uq_d.rearrange("(kc p) n -> p kc n", p=128))
            for kc in range(3):
                P.ts("dve", wuq[:, kc, :], wtmp[:, kc, 0:768], qncol[:, kc:kc + 1], ALU.mult)
            w_ukv_v = w_ukv_d.rearrange("(kc p) (h t d) -> p kc t h d", p=128, t=2, d=64)
            wtmp2 = A.alloc([128, 2, 1024], F32)
            for kc in range(2):
                for t in range(2):
                    P.dma("sp", out=wtmp2[:, kc, t * 512:(t + 1) * 512].rearrange("p (h d) -> p h d", d=64),
                          in_=w_ukv_v[:, kc, t, :, :])
            for kc in range(2):
                P.ts("dve", wukv[:, kc, :], wtmp2[:, kc, :], kvncol[:, kc:kc + 1], ALU.mult)

            A.top = s0
            ck(0)

            def load_hlT(src, nblk, mcol, hlT, xbufs):
                for j in range(nblk):
                    xb = xbufs[j % 2]
                    P.dma("pool", out=xb, in_=src[j * 128:(j + 1) * 128, :])
                    for kc in range(KC):
                        P.tr(psb[kc // 2][:, (kc % 2) * 512 + j * 128:(kc % 2) * 512 + (j + 1) * 128],
                             xb[:, kc * 128:(kc + 1) * 128], ident[:])
                n = nblk * 128
                for kc in range(KC):
                    P.act(hlT[:, kc, 0:n], psb[kc // 2][:, (kc % 2) * 512:(kc % 2) * 512 + n], AF.Identity,
                          scale=m1[:, kc, mcol:mcol + 1], bias=m1[:, kc, mcol + 1:mcol + 2])

            def rope_tm(src, nh, RC, RS, t1, t2, outv):
                P.tt("dve", t1, src, bc(RC, [128, nh, 128], 1), ALU.mult)
                P.tt("dve", t2[:, :, 0:64], src[:, :, 64:128], bc(RS[:, 0:64], [128, nh, 64], 1), ALU.mult)
                P.tt("dve", t2[:, :, 64:128], src[:, :, 0:64], bc(RS[:, 64:128], [128, nh, 64], 1), ALU.mult)
                P.tt("pool", outv, t1, t2, ALU.add)

            def mrope(src, nh, C, S, t1, t2, outv):
                P.tt("dve", t1, src, bc(C, [128, nh, 32], 1), ALU.mult)
                for b in range(2):
                    for hf in range(2):
                        o = b * 16 + hf * 8
                        i = b * 16 + (1 - hf) * 8
                        P.tt("dve", t2[:, :, o:o + 8], src[:, :, i:i + 8], bc(S[:, o:o + 8], [128, nh, 8], 1), ALU.mult)
                P.tt("pool", outv, t1, t2, ALU.add)

            s1 = A.top
            w1 = A.alloc([128, 8, 1824], BF16)
            P.dma("sp", out=w1[:, :, 0:1536], in_=w_in_s[:, :, RK:RV + 1024])
            P.dma("sp", out=w1[:, :, 1536:1824], in_=w_in_s[:, :, DKV:GR])
            hlTs = [A.alloc([128, 8, 512], BF16) for _ in range(2)]
            xsets = [[A.alloc([128, D], BF16) for _ in range(4)] for _ in range(2)]
            ropeTs = [A.alloc([128, 4, 2, 128], F32) for _ in range(2)]
            mropeTs = [A.alloc([128, 4, 2, 32], F32) for _ in range(2)]

            def mkset():
                d = {}
                d["t1"] = A.alloc([128, 4, 128], F32)
                d["t2"] = A.alloc([128, 4, 128], F32)
                d["kr32"] = A.alloc([128, 4, 128], F32)
                d["Kf"] = A.alloc([128, 4, 128], BF16)
                d["Kb"] = A.alloc([128, 4, 128], BF16)
                d["Vr"] = A.alloc([128, 1024], BF16)
                d["junk"] = A.alloc([128, 256], F32)
                d["dkvn"] = A.alloc([128, 256], BF16)
                d["dkvnT"] = A.alloc([128, 2, 128], BF16)
                d["mt1"] = A.alloc([128, 1, 32], F32)
                d["mt2"] = A.alloc([128, 1, 32], F32)
                d["krr"] = A.alloc([128, 1, 32], F32)
                d["kfull"] = A.alloc([128, 8, 97], BF16)
                d["vst"] = A.alloc([128, 8, 65], BF16)
                d["kst"] = A.alloc([128, 8, 128], BF16)
                d["ssq"] = A.alloc([128, 2], F32)
                d["kbf"] = A.alloc([128, 512], BF16)
                return d

            tsets = [mkset(), mkset()]
            Sfc = A.alloc([128, 4, 256], F32)
            Sbc = A.alloc([128, 4, 256], F32)
            Bo = A.alloc([128, 4, 256], F32)
            Bm = A.alloc([128, 4, 4, 256], F32)
            for i in range(2):
                P.memset("pool", tsets[i]["kfull"][:, :, 96:97], 1.0)
                P.memset("pool", tsets[i]["vst"][:, :, 64:65], 1.0)

            cast_q = []
            for (c0, c1) in ((RQ, RK), (RG, DKV), (GR, IN_W)):
                for i in range(4):
                    cast_q.append((w_in_s0[i * 256:(i + 1) * 256, c0:c1], w_in_d[i * 256:(i + 1) * 256, c0:c1]))
            for (dst, src, ns) in ((w_ret_o_s0, w_ret_o_d, 2), (w_mla_o_s0, w_mla_o_d, 1), (w_out_s0, w_out_d, 2),
                                   (w_gu_s0, w_gu_d, 8), (w_down_s0, w_down_d, 5)):
                n = src.shape[0] // ns
                for i in range(ns):
                    r1 = src.shape[0] if i == ns - 1 else (i + 1) * n
                    cast_q.append((dst[i * n:r1, :], src[i * n:r1, :]))

            segs = [("ctx", ctx_d, 2, None, None), ("oth", x_oth, 16, rope_oth_d, mrope_oth_d),
                    ("own", x_own, 16, rope_own_d, mrope_own_d)]
            groups = []
            blocks = []
            gb = 0
            for sname, src, nblk_seg, rope_d, mrope_d in segs:
                for g in range((nblk_seg + 3) // 4):
                    nb = min(4, nblk_seg - g * 4)
                    gi = len(groups)
                    groups.append(dict(sname=sname, src=src[g * 512:g * 512 + nb * 128, :], nb=nb, g=g,
                                       rope=None if rope_d is None else rope_d[g * 512:g * 512 + nb * 128],
                                       mrope=None if mrope_d is None else mrope_d[g * 512:g * 512 + nb * 128],
                                       gb0=gb))
                    for j in range(nb):
                        blk = g * 4 + j
                        first = (blk % 4 == 0) if sname == "own" else (blk == 0)
                        last = (blk % 4 == 3) if sname == "own" else (blk == nblk_seg - 1)
                        blocks.append(dict(gi=gi, j=j, sname=sname, g=g, first=first, last=last, gblk=gb))
                        gb += 1

            def issue_x(gi):
                G = groups[gi]
                for j in range(G["nb"]):
                    P.dma("pool", out=xsets[gi % 2][j], in_=G["src"][j * 128:(j + 1) * 128, :])

            def prep(gi):
                G = groups[gi]
                bs = (G["gb0"] % 2) * 4
                nb = G["nb"]
                mcol = 2 if G["sname"] == "ctx" else 0
                hl = hlTs[gi % 2]
                for j in range(nb):
                    for kc in range(KC):
                        P.tr(psb[bs + kc // 2][:, (kc % 2) * 512 + j * 128:(kc % 2) * 512 + (j + 1) * 128],
                             xsets[gi % 2][j][:, kc * 128:(kc + 1) * 128], ident[:], inc=(kc == KC - 1))
                if gi + 1 < len(groups):
                    issue_x(gi + 1)
                n = nb * 128
                for kc in range(KC):
                    srcp = psb[bs + kc // 2][:, (kc % 2) * 512:(kc % 2) * 512 + n]
                    if kc % 2 == 0:
                        P.act(hl[:, kc, 0:n], srcp, AF.Identity, scale=m1[:, kc, mcol:mcol + 1], bias=m1[:, kc, mcol + 1:mcol + 2])
                    else:
                        P.I("dve", "tensor_scalar", out=hl[:, kc, 0:n], in0=srcp, scalar1=m1[:, kc, mcol:mcol + 1],
                            scalar2=m1[:, kc, mcol + 1:mcol + 2], op0=ALU.mult, op1=ALU.add)
                rT, mT = ropeTs[gi % 2], mropeTs[gi % 2]
                if G["rope"] is None:
                    P.memset("pool", rT[:, :, 0, :], 1.0)
                    P.memset("pool", rT[:, :, 1, :], 0.0)
                    P.memset("pool", mT[:, :, 0, :], 1.0)
                    P.memset("pool", mT[:, :, 1, :], 0.0)
                else:
                    P.dma("sp", out=rT[:, 0:nb], in_=G["rope"].rearrange("(j p) t d -> p j t d", p=128))
                    P.dma("sp", out=mT[:, 0:nb], in_=G["mrope"].rearrange("(j p) t d -> p j t d", p=128))

            IPC = ((0, 0, 512), (1, 512, 512), (2, 1024, 512), (3, 1536, 288))

            def ip_piece(B, k):
                bs = (B["gblk"] % 2) * 4
                bank, c0, cn = IPC[k]
                hl = hlTs[B["gi"] % 2]
                tok = slice(B["j"] * 128, (B["j"] + 1) * 128)
                for kc in range(KC):
                    P.mm(ps[bs + bank][:, 0:cn], hl[:, kc, tok], w1[:, kc, c0:c0 + cn],
                         start=(kc == 0), stop=(kc == KC - 1), inc=(kc == KC - 1))

            def head_ops(B):
                bs = (B["gblk"] % 2) * 4
                T = tsets[B["gblk"] % 2]
                rT, mT = ropeTs[B["gi"] % 2], mropeTs[B["gi"] % 2]
                j = B["j"]
                pk, pv0, pv1, pd = ps[bs], ps[bs + 1], ps[bs + 2], ps[bs + 3]
                rope_tm(pk[:].rearrange("p (h d) -> p h d", d=128), 4, rT[:, j, 0, :], rT[:, j, 1, :], T["t1"], T["t2"], T["kr32"])
                P.cp("act", T["Vr"][:, 0:512], pv0[:])
                P.cp("act", T["Vr"][:, 512:1024], pv1[:])
                dirs = (0, 1) if B["sname"] == "ctx" else (1,)
                for d in dirs:
                    Kx = T["Kf"] if d == 0 else T["Kb"]
                    P.tt("pool", Kx, T["kr32"], Dtab[:, B["gblk"], d, :].unsqueeze(2).to_broadcast([128, 4, 128]), ALU.mult)
                if B["sname"] == "own":
                    r0 = (B["g"] * 4 + j) * 128
                    P.cp("pool", T["kbf"], T["kr32"][:].rearrange("p h d -> p (h d)"))
                    P.dma("sp", out=kret_s[r0:r0 + 128, :], in_=T["kbf"])
                    P.dma("sp", out=vret_s[r0:r0 + 128, :], in_=T["Vr"])
                ssq = T["ssq"][:, 0:1]
                rstd = T["ssq"][:, 1:2]
                P.memset("pool", ssq, 0.0)
                P.act(T["junk"], pd[:, 0:256], AF.Square, accum_out=ssq)
                P.act(rstd, ssq, AF.Sqrt, scale=1.0 / 256.0, bias=RMS_EPS)
                P.I("dve", "reciprocal", out=rstd, in_=rstd)
                P.ts("dve", T["dkvn"], pd[:, 0:256], rstd, ALU.mult)
                mrope(pd[:, 256:288].unsqueeze(1), 1, mT[:, j, 0, :], mT[:, j, 1, :], T["mt1"], T["mt2"], T["krr"])

            def state_part(B, heads, bank):
                T = tsets[B["gblk"] % 2]
                dirs = (0, 1) if B["sname"] == "ctx" else (1,)
                for d in dirs:
                    Kx = T["Kf"] if d == 0 else T["Kb"]
                    if B["sname"] == "ctx":
                        dst = Sfc if d == 0 else Sbc
                    elif B["sname"] == "oth":
                        dst = Bo
                    else:
                        dst = Bm[:, B["g"]]
                    regs = [(h, bank[i][:, 0:256]) for i, h in enumerate(heads)]
                    for h, reg in regs:
                        P.mm(reg, Kx[:, h, :], T["Vr"][:, h * 256:(h + 1) * 256])
                    for h, reg in regs:
                        if B["first"]:
                            P.cp("dve", dst[:, h, :], reg)
                        else:
                            P.tt("dve", dst[:, h, :], reg, dst[:, h, :], ALU.add)

            def chain_steps(B):
                bs = (B["gblk"] % 2) * 4
                T = tsets[B["gblk"] % 2]
                gblk = B["gblk"]

                def c1():
                    state_part(B, (0, 1), (ps[bs], ps[bs + 1]))

                def c2():
                    for kc in range(2):
                        P.tr(psb[bs + 3][:, kc * 128:(kc + 1) * 128], T["dkvn"][:, kc * 128:(kc + 1) * 128], ident[:], inc=(kc == 1))
                    P.cp("dve", T["dkvnT"], psb[bs + 3][:, 0:256].rearrange("p (k t) -> p k t", t=128))

                def c3():
                    for t in range(2):
                        for kc in range(2):
                            P.mm(ps[bs + 1 + t][:], T["dkvnT"][:, kc, :], wukv[:, kc, t * 512:(t + 1) * 512],
                                 start=(kc == 0), stop=(kc == 1), inc=(kc == 1))
                    P.cp("act", T["kfull"][:, :, 0:64], ps[bs + 1][:].rearrange("p (h d) -> p h d", d=64))
                    P.cp("pool", T["kfull"][:, :, 64:96], T["krr"].to_broadcast([128, 8, 32]))
                    P.cp("act", T["vst"][:, :, 0:64], ps[bs + 2][:].rearrange("p (h d) -> p h d", d=64))

                def c4():
                    state_part(B, (2, 3), (ps[bs + 3], ps[bs + 1]))

                def c5():
                    for h in range(8):
                        P.tr(psb[bs][0:97, h * 128:(h + 1) * 128], T["kfull"][:, h, :], ident[:], inc=(h == 7))
                    P.cp("dve", T["kst"][0:97], psb[bs][0:97, :].rearrange("p (h t) -> p h t", t=128))
                    ktv = ktaug_s.rearrange("h r t -> r h t")
                    P.dma("sp", out=ktv[0:96, :, gblk * 128:(gblk + 1) * 128], in_=T["kst"][0:96])
                    P.dma("sp", out=ktv[96:97, :, gblk * 128:(gblk + 1) * 128], in_=T["kst"][96:97])
                    P.dma("sp", out=vaug_s.rearrange("h p c e -> p h c e")[:, :, gblk, :], in_=T["vst"])

                return [c1, c2, c3, c4, c5]

            issue_x(0)
            prep(0)
            for k in range(4):
                ip_piece(blocks[0], k)
            for bi, B in enumerate(blocks):
                head_ops(B)
                nxt = []
                if bi + 1 < len(blocks):
                    NB = blocks[bi + 1]
                    if NB["gi"] != B["gi"]:
                        nxt.append(lambda NB=NB: prep(NB["gi"]))
                    for k in range(4):
                        nxt.append(lambda NB=NB, k=k: ip_piece(NB, k))
                ch = chain_steps(B)
                for st in nxt:
                    st()
                for st in ch:
                    st()
                if B["sname"] != "ctx" and cast_q:
                    d_, s_ = cast_q.pop(0)
                    P.dma("pool", out=d_, in_=s_)
            while cast_q:
                d_, s_ = cast_q.pop(0)
                P.dma("pool", out=d_, in_=s_)
            for h in range(4):
                P.ts("pool", Sf[:, h, :], Sfc[:, h, :], wfb[:, h:h + 1], ALU.mult)
                P.stt("dve", Sf[:, h, :], Bo[:, h, :], flags[:, 0:1], Sf[:, h, :], ALU.mult, ALU.add)
                P.ts("pool", SB[:, 3, h, :], Sbc[:, h, :], wfb[:, 4 + h:5 + h], ALU.mult)
                P.stt("dve", SB[:, 3, h, :], Bo[:, h, :], flags[:, 1:2], SB[:, 3, h, :], ALU.mult, ALU.add)
                for m in (3, 2, 1):
                    P.stt("dve", SB[:, m - 1, h, :], SB[:, m, h, :], c512[:, 4 + h:5 + h], Bm[:, m, h, :], ALU.mult, ALU.add)
            P.cp("act", Sf_bf[:], Sf[:])
            tap("Sf", Sf[:].rearrange("p h d -> p (h d)"))
            tap("SB0", SB[:, 0].rearrange("p h d -> p (h d)"))
            ck(1)
            A.top = s1

            wp0 = A.top
            wpool[:] = [A.alloc([128, 8, 512], BF16) for _ in range(3)]
            _sv = A.top
            A.top = wp0
            wd2 = A.alloc([128, 22, 512], BF16)
            assert A.top <= _sv
            A.top = _sv
            hlT = A.alloc([128, 8, 512], BF16)
            xbufs = [A.alloc([128, D], BF16) for _ in range(4)]
            ropeT = A.alloc([128, 4, 2, 128], F32)
            mropeT = A.alloc([128, 4, 2, 32], F32)
            sgrT = A.alloc([128, 8, 512], BF16)

            def issue_x2(m):
                for j in range(4):
                    P.dma("pool", out=xbufs[j], in_=x_own[m * 512 + j * 128:m * 512 + (j + 1) * 128, :])

            def evac_T(dst, mtab, c0):
                for kc in range(KC):
                    srcp = psb[kc // 2][:, (kc % 2) * 512:(kc % 2) * 512 + 512]
                    if kc % 2 == 0:
                        P.act(dst[:, kc, :], srcp, AF.Identity, scale=mtab[:, kc, c0:c0 + 1], bias=mtab[:, kc, c0 + 1:c0 + 2])
                    else:
                        P.I("dve", "tensor_scalar", out=dst[:, kc, :], in0=srcp, scalar1=mtab[:, kc, c0:c0 + 1],
                            scalar2=mtab[:, kc, c0 + 1:c0 + 2], op0=ALU.mult, op1=ALU.add)

            issue_x2(0)
            sgmT = A.alloc([128, 8, 512], BF16)
            dqnT = A.alloc([128, 3, 512], BF16)
            s2 = A.top
            for m in range(4):
                A.top = s2
                rows = slice(m * 512, (m + 1) * 512)
                for j in range(4):
                    for kc in range(KC):
                        P.tr(psb[kc // 2][:, (kc % 2) * 512 + j * 128:(kc % 2) * 512 + (j + 1) * 128],
                             xbufs[j][:, kc * 128:(kc + 1) * 128], ident[:], inc=(kc == KC - 1))
                if m + 1 < 4:
                    issue_x2(m + 1)
                evac_T(hlT, m1, 0)
                P.dma("sp", out=ropeT[:], in_=rope_own_d[rows].rearrange("(j p) t d -> p j t d", p=128))
                P.dma("sp", out=mropeT[:], in_=mrope_own_d[rows].rearrange("(j p) t d -> p j t d", p=128))
                pa = A.top
                q_r = A.alloc([128, 4, 4, 128], BF16)
                k_r = A.alloc([128, 4, 4, 128], BF16)
                Kf = A.alloc([128, 4, 4, 128], BF16)
                Kb = A.alloc([128, 4, 4, 128], BF16)
                V = A.alloc([128, 4, 1024], BF16)
                sg = A.alloc([128, 4, 1024], BF16)
                t1 = A.alloc([128, 4, 128], F32)
                t2 = A.alloc([128, 4, 128], F32)
                r32 = A.alloc([128, 4, 128], F32)
                junk = A.alloc([128, 512], F32)
                dqn = A.alloc([128, 384], BF16)
                QT = A.alloc([128, 4, 128], BF16)
                KT = A.alloc([128, 4, 128], BF16)
                QdfT = A.alloc([128, 4, 128], BF16)
                QdbT = A.alloc([128, 4, 128], BF16)
                AT = A.alloc([128, 4, 128], BF16)
                on = A.alloc([128, 1024], F32)
                gated = A.alloc([128, 1024], BF16)
                retgT = A.alloc([128, 8, 512], BF16)
                Sb_ch = A.alloc([128, 4, 4, 256], BF16)
                Sbw = A.alloc([128, 4, 256], F32)
                stt6 = A.alloc([128, 4, 6], F32)
                mv = A.alloc([128, 4, 2], F32)
                rs4 = A.alloc([128, 4], F32)
                nb4 = A.alloc([128, 4], F32)
                ssq = small[:, 48:49]
                rstd = small[:, 49:50]
                wk = [ps[0], ps[1], ps[2], ps[3]]
                wi = [0]

                def nextbank():
                    b = wk[wi[0] % 4]
                    wi[0] += 1
                    return b

                W = wload(w_in_s[:, :, RQ:RQ + 512])
                for j in range(4):
                    pt = nextbank()
                    for kc in range(KC):
                        P.mm(pt[:], hlT[:, kc, j * 128:(j + 1) * 128], W[:, kc, :], start=(kc == 0), stop=(kc == KC - 1), inc=(kc == KC - 1))
                    rope_tm(pt[:].rearrange("p (h d) -> p h d", d=128), 4, ropeT[:, j, 0, :], ropeT[:, j, 1, :], t1, t2, r32)
                    P.cp("act", q_r[:, j], r32)
                P.dma("sp", out=k_r[:].rearrange("p j h d -> p j (h d)"), in_=kret_s[rows, :].rearrange("(j p) d -> p j d", p=128))
                P.dma("sp", out=V[:], in_=vret_s[rows, :].rearrange("(j p) d -> p j d", p=128))
                for j in range(4):
                    P.tt("pool", Kf[:, j], k_r[:, j], kd[:, 0, :].unsqueeze(2).to_broadcast([128, 4, 128]), ALU.mult)
                    P.tt("pool", Kb[:, j], k_r[:, j], kd[:, 1, :].unsqueeze(2).to_broadcast([128, 4, 128]), ALU.mult)
                ck(20)
                for hf in range(2):
                    W = wload(w_in_s[:, :, RG + hf * 512:RG + (hf + 1) * 512])
                    for j in range(4):
                        pt = nextbank()
                        for kc in range(KC):
                            P.mm(pt[:], hlT[:, kc, j * 128:(j + 1) * 128], W[:, kc, :], start=(kc == 0), stop=(kc == KC - 1), inc=(kc == KC - 1))
                        P.act(sg[:, j, hf * 512:(hf + 1) * 512], pt[:], AF.Silu)
                ck(21)
                W = wload(w_in_s[:, :, DQ:DQ + 384])
                for j in range(4):
                    pt = nextbank()
                    for kc in range(KC):
                        P.mm(pt[:, 0:384], hlT[:, kc, j * 128:(j + 1) * 128], W[:, kc, 0:384], start=(kc == 0), stop=(kc == KC - 1), inc=(kc == KC - 1))
                    P.memset("pool", ssq, 0.0)
                    P.act(junk[:, 0:384], pt[:, 0:384], AF.Square, accum_out=ssq)
                    P.act(rstd, ssq, AF.Sqrt, scale=1.0 / 384.0, bias=RMS_EPS)
                    P.I("dve", "reciprocal", out=rstd, in_=rstd)
                    P.ts("dve", dqn, pt[:, 0:384], rstd, ALU.mult)
                    for kc in range(3):
                        P.tr(psb[7][:, kc * 128:(kc + 1) * 128], dqn[:, kc * 128:(kc + 1) * 128], ident[:], inc=(kc == 2))
                    P.cp("dve", dqnT[:, :, j * 128:(j + 1) * 128], psb[7][:, 0:384].rearrange("p (k t) -> p k t", t=128))
                ck(22)
                for (dst, c0) in ((sgrT, GR), (sgmT, GM)):
                    for hf in range(2):
                        W = wload(w_in_s[:, :, c0 + hf * 512:c0 + (hf + 1) * 512])
                        for c4 in range(4):
                            pt = nextbank()
                            for kc in range(KC):
                                P.mm(pt[:], W[:, kc, c4 * 128:(c4 + 1) * 128], hlT[:, kc, :], start=(kc == 0), stop=(kc == KC - 1), inc=(kc == KC - 1))
                            P.act(dst[:, hf * 4 + c4, :], pt[:], AF.Sigmoid)

                ck(2)
                accA = [ps[0 + h // 2][:, (h % 2) * 256:(h % 2) * 256 + 256] for h in range(4)]
                oacc = [ps[4 + h // 2][:, (h % 2) * 256:(h % 2) * 256 + 256] for h in range(4)]
                P.cp("pool", Sbw[:], SB[:, m])
                P.cp("act", Sb_ch[:, 3], SB[:, m])
                ck(23)
                for c in (3, 2, 1):
                    for h in range(4):
                        P.mm(accA[h], Kb[:, c, h, :], V[:, c, h * 256:(h + 1) * 256])
                        P.ts("dve", Sbw[:, h, :], Sbw[:, h, :], cdec[:, 4 + h:5 + h], ALU.mult)
                        P.tt("dve", Sbw[:, h, :], accA[h], Sbw[:, h, :], ALU.add)
                        if c == 3 and h == 0:
                            ck(24)
                        if c == 3 and h == 3:
                            ck(28)
                        if c == 2 and h == 0:
                            ck(29)
                    P.cp("dve", Sb_ch[:, c - 1], Sbw[:])
                ck(25)
                for c in range(4):
                    if c == 1:
                        ck(26)
                    for h in range(4):
                        P.tr(psb[7][:, h * 128:(h + 1) * 128], q_r[:, c, h, :], ident[:], inc=False)
                    for h in range(4):
                        P.tr(psb[7][:, (4 + h) * 128:(5 + h) * 128], k_r[:, c, h, :], ident[:], inc=(h == 3))
                    p7q = psb[7][:, 0:512].rearrange("p (h t) -> p h t", t=128)
                    p7k = psb[7][:, 512:1024].rearrange("p (h t) -> p h t", t=128)
                    P.cp("act", QT, p7q)
                    P.cp("act", KT, p7k)
                    P.tt("dve", QdfT, p7q, qd[:, 0], ALU.mult)
                    P.tt("dve", QdbT, p7q, qd[:, 1], ALU.mult)
                    for h in range(4):
                        P.mm(ps[6][:, h * 128:(h + 1) * 128], KT[:, h, :], QT[:, h, :], inc=(h == 3))
                    P.tt("dve", AT, ps[6][:].rearrange("p (h t) -> p h t", t=128), Mcomb[:], ALU.mult)
                    for h in range(4):
                        P.mm(oacc[h], AT[:, h, :], V[:, c, h * 256:(h + 1) * 256], start=True, stop=False, inc=False)
                        P.mm(oacc[h], QdfT[:, h, :], Sf_bf[:, h, :], start=False, stop=False, inc=False)
                        P.mm(oacc[h], QdbT[:, h, :], Sb_ch[:, c, h, :], start=False, stop=True, inc=True)
                    for h in range(4):
                        P.I("dve", "bn_stats", out=stt6[:, h, :], in_=oacc[h])
                        P.I("dve", "bn_aggr", out=mv[:, h, :], in_=stt6[:, h, :])
                    P.act(rs4, mv[:, :, 1], AF.Sqrt, bias=LN_EPS)
                    P.I("dve", "reciprocal", out=rs4, in_=rs4)
                    P.stt("dve", nb4, mv[:, :, 0], -1.0, rs4, ALU.mult, ALU.mult)
                    for h in range(4):
                        P.act(on[:, h * 256:(h + 1) * 256], oacc[h], AF.Identity, scale=rs4[:, h:h + 1], bias=nb4[:, h:h + 1])
                    P.tt("pool", gated, on, sg[:, c, :], ALU.mult)
                    for kc in range(KC):
                        P.tr(psb[7][:, kc * 128:(kc + 1) * 128], gated[:, kc * 128:(kc + 1) * 128], ident[:], inc=(kc == KC - 1))
                    P.cp("act", retgT[:, :, c * 128:(c + 1) * 128], psb[7][:].rearrange("p (k t) -> p k t", t=128))
                    for h in range(4):
                        P.mm(accA[h], Kf[:, c, h, :], V[:, c, h * 256:(h + 1) * 256])
                        P.ts("dve", Sf[:, h, :], Sf[:, h, :], cdec[:, h:h + 1], ALU.mult)
                        P.tt("dve", Sf[:, h, :], accA[h], Sf[:, h, :], ALU.add)
                    P.cp("act", Sf_bf[:], Sf[:])
                ck(27)
                for hf in range(2):
                    W = wload(w_ret_o_s[:, :, hf * 512:(hf + 1) * 512])
                    for c4 in range(4):
                        pt = nextbank()
                        for kc in range(KC):
                            P.mm(pt[:], W[:, kc, c4 * 128:(c4 + 1) * 128], retgT[:, kc, :], start=(kc == 0), stop=(kc == KC - 1), inc=(kc == KC - 1))
                        P.tt("dve", sgrT[:, hf * 4 + c4, :], pt[:], sgrT[:, hf * 4 + c4, :], ALU.mult)
                if m == 0:
                    tap("m1T", sgrT[:, 0, :])

                ck(3)
                A.top = pa
                q_sb = A.alloc([128, 8, 96], F32)
                q_bf = A.alloc([128, 8, 97], BF16)
                qt1 = A.alloc([128, 8, 32], F32)
                qt2 = A.alloc([128, 8, 32], F32)
                QTa = A.alloc([128, 8, 512], BF16)
                pT = [A.alloc([128, 512], BF16) for _ in range(5)]
                o_sb = A.alloc([128, 512], F32)
                rrow = A.alloc([128, 512], F32)
                OTn = A.alloc([128, 8, 512], BF16)
                mtmp = A.alloc([128, 512], F32)
                kbuf = [A.alloc([128, NKEY], BF16) for _ in range(2)]
                vbuf = [A.alloc([128, NKB, 65], BF16) for _ in range(2)]
                P.memset("pool", q_bf[:, :, 96:97], 0.0)
                for j in range(4):
                    for hf in range(2):
                        pt = ps[hf]
                        for kc in range(3):
                            P.mm(pt[:, 0:384], dqnT[:, kc, j * 128:(j + 1) * 128], wuq[:, kc, hf * 384:(hf + 1) * 384],
                                 start=(kc == 0), stop=(kc == 2), inc=(kc == 2))
                        P.cp("act", q_sb[:, hf * 4:(hf + 1) * 4, :], pt[:, 0:384].rearrange("p (h d) -> p h d", d=96))
                    mrope(q_sb[:, :, 64:96], 8, mropeT[:, j, 0, :], mropeT[:, j, 1, :], qt1, qt2, q_bf[:, :, 64:96])
                    P.cp("pool", q_bf[:, :, 0:64], q_sb[:, :, 0:64])
                    for h in range(8):
                        P.tr(psb[7][0:97, h * 128:(h + 1) * 128], q_bf[:, h, :], ident[:], inc=(h == 7))
                    P.cp("dve", QTa[0:97, :, j * 128:(j + 1) * 128], psb[7][0:97, :].rearrange("p (h t) -> p h t", t=128))
                pend = [None]
                psc = [ps[5], ps[6], ps[0], ps[1], ps[2]]
                LA = 3
                for h in range(8):
                    KTh = kbuf[h % 2]
                    Vh = vbuf[h % 2]
                    P.dma("sp", out=KTh[0:96, :], in_=ktaug_s[h, 0:96, :])
                    P.dma("sp", out=KTh[96:97, :], in_=ktaug_s[h, 96:97, :])
                    P.dma("sp", out=Vh[:], in_=vaug_s[h])
                    po = (ps[4] if h % 2 == 0 else ps[7])[0:65, :]

                    def S(c):
                        P.mm(psc[c % 5][:], KTh[0:97, c * 128:(c + 1) * 128], QTa[0:97, h, :])

                    for c in range(LA):
                        S(c)
                    for c in range(NKB):
                        if c + LA < NKB:
                            S(c + LA)
                        P.act(pT[c % 5], psc[c % 5][:], AF.Exp, scale=ASCALE)
                        P.mm(po, Vh[:, c, :], pT[c % 5], start=(c == 0), stop=(c == NKB - 1), inc=(c == NKB - 1))
                        if c == 4 and pend[0] is not None:
                            pend[0]()
                            pend[0] = None
                    P.I("dve", "reciprocal", out=rrow[64:65, :], in_=po[64:65, :])
                    P.cp("dve", o_sb[0:64, :], po[0:64, :])

                    def epi(h=h):
                        P.mm(ps[3][0:64, :], ones_f[64:65, 0:64], rrow[64:65, :])
                        P.tt("dve", OTn[0:64, h, :], ps[3][0:64, :], o_sb[0:64, :], ALU.mult)

                    pend[0] = epi
                pend[0]()
                for hf in range(2):
                    W = wload(w_mla_o_s[:, :, hf * 512:(hf + 1) * 512], rows=64)
                    for c4 in range(4):
                        pt = nextbank()
                        for h in range(8):
                            P.mm(pt[:], W[0:64, h, c4 * 128:(c4 + 1) * 128], OTn[0:64, h, :], start=(h == 0), stop=(h == 7), inc=(h == 7))
                        oc = hf * 4 + c4
                        P.tt("dve", mtmp, pt[:], sgmT[:, oc, :], ALU.mult)
                        P.tt("pool", sgrT[:, oc, :], mtmp, sgrT[:, oc, :], ALU.add)
                if m == 0:
                    tap("mergedT", sgrT[:, 0, :])

                ck(4)
                A.top = pa
                ty = A.alloc([128, 4, D], F32)
                xr = A.alloc([128, D], F32)
                xn = A.alloc([128, D], F32)
                xnb = [A.alloc([128, D], BF16) for _ in range(2)]
                actT = A.alloc([128, 22, 512], BF16)
                sa = [A.alloc([128, 512], F32) for _ in range(2)]
                wd = A.alloc([128, 22, 512], BF16)
                st2 = A.alloc([128, 2, 6], F32)
                mv2 = A.alloc([128, 2], F32)
                rs1 = small[:, 50:51]
                nb1 = small[:, 51:52]
                for hf in range(2):
                    W = wload(w_out_s[:, :, hf * 512:(hf + 1) * 512])
                    for j in range(4):
                        pt = nextbank()
                        for kc in range(KC):
                            P.mm(pt[:], sgrT[:, kc, j * 128:(j + 1) * 128], W[:, kc, :], start=(kc == 0), stop=(kc == KC - 1), inc=(kc == KC - 1))
                        P.tt("dve", ty[:, j, hf * 512:(hf + 1) * 512], pt[:], g12[:, 0, hf * 512:(hf + 1) * 512], ALU.mult)

                def layer_norm(buf, gi, xn_out):
                    for hf in range(2):
                        P.I("dve", "bn_stats", out=st2[:, hf, :], in_=buf[:, hf * 512:(hf + 1) * 512])
                    P.I("dve", "bn_aggr", out=mv2, in_=st2[:].rearrange("p a b -> p (a b)"))
                    P.act(rs1, mv2[:, 1:2], AF.Sqrt, bias=LN_EPS)
                    P.I("dve", "reciprocal", out=rs1, in_=rs1)
                    P.stt("dve", nb1, mv2[:, 0:1], -1.0, rs1, ALU.mult, ALU.mult)
                    P.act(xn_out, buf, AF.Identity, scale=rs1, bias=nb1)
                    P.tt("pool", buf, xn_out, lnbc[:, gi, :], ALU.mult)
                    P.tt("pool", buf, buf, lnbc[:, gi + 1, :], ALU.add)

                for j in range(4):
                    P.dma("sp", out=xr, in_=x_own[m * 512 + j * 128: m * 512 + (j + 1) * 128, :])
                    P.stt("dve", ty[:, j, :], xr, ALPHA, ty[:, j, :], ALU.mult, ALU.add)
                    layer_norm(ty[:, j, :], 0, xn)
                    xb = xnb[j % 2]
                    P.cp("act", xb, xn)
                    for kc in range(KC):
                        P.tr(psb[kc // 2][:, (kc % 2) * 512 + j * 128:(kc % 2) * 512 + (j + 1) * 128],
                             xb[:, kc * 128:(kc + 1) * 128], ident[:])
                evac_T(hlT, a2b2, 0)
                if m == 0:
                    tap("x1", ty[:, 0, :])
                for i in range(6):
                    ncol = 512 if i < 5 else 256
                    Wa = wload(w_gu_s[:, :, i * 512:i * 512 + ncol])
                    Wb = wload(w_gu_s[:, :, DFF + i * 512:DFF + i * 512 + ncol])
                    for c4 in range(ncol // 128):
                        c = i * 4 + c4
                        pa_, pb_ = (ps[0], ps[1]) if c % 2 == 0 else (ps[2], ps[3])
                        for kc in range(KC):
                            P.mm(pa_[:], Wa[:, kc, c4 * 128:(c4 + 1) * 128], hlT[:, kc, :], start=(kc == 0), stop=(kc == KC - 1), inc=(kc == KC - 1))
                        for kc in range(KC):
                            P.mm(pb_[:], Wb[:, kc, c4 * 128:(c4 + 1) * 128], hlT[:, kc, :], start=(kc == 0), stop=(kc == KC - 1), inc=(kc == KC - 1))
                        P.act(sa[c % 2], pa_[:], AF.Silu)
                        P.tt("dve", actT[:, c, :], pb_[:], sa[c % 2], ALU.mult)
                for hf in range(2):
                    wdx = wd if hf == 0 else wd2
                    P.dma("sp", out=wdx[:], in_=w_down_s[:, :, hf * 512:(hf + 1) * 512])
                    for j in range(4):
                        pt = ps[4 + (j % 2)]
                        for kc in range(22):
                            P.mm(pt[:], actT[:, kc, j * 128:(j + 1) * 128], wdx[:, kc, :], start=(kc == 0), stop=(kc == 21), inc=(kc == 21))
                        P.tt("dve", sa[j % 2], pt[:], g12[:, 1, hf * 512:(hf + 1) * 512], ALU.mult)
                        P.stt("dve", ty[:, j, hf * 512:(hf + 1) * 512], ty[:, j, hf * 512:(hf + 1) * 512], ALPHA, sa[j % 2], ALU.mult, ALU.add)
                for j in range(4):
                    layer_norm(ty[:, j, :], 2, xn)
                    P.dma("sp", out=out_d[m * 512 + j * 128: m * 512 + (j + 1) * 128, :], in_=ty[:, j, :])

        except _Stop:
            pass
        P.finish()
        with nc.Block() as block:
            @block.tensor
            def _(e):
                P.emit("pe", e)

            @block.scalar
            def _(e):
                P.emit("act", e)

            @block.vector
            def _(e):
                P.emit("dve", e)

            @block.gpsimd
            def _(e):
                P.emit("pool", e)

            @block.sync
            def _(e):
                P.emit("sp", e)
    return nc


def _rope_tab(pos):
    half = 64
    freqs = (10000.0 ** (-np.arange(half, dtype=np.float32) / half)).astype(np.float32)
    ang = pos[:, None].astype(np.float32) * freqs[None, :]
    c = np.cos(ang).astype(np.float32)
    s = np.sin(ang).astype(np.float32)
    out = np.zeros((pos.shape[0], 2, 128), np.float32)
    out[:, 0, :64] = c
    out[:, 0, 64:] = c
    out[:, 1, :64] = -s
    out[:, 1, 64:] = s
    return out


def _mrope_tab(pos):
    row = (pos // 64).astype(np.float32)
    col = (pos % 64).astype(np.float32)
    freqs = (10000.0 ** (-np.arange(8, dtype=np.float32) / 8)).astype(np.float32)
    out = np.zeros((pos.shape[0], 2, 32), np.float32)
    for b, p in enumerate((row, col)):
        ang = p[:, None] * freqs[None, :]
        c = np.cos(ang).astype(np.float32)
        s = np.sin(ang).astype(np.float32)
        out[:, 0, b * 16:b * 16 + 8] = c
        out[:, 0, b * 16 + 8:b * 16 + 16] = c
        out[:, 1, b * 16:b * 16 + 8] = -s
        out[:, 1, b * 16 + 8:b * 16 + 16] = s
    return out


def _dexp_core(dexp, h):
    d = dexp.copy()
    p = np.arange(128)
    for b in range(16):
        jj = b * 128 + p
        d[:, 2 + b, 1] = jj if h == 0 else 2047 - jj
    return d


def _col(v, n):
    return np.ascontiguousarray(np.asarray(v, np.float32).reshape(n, 128).T)


def make_in_maps(x, c, ctx, c_ctx, w_ada, b_ada, w_in, ret_decay_f, ret_decay_b, w_ret_o, mla_q_norm, w_uq,
                 mla_kv_norm, w_ukv, w_mla_o, w_out, ln1_g, ln1_b, w_gu, w_down, ln2_g, ln2_b):
    f = lambda a: np.ascontiguousarray(np.asarray(a, np.float32))
    x = f(x); c = f(c); ctx = f(ctx); c_ctx = f(c_ctx)
    p = np.arange(128)
    iot = np.stack([p, 127 - p, p + 1, 128 - p], 1).astype(np.float32)
    i = np.arange(128)[None, :]
    j = np.arange(128)[:, None]
    masktab = np.stack([np.maximum(i - j, 0), np.maximum(j - i, 0), (i >= j), (j >= i)], 1).astype(np.float32)
    rowt = np.zeros((128, 2, 128), np.float32)
    rowt[:, 0, :] = np.arange(128)[None, :] + 1
    rowt[:, 1, :] = 128 - np.arange(128)[None, :]
    dexp = np.zeros((128, NKB, 2), np.float32)
    for b in range(2):
        jj = b * 128 + p
        dexp[:, b, 0] = 255 - jj
        dexp[:, b, 1] = jj
    for b in range(16):
        jj = b * 128 + p
        dexp[:, 2 + b, 0] = 2047 - jj
        dexp[:, 2 + b, 1] = jj
        dexp[:, 18 + b, 0] = 0
        dexp[:, 18 + b, 1] = (b % 4) * 128 + p
    shared = {
        "w_ada": f(w_ada[0]), "bada_col": _col(b_ada[0], 48), "bada_row": f(b_ada[0]).reshape(1, -1),
        "w_in": f(w_in[0]), "w_ret_o": f(w_ret_o[0]), "w_uq": f(w_uq[0]), "w_ukv": f(w_ukv[0]),
        "w_mla_o": f(w_mla_o[0]), "w_out": f(w_out[0]), "w_gu": f(w_gu[0]), "w_down": f(w_down[0]),
        "qn_col": _col(mla_q_norm[0], 3), "kvn_col": _col(mla_kv_norm[0], 2),
        "ln1_col": np.ascontiguousarray(np.stack([_col(ln1_g[0], 8), _col(ln1_b[0], 8)], 2)),
        "lnrows": np.ascontiguousarray(np.stack([f(ln1_g[0]), f(ln1_b[0]), f(ln2_g[0]), f(ln2_b[0])], 0)),
        "dec": np.concatenate([f(ret_decay_f[0]), f(ret_decay_b[0])]).reshape(1, 8),
        "iot": iot, "masktab": masktab, "rowt": rowt, "ident": np.eye(128, dtype=np.float32),
    }
    maps = []
    for core in range(8):
        b, h = core // 2, core % 2
        own = np.arange(h * LH, (h + 1) * LH)
        oth = np.arange((1 - h) * LH, (2 - h) * LH)
        fl = np.zeros((128, 4), np.float32)
        fl[:, 0] = h
        fl[:, 1] = 1 - h
        fl[:, 2] = 1 - h
        fl[:, 3] = h
        mp = dict(shared)
        mp.update({
            "x_own": np.ascontiguousarray(x[b, own]), "x_oth": np.ascontiguousarray(x[b, oth]),
            "ctx": np.ascontiguousarray(ctx[b]),
            "ccol": np.ascontiguousarray(np.stack([_col(c[b], 8), _col(c_ctx, 8)], 2)),
            "flags": fl,
            "dexp": _dexp_core(dexp, h),
            "rope_own": _rope_tab(own), "rope_oth": _rope_tab(oth),
            "mrope_own": _mrope_tab(own), "mrope_oth": _mrope_tab(oth),
        })
        maps.append(mp)
    return maps


_NC_CACHE = {}


def kernel(**inputs):
    if "nc" not in _NC_CACHE:
        _NC_CACHE["nc"] = build()
    nc = _NC_CACHE["nc"]
    in_maps = make_in_maps(**inputs)
    res = run_bass_kernel_spmd(nc, in_maps, core_ids=list(range(8)))
    out = np.zeros((4, 4096, D), np.float32)
    for core in range(8):
        b, h = core // 2, core % 2
        out[b, h * LH:(h + 1) * LH] = np.asarray(res.results[core]["out"], np.float32)
    return out
```

```python
import math
from contextlib import ExitStack

import numpy as np
import concourse.bass as bass
import concourse.mybir as mybir
from concourse.bass_utils import run_bass_kernel_spmd

F32 = mybir.dt.float32
BF16 = mybir.dt.bfloat16
AF = mybir.ActivationFunctionType
ALU = mybir.AluOpType
AX = mybir.AxisListType

D = 1024
KC = 8
LH = 2048
NCTX = 256
NKEY = 4352
NKB = 34
IN_W = 5792
RQ, RK, RV, RG, DQ, DKV, KR, GR, GM = 0, 512, 1024, 2048, 3072, 3456, 3712, 3744, 4768
DFF = 2816
LN_EPS = 1e-5
RMS_EPS = 1e-6
ALPHA = 2.0 ** 0.25
KSCALE = 128.0 ** -0.5
ASCALE = 96.0 ** -0.5
ARN = 35968

ENGS = ("pe", "act", "dve", "pool", "sp")
WKEYS = ("out", "accum_out", "ap")


def _isap(v):
    return hasattr(v, "tensor") and hasattr(v, "offset") and hasattr(v, "ap")


class Prog:
    def __init__(self, nc, esem, dsem):
        self.nc = nc
        self.esem = esem
        self.dsem = dsem
        self.dcnt = [0] * len(dsem)
        self.dnext = {"sp": 0, "pool": 0, "act": 0}
        self.dper = len(dsem) // 2
        self.ops = {e: [] for e in ENGS}
        self.cnt = {e: 0 for e in ENGS}
        self.seen = {e: {} for e in ENGS}
        self.recs = {}

    @staticmethod
    def _range(ap):
        name = ap.tensor.name
        pairs = ap.ap
        ds = mybir.dt.size(ap.dtype)
        if str(ap.space) == "PSUM":
            return name, 0, 2048
        if str(ap.space) == "DRAM":
            off = ap.offset
            dims = pairs
        else:
            pitch = pairs[0][0]
            off = ap.offset % pitch if pitch > 0 else ap.offset
            dims = pairs[1:]
        ext = 0
        for s, c in dims:
            ext += (c - 1) * abs(s)
        return name, off * ds, (off + ext + 1) * ds

    def _deps(self, eng, reads, writes, ev):
        need = {}

        def want(k, v):
            if k == ev[0] and v >= ev[1]:
                return
            if need.get(k, 0) < v:
                need[k] = v

        rr = [self._range(a) for a in reads]
        wr = [self._range(a) for a in writes]
        for name, lo, hi in rr:
            for rec in self.recs.get(name, ()):
                if rec[0] < hi and lo < rec[1] and rec[2] is not None:
                    want(*rec[2])
        for name, lo, hi in wr:
            for rec in self.recs.get(name, ()):
                if rec[0] < hi and lo < rec[1]:
                    if rec[2] is not None:
                        want(*rec[2])
                    for k, v in rec[3].items():
                        want(k, v)
        for name, lo, hi in rr:
            lst = self.recs.setdefault(name, [])
            hit = False
            for rec in lst:
                if rec[0] < hi and lo < rec[1]:
                    hit = True
                    if rec[3].get(ev[0], 0) < ev[1]:
                        rec[3][ev[0]] = ev[1]
            if not hit:
                lst.append([lo, hi, None, {ev[0]: ev[1]}])
        for name, lo, hi in wr:
            lst = self.recs.setdefault(name, [])
            new = []
            for rec in lst:
                if rec[0] < hi and lo < rec[1]:
                    if rec[0] < lo:
                        new.append([rec[0], lo, rec[2], dict(rec[3])])
                    if hi < rec[1]:
                        new.append([hi, rec[1], rec[2], dict(rec[3])])
                else:
                    new.append(rec)
            new.append([lo, hi, ev, {}])
            self.recs[name] = new
        waits = []
        for k, v in need.items():
            if k == ("e", "pe") and eng == "pe":
                continue
            if self.seen[eng].get(k, 0) >= v:
                continue
            self.seen[eng][k] = v
            waits.append((k, v))
        return waits

    def I(self, eng, meth, inc=True, **kw):
        reads, writes = [], []
        for k, v in kw.items():
            if _isap(v):
                (writes if k in WKEYS else reads).append(v)
        writes = writes + [a for a in reads if str(a.space) == "PSUM"]
        ev = (("e", eng), self.cnt[eng] + 1)
        waits = self._deps(eng, reads, writes, ev)
        if inc:
            self.cnt[eng] += 1
        self.ops[eng].append((waits, meth, kw, self.esem[eng] if inc else None, 1))

    def dma(self, q, out, in_, **kw):
        base = 0 if q == "sp" else self.dper
        idx = base + self.dnext[q]
        self.dnext[q] = (self.dnext[q] + 1) % self.dper
        prev = self.dcnt[idx]
        self.dcnt[idx] += 16
        ev = (("d", idx), self.dcnt[idx])
        waits = self._deps(q, [in_], [out], ev)
        if prev > 0 and self.seen[q].get(("d", idx), 0) < prev:
            self.seen[q][("d", idx)] = prev
            waits.append((("d", idx), prev))
        kw = dict(kw)
        kw["out"] = out
        kw["in_"] = in_
        self.ops[q].append((waits, "dma_start", kw, self.dsem[idx], 16))

    def finish(self):
        waits = [(("d", i), c) for i, c in enumerate(self.dcnt) if c > 0]
        self.ops["sp"].append((waits, None, None, None, 0))

    def emit(self, eng, e):
        for waits, meth, kw, sem, amt in self.ops[eng]:
            for (kind, key), val in waits:
                s = self.esem[key] if kind == "e" else self.dsem[key]
                e.wait_ge(s, val)
            if meth is None:
                continue
            ins = getattr(e, meth)(**kw)
            if sem is not None:
                ins.then_inc(sem, amt)

    def mm(self, out, lhsT, rhs, start=True, stop=True, inc=True):
        self.I("pe", "matmul", inc=inc, out=out, lhsT=lhsT, rhs=rhs, start=start, stop=stop)

    def tr(self, out, in_, ident, inc=True):
        self.I("pe", "transpose", inc=inc, out=out, in_=in_, identity=ident)

    def act(self, out, in_, func, **kw):
        self.I("act", "activation", out=out, in_=in_, func=func, **kw)

    def tt(self, eng, out, in0, in1, op):
        self.I(eng, "tensor_tensor", out=out, in0=in0, in1=in1, op=op)

    def ts(self, eng, out, in0, s1, op0, s2=None, op1=None):
        if op1 is None:
            self.I(eng, "tensor_scalar", out=out, in0=in0, scalar1=s1, scalar2=None, op0=op0)
        else:
            self.I(eng, "tensor_scalar", out=out, in0=in0, scalar1=s1, scalar2=s2, op0=op0, op1=op1)

    def stt(self, eng, out, in0, scalar, in1, op0, op1):
        self.I(eng, "scalar_tensor_tensor", out=out, in0=in0, scalar=scalar, in1=in1, op0=op0, op1=op1)

    def cp(self, eng, out, in_):
        if eng == "act":
            self.I("act", "activation", out=out, in_=in_, func=AF.Copy)
        else:
            self.I(eng, "tensor_copy", out=out, in_=in_)

    def memset(self, eng, ap, val):
        self.I(eng, "memset", ap=ap, constant=val)


def bc(ap, shape, axis):
    return ap.unsqueeze(axis).to_broadcast(list(shape))


class _Stop(Exception):
    pass


def build(debug=None, stop=99):
    debug = debug or {}

    def ck(k):
        if stop == k:
            raise _Stop()
    nc = bass.Bass("TRN2", target_bir_lowering=False)

    def din(name, shape, dt=F32):
        return nc.dram_tensor(name, list(shape), dt, kind="ExternalInput").ap()

    def dscr(name, shape, dt=BF16):
        return nc.dram_tensor(name, list(shape), dt, kind="Internal").ap()

    x_own = din("x_own", [LH, D])
    x_oth = din("x_oth", [LH, D])
    ctx_d = din("ctx", [NCTX, D])
    ccol_d = din("ccol", [128, 8, 2])
    w_ada_d = din("w_ada", [D, 6 * D])
    bada_col_d = din("bada_col", [128, 48])
    bada_row_d = din("bada_row", [1, 6 * D])
    w_in_d = din("w_in", [D, IN_W])
    w_ret_o_d = din("w_ret_o", [D, D])
    w_uq_d = din("w_uq", [384, 768])
    w_ukv_d = din("w_ukv", [256, 1024])
    w_mla_o_d = din("w_mla_o", [512, D])
    w_out_d = din("w_out", [D, D])
    w_gu_d = din("w_gu", [D, 2 * DFF])
    w_down_d = din("w_down", [DFF, D])
    qn_col_d = din("qn_col", [128, 3])
    kvn_col_d = din("kvn_col", [128, 2])
    ln1_col_d = din("ln1_col", [128, 8, 2])
    lnrows_d = din("lnrows", [4, D])
    dec_d = din("dec", [1, 8])
    flags_d = din("flags", [128, 4])
    rope_own_d = din("rope_own", [LH, 2, 128])
    rope_oth_d = din("rope_oth", [LH, 2, 128])
    mrope_own_d = din("mrope_own", [LH, 2, 32])
    mrope_oth_d = din("mrope_oth", [LH, 2, 32])
    dexp_d = din("dexp", [128, NKB, 2])
    iot_d = din("iot", [128, 4])
    masktab_d = din("masktab", [128, 4, 128])
    rowt_d = din("rowt", [128, 2, 128])
    ident_d = din("ident", [128, 128])
    out_d = nc.dram_tensor("out", [LH, D], F32, kind="ExternalOutput").ap()
    dbg_d = {k: nc.dram_tensor("dbg_" + k, list(shp), F32, kind="ExternalOutput").ap()
             for k, shp in debug.items()}

    w_in_s0 = dscr("w_in_s", [D, IN_W])
    w_in_s = w_in_s0.rearrange("(kc p) n -> p kc n", p=128)
    w_gu_s0 = dscr("w_gu_s", [D, 2 * DFF])
    w_gu_s = w_gu_s0.rearrange("(kc p) n -> p kc n", p=128)
    w_down_s0 = dscr("w_down_s", [DFF, D])
    w_down_s = w_down_s0.rearrange("(kc p) n -> p kc n", p=128)
    w_ret_o_s0 = dscr("w_ret_o_s", [D, D])
    w_ret_o_s = w_ret_o_s0.rearrange("(kc p) n -> p kc n", p=128)
    w_out_s0 = dscr("w_out_s", [D, D])
    w_out_s = w_out_s0.rearrange("(kc p) n -> p kc n", p=128)
    w_mla_o_s0 = dscr("w_mla_o_s", [512, D])
    w_mla_o_s = w_mla_o_s0.rearrange("(h p) n -> p h n", p=64)
    ktaug_s = dscr("ktaug_s", [8, 97, NKEY])
    kret_s = dscr("kret_s", [LH, 512])
    vret_s = dscr("vret_s", [LH, 1024])
    vaug_s = dscr("vaug_s", [8, 128, NKB, 65])

    with ExitStack() as es:
        def sb(name, shape, dt):
            return es.enter_context(nc.sbuf_tensor("s_" + name, list(shape), dt))

        ps = [es.enter_context(nc.psum_tensor("ps%d" % i, [128, 512], F32)) for i in range(8)]
        psb = [p[:].bitcast(BF16) for p in ps]
        esem = {e: es.enter_context(nc.semaphore("sem_" + e)) for e in ENGS}
        dsem = [es.enter_context(nc.semaphore("dsem%d" % i)) for i in range(24)]
        P = Prog(nc, esem, dsem)

        ident_f = sb("ident_f", [128, 128], F32)
        ident = sb("ident", [128, 128], BF16)
        ones_f = sb("ones_f", [128, 128], F32)
        lnbc = sb("lnbc", [128, 4, D], F32)
        g12 = sb("g12", [128, 2, D], F32)
        adaT = sb("adaT", [128, 48, 2], F32)
        m1 = sb("m1", [128, 8, 4], F32)
        a2b2 = sb("a2b2", [128, 8, 2], F32)
        lg = sb("lg", [128, 8], F32)
        cdec = sb("cdec", [128, 8], F32)
        c512 = sb("c512", [128, 8], F32)
        c2048 = sb("c2048", [128, 8], F32)
        kd = sb("kd", [128, 2, 4], F32)
        Dtab = sb("Dtab", [128, NKB, 2, 4], F32)
        qd = sb("qd", [128, 2, 4, 128], F32)
        Mcomb = sb("Mcomb", [128, 4, 128], F32)
        flags = sb("flags", [128, 4], F32)
        wfb = sb("wfb", [128, 8], F32)
        wuq = sb("wuq", [128, 3, 768], BF16)
        wukv = sb("wukv", [128, 2, 1024], BF16)
        Sf = sb("Sf", [128, 4, 256], F32)
        Sf_bf = sb("Sf_bf", [128, 4, 256], BF16)
        SB = sb("SB", [128, 4, 4, 256], F32)
        small = sb("small", [128, 64], F32)
        wpool = []
        arena = sb("arena", [128, ARN], F32)
        arena_b = arena[:].bitcast(BF16)

        class Arena:
            def __init__(self):
                self.top = 0

            def alloc(self, shape, dt):
                n = 1
                for s in shape[1:]:
                    n *= s
                ds = mybir.dt.size(dt)
                nb = (n * ds + 31) // 32 * 32
                lo = self.top
                self.top += nb
                assert self.top <= ARN * 4, "arena overflow %d" % self.top
                if dt == F32:
                    v = arena[0:shape[0], lo // 4: lo // 4 + n]
                else:
                    v = arena_b[0:shape[0], lo // 2: lo // 2 + n]
                if len(shape) == 2:
                    return v
                if len(shape) == 3:
                    return v.rearrange("p (a b) -> p a b", b=shape[2])
                if len(shape) == 4:
                    return v.rearrange("p (a b c) -> p a b c", b=shape[2], c=shape[3])
                raise ValueError

        A = Arena()
        wctr = [0]

        def wload(src, rows=128):
            b = wpool[wctr[0] % 3]
            wctr[0] += 1
            ncols = src.shape[-1]
            nk = src.shape[1]
            dst = b[0:rows, 0:nk, 0:ncols]
            P.dma("sp", out=dst, in_=src)
            return b

        def tap(name, view):
            if name in dbg_d:
                P.dma("sp" if view.dtype == F32 else "pool", out=dbg_d[name], in_=view)

        try:
            P.dma("sp", out=ident_f[:], in_=ident_d)
            P.cp("dve", ident[:], ident_f[:])
            P.memset("pool", ones_f[:], 1.0)
            P.dma("sp", out=flags[:], in_=flags_d)
            for i in range(4):
                P.dma("sp", out=lnbc[:, i, :], in_=lnrows_d[i, :].partition_broadcast(128))
            def cast_w(dst, src, nsplit):
                n = src.shape[0] // nsplit
                for i in range(nsplit):
                    P.dma("pool", out=dst[i * n:(i + 1) * n, :], in_=src[i * n:(i + 1) * n, :])

            for i in range(4):
                P.dma("pool", out=w_in_s0[i * 256:(i + 1) * 256, RK:RV + 1024], in_=w_in_d[i * 256:(i + 1) * 256, RK:RV + 1024])
            P.dma("pool", out=w_in_s0[:, DKV:GR], in_=w_in_d[:, DKV:GR])

            dect = small[:, 0:8]
            P.dma("sp", out=dect, in_=dec_d[0, :].partition_broadcast(128))
            P.act(small[:, 8:16], dect, AF.Exp, scale=-1.0)
            P.act(small[:, 16:24], small[:, 8:16], AF.Ln, bias=1.0)
            P.ts("dve", lg[:], small[:, 16:24], -1.0, ALU.mult)
            P.act(cdec[:], lg[:], AF.Exp, scale=128.0)
            P.act(c512[:], lg[:], AF.Exp, scale=512.0)
            P.act(c2048[:], lg[:], AF.Exp, scale=2048.0)
            iot = small[:, 24:28]
            P.dma("sp", out=iot, in_=iot_d)
            s0 = A.top
            dexp = A.alloc([128, NKB, 2], F32)
            P.dma("sp", out=dexp, in_=dexp_d)
            masktab = A.alloc([128, 4, 128], F32)
            P.dma("sp", out=masktab, in_=masktab_d)
            rowt = A.alloc([128, 2, 128], F32)
            P.dma("sp", out=rowt, in_=rowt_d)
            etmp = A.alloc([128, 2, 128], F32)
            for h in range(4):
                for d in range(2):
                    lcol = lg[:, d * 4 + h: d * 4 + h + 1]
                    P.act(Dtab[:, :, d, h], dexp[:, :, d], AF.Exp, scale=lcol)
                    P.act(qd[:, d, h, :], rowt[:, d, :], AF.Exp, scale=lcol)
                    P.act(kd[:, d, h: h + 1], iot[:, 1 - d: 2 - d], AF.Exp, scale=lcol)
                    P.act(etmp[:, d, :], masktab[:, d, :], AF.Exp, scale=lcol)
                    P.tt("dve", etmp[:, d, :], etmp[:, d, :], masktab[:, 2 + d, :], ALU.mult)
                P.tt("dve", etmp[:, 0, :], etmp[:, 0, :], etmp[:, 1, :], ALU.add)
                P.ts("dve", Mcomb[:, h, :], etmp[:, 0, :], KSCALE, ALU.mult)
            lgsel = small[:, 52:56]
            P.ts("dve", lgsel, lg[:, 0:4], flags[:, 0:1], ALU.mult)
            P.stt("dve", lgsel, lg[:, 4:8], flags[:, 1:2], lgsel, ALU.mult, ALU.add)
            for h in range(4):
                P.act(Dtab[:, 2:18, 1, h], dexp[:, 2:18, 1], AF.Exp, scale=lgsel[:, h:h + 1])
            P.ts("dve", Dtab[:], Dtab[:], KSCALE, ALU.mult)
            P.ts("dve", kd[:], kd[:], KSCALE, ALU.mult)
            for d in range(2):
                P.stt("dve", wfb[:, d * 4:(d + 1) * 4], c2048[:, d * 4:(d + 1) * 4], flags[:, d:d + 1],
                      flags[:, 2 + d:3 + d].to_broadcast([128, 4]), ALU.mult, ALU.add)

            ccol = A.alloc([128, 8, 2], F32)
            P.dma("sp", out=ccol, in_=ccol_d)
            scb = A.alloc([128, 8, 2], BF16)
            P.act(scb, ccol, AF.Silu)
            screp = A.alloc([128, 8, 128], BF16)
            P.cp("dve", screp, scb[:, :, 0:1].to_broadcast([128, 8, 128]))
            bcol = A.alloc([128, 48], F32)
            P.dma("sp", out=bcol, in_=bada_col_d)
            brow = A.alloc([128, 2, D], F32)
            P.dma("sp", out=brow[:, 0, :], in_=bada_row_d[0, 2048:3072].partition_broadcast(128))
            P.dma("sp", out=brow[:, 1, :], in_=bada_row_d[0, 5120:6144].partition_broadcast(128))
            ln1col = A.alloc([128, 8, 2], F32)
            P.dma("sp", out=ln1col, in_=ln1_col_d)
            wa = [A.alloc([128, 8, 1536], BF16) for _ in range(2)]
            w_ada_v = w_ada_d.rearrange("(kc p) n -> p kc n", p=128)
            ps_ada = ps[0][:, 0:96]
            for pc in range(4):
                w = wa[pc % 2]
                for kc in range(KC):
                    P.dma("pool", out=w[:, kc, :], in_=w_ada_v[:, kc, pc * 1536:(pc + 1) * 1536])
                for ch in range(12):
                    CH = pc * 12 + ch
                    for kc in range(KC):
                        P.mm(ps_ada[:, 2 * CH:2 * CH + 2], w[:, kc, ch * 128:(ch + 1) * 128], scb[:, kc, :],
                             start=(kc == 0), stop=(kc == KC - 1), inc=(kc == KC - 1))
                if pc in (1, 3):
                    gi = 0 if pc == 1 else 1
                    for hf in range(2):
                        pg = ps[1 + hf]
                        for kc in range(KC):
                            P.mm(pg[:], screp[:, kc, :], w[:, kc, 512 + hf * 512: 1024 + hf * 512],
                                 start=(kc == 0), stop=(kc == KC - 1), inc=(kc == KC - 1))
                        P.tt("dve", g12[:, gi, hf * 512:(hf + 1) * 512], pg[:], brow[:, gi, hf * 512:(hf + 1) * 512], ALU.add)
            P.tt("dve", adaT[:], ps_ada.rearrange("p (c t) -> p c t", t=2),
                 bcol.unsqueeze(2).to_broadcast([128, 48, 2]), ALU.add)
            P.ts("dve", m1[:, :, 0], adaT[:, 8:16, 0], 1.0, ALU.add)
            P.cp("dve", m1[:, :, 1], adaT[:, 0:8, 0])
            P.ts("dve", m1[:, :, 2], adaT[:, 8:16, 1], 1.0, ALU.add)
            P.cp("dve", m1[:, :, 3], adaT[:, 0:8, 1])
            s2p = small[:, 32:40]
            P.ts("dve", s2p, adaT[:, 32:40, 0], 1.0, ALU.add)
            P.tt("dve", a2b2[:, :, 0], ln1col[:, :, 0], s2p, ALU.mult)
            P.tt("dve", a2b2[:, :, 1], ln1col[:, :, 1], s2p, ALU.mult)
            P.tt("dve", a2b2[:, :, 1], a2b2[:, :, 1], adaT[:, 24:32, 0], ALU.add)
            tap("adaT", adaT[:].rearrange("p c t -> p (c t)"))
            tap("g12", g12[:, 0, :])

            wtmp = A.alloc([128, 3, 1024], F32)
            qncol = small[:, 40:43]
            kvncol = small[:, 44:46]
            P.dma("sp", out=qncol, in_=qn_col_d)
            P.dma("sp", out=kvncol, in_=kvn_col_d)
            P.dma("sp", out=wtmp[:, :, 0:768], in_=w_uq_d.rearrange("(kc p) n -> p kc n", p=128))
            for kc in range(3):
                P.ts("dve", wuq[:, kc, :], wtmp[:, kc, 0:768], qncol[:, kc:kc + 1], ALU.mult)
            w_ukv_v = w_ukv_d.rearrange("(kc p) (h t d) -> p kc t h d", p=128, t=2, d=64)
            wtmp2 = A.alloc([128, 2, 1024], F32)
            for kc in range(2):
                for t in range(2):
                    P.dma("sp", out=wtmp2[:, kc, t * 512:(t + 1) * 512].rearrange("p (h d) -> p h d", d=64),
                          in_=w_ukv_v[:, kc, t, :, :])
            for kc in range(2):
                P.ts("dve", wukv[:, kc, :], wtmp2[:, kc, :], kvncol[:, kc:kc + 1], ALU.mult)

            A.top = s0
            ck(0)

            def load_hlT(src, nblk, mcol, hlT, xbufs):
                for j in range(nblk):
                    xb = xbufs[j % 2]
                    P.dma("pool", out=xb, in_=src[j * 128:(j + 1) * 128, :])
                    for kc in range(KC):
                        P.tr(psb[kc // 2][:, (kc % 2) * 512 + j * 128:(kc % 2) * 512 + (j + 1) * 128],
                             xb[:, kc * 128:(kc + 1) * 128], ident[:])
                n = nblk * 128
                for kc in range(KC):
                    P.act(hlT[:, kc, 0:n], psb[kc // 2][:, (kc % 2) * 512:(kc % 2) * 512 + n], AF.Identity,
                          scale=m1[:, kc, mcol:mcol + 1], bias=m1[:, kc, mcol + 1:mcol + 2])

            def rope_tm(src, nh, RC, RS, t1, t2, outv):
                P.tt("dve", t1, src, bc(RC, [128, nh, 128], 1), ALU.mult)
                P.tt("dve", t2[:, :, 0:64], src[:, :, 64:128], bc(RS[:, 0:64], [128, nh, 64], 1), ALU.mult)
                P.tt("dve", t2[:, :, 64:128], src[:, :, 0:64], bc(RS[:, 64:128], [128, nh, 64], 1), ALU.mult)
                P.tt("pool", outv, t1, t2, ALU.add)

            def mrope(src, nh, C, S, t1, t2, outv):
                P.tt("dve", t1, src, bc(C, [128, nh, 32], 1), ALU.mult)
                for b in range(2):
                    for hf in range(2):
                        o = b * 16 + hf * 8
                        i = b * 16 + (1 - hf) * 8
                        P.tt("dve", t2[:, :, o:o + 8], src[:, :, i:i + 8], bc(S[:, o:o + 8], [128, nh, 8], 1), ALU.mult)
                P.tt("pool", outv, t1, t2, ALU.add)

            s1 = A.top
            w1 = A.alloc([128, 8, 1824], BF16)
            P.dma("sp", out=w1[:, :, 0:1536], in_=w_in_s[:, :, RK:RV + 1024])
            P.dma("sp", out=w1[:, :, 1536:1824], in_=w_in_s[:, :, DKV:GR])
            hlTs = [A.alloc([128, 8, 512], BF16) for _ in range(2)]
            xsets = [[A.alloc([128, D], BF16) for _ in range(4)] for _ in range(2)]
            ropeTs = [A.alloc([128, 4, 2, 128], F32) for _ in range(2)]
            mropeTs = [A.alloc([128, 4, 2, 32], F32) for _ in range(2)]

            def mkset():
                d = {}
                d["t1"] = A.alloc([128, 4, 128], F32)
                d["t2"] = A.alloc([128, 4, 128], F32)
                d["kr32"] = A.alloc([128, 4, 128], F32)
                d["Kf"] = A.alloc([128, 4, 128], BF16)
                d["Kb"] = A.alloc([128, 4, 128], BF16)
                d["Vr"] = A.alloc([128, 1024], BF16)
                d["junk"] = A.alloc([128, 256], F32)
                d["dkvn"] = A.alloc([128, 256], BF16)
                d["dkvnT"] = A.alloc([128, 2, 128], BF16)
                d["mt1"] = A.alloc([128, 1, 32], F32)
                d["mt2"] = A.alloc([128, 1, 32], F32)
                d["krr"] = A.alloc([128, 1, 32], F32)
                d["kfull"] = A.alloc([128, 8, 97], BF16)
                d["vst"] = A.alloc([128, 8, 65], BF16)
                d["kst"] = A.alloc([128, 8, 128], BF16)
                d["ssq"] = A.alloc([128, 2], F32)
                d["kbf"] = A.alloc([128, 512], BF16)
                return d

            tsets = [mkset(), mkset()]
            Sfc = A.alloc([128, 4, 256], F32)
            Sbc = A.alloc([128, 4, 256], F32)
            Bo = A.alloc([128, 4, 256], F32)
            Bm = A.alloc([128, 4, 4, 256], F32)
            for i in range(2):
                P.memset("pool", tsets[i]["kfull"][:, :, 96:97], 1.0)
                P.memset("pool", tsets[i]["vst"][:, :, 64:65], 1.0)

            cast_q = []
            for (c0, c1) in ((RQ, RK), (RG, DKV), (GR, IN_W)):
                for i in range(4):
                    cast_q.append((w_in_s0[i * 256:(i + 1) * 256, c0:c1], w_in_d[i * 256:(i + 1) * 256, c0:c1]))
            for (dst, src, ns) in ((w_ret_o_s0, w_ret_o_d, 2), (w_mla_o_s0, w_mla_o_d, 1), (w_out_s0, w_out_d, 2),
                                   (w_gu_s0, w_gu_d, 8), (w_down_s0, w_down_d, 5)):
                n = src.shape[0] // ns
                for i in range(ns):
                    r1 = src.shape[0] if i == ns - 1 else (i + 1) * n
                    cast_q.append((dst[i * n:r1, :], src[i * n:r1, :]))

            segs = [("ctx", ctx_d, 2, None, None), ("oth", x_oth, 16, rope_oth_d, mrope_oth_d),
                    ("own", x_own, 16, rope_own_d, mrope_own_d)]
            groups = []
            blocks = []
            gb = 0
            for sname, src, nblk_seg, rope_d, mrope_d in segs:
                for g in range((nblk_seg + 3) // 4):
                    nb = min(4, nblk_seg - g * 4)
                    gi = len(groups)
                    groups.append(dict(sname=sname, src=src[g * 512:g * 512 + nb * 128, :], nb=nb, g=g,
                                       rope=None if rope_d is None else rope_d[g * 512:g * 512 + nb * 128],
                                       mrope=None if mrope_d is None else mrope_d[g * 512:g * 512 + nb * 128],
                                       gb0=gb))
                    for j in range(nb):
                        blk = g * 4 + j
                        first = (blk % 4 == 0) if sname == "own" else (blk == 0)
                        last = (blk % 4 == 3) if sname == "own" else (blk == nblk_seg - 1)
                        blocks.append(dict(gi=gi, j=j, sname=sname, g=g, first=first, last=last, gblk=gb))
                        gb += 1

            def issue_x(gi):
                G = groups[gi]
                for j in range(G["nb"]):
                    P.dma("pool", out=xsets[gi % 2][j], in_=G["src"][j * 128:(j + 1) * 128, :])

            def prep(gi):
                G = groups[gi]
                bs = (G["gb0"] % 2) * 4
                nb = G["nb"]
                mcol = 2 if G["sname"] == "ctx" else 0
                hl = hlTs[gi % 2]
                for j in range(nb):
                    for kc in range(KC):
                        P.tr(psb[bs + kc // 2][:, (kc % 2) * 512 + j * 128:(kc % 2) * 512 + (j + 1) * 128],
                             xsets[gi % 2][j][:, kc * 128:(kc + 1) * 128], ident[:], inc=(kc == KC - 1))
                if gi + 1 < len(groups):
                    issue_x(gi + 1)
                n = nb * 128
                for kc in range(KC):
                    srcp = psb[bs + kc // 2][:, (kc % 2) * 512:(kc % 2) * 512 + n]
                    if kc % 2 == 0:
                        P.act(hl[:, kc, 0:n], srcp, AF.Identity, scale=m1[:, kc, mcol:mcol + 1], bias=m1[:, kc, mcol + 1:mcol + 2])
                    else:
                        P.I("dve", "tensor_scalar", out=hl[:, kc, 0:n], in0=srcp, scalar1=m1[:, kc, mcol:mcol + 1],
                            scalar2=m1[:, kc, mcol + 1:mcol + 2], op0=ALU.mult, op1=ALU.add)
                rT, mT = ropeTs[gi % 2], mropeTs[gi % 2]
                if G["rope"] is None:
                    P.memset("pool", rT[:, :, 0, :], 1.0)
                    P.memset("pool", rT[:, :, 1, :], 0.0)
                    P.memset("pool", mT[:, :, 0, :], 1.0)
                    P.memset("pool", mT[:, :, 1, :], 0.0)
                else:
                    P.dma("sp", out=rT[:, 0:nb], in_=G["rope"].rearrange("(j p) t d -> p j t d", p=128))
                    P.dma("sp", out=mT[:, 0:nb], in_=G["mrope"].rearrange("(j p) t d -> p j t d", p=128))

            IPC = ((0, 0, 512), (1, 512, 512), (2, 1024, 512), (3, 1536, 288))

            def ip_piece(B, k):
                bs = (B["gblk"] % 2) * 4
                bank, c0, cn = IPC[k]
                hl = hlTs[B["gi"] % 2]
                tok = slice(B["j"] * 128, (B["j"] + 1) * 128)
                for kc in range(KC):
                    P.mm(ps[bs + bank][:, 0:cn], hl[:, kc, tok], w1[:, kc, c0:c0 + cn],
                         start=(kc == 0), stop=(kc == KC - 1), inc=(kc == KC - 1))

            def head_ops(B):
                bs = (B["gblk"] % 2) * 4
                T = tsets[B["gblk"] % 2]
                rT, mT = ropeTs[B["gi"] % 2], mropeTs[B["gi"] % 2]
                j = B["j"]
                pk, pv0, pv1, pd = ps[bs], ps[bs + 1], ps[bs + 2], ps[bs + 3]
                rope_tm(pk[:].rearrange("p (h d) -> p h d", d=128), 4, rT[:, j, 0, :], rT[:, j, 1, :], T["t1"], T["t2"], T["kr32"])
                P.cp("act", T["Vr"][:, 0:512], pv0[:])
                P.cp("act", T["Vr"][:, 512:1024], pv1[:])
                dirs = (0, 1) if B["sname"] == "ctx" else (1,)
                for d in dirs:
                    Kx = T["Kf"] if d == 0 else T["Kb"]
                    P.tt("pool", Kx, T["kr32"], Dtab[:, B["gblk"], d, :].unsqueeze(2).to_broadcast([128, 4, 128]), ALU.mult)
                if B["sname"] == "own":
                    r0 = (B["g"] * 4 + j) * 128
                    P.cp("pool", T["kbf"], T["kr32"][:].rearrange("p h d -> p (h d)"))
                    P.dma("sp", out=kret_s[r0:r0 + 128, :], in_=T["kbf"])
                    P.dma("sp", out=vret_s[r0:r0 + 128, :], in_=T["Vr"])
                ssq = T["ssq"][:, 0:1]
                rstd = T["ssq"][:, 1:2]
                P.memset("pool", ssq, 0.0)
                P.act(T["junk"], pd[:, 0:256], AF.Square, accum_out=ssq)
                P.act(rstd, ssq, AF.Sqrt, scale=1.0 / 256.0, bias=RMS_EPS)
                P.I("dve", "reciprocal", out=rstd, in_=rstd)
                P.ts("dve", T["dkvn"], pd[:, 0:256], rstd, ALU.mult)
                mrope(pd[:, 256:288].unsqueeze(1), 1, mT[:, j, 0, :], mT[:, j, 1, :], T["mt1"], T["mt2"], T["krr"])

            def state_part(B, heads, bank):
                T = tsets[B["gblk"] % 2]
                dirs = (0, 1) if B["sname"] == "ctx" else (1,)
                for d in dirs:
                    Kx = T["Kf"] if d == 0 else T["Kb"]
                    if B["sname"] == "ctx":
                        dst = Sfc if d == 0 else Sbc
                    elif B["sname"] == "oth":
                        dst = Bo
                    else:
                        dst = Bm[:, B["g"]]
                    regs = [(h, bank[i][:, 0:256]) for i, h in enumerate(heads)]
                    for h, reg in regs:
                        P.mm(reg, Kx[:, h, :], T["Vr"][:, h * 256:(h + 1) * 256])
                    for h, reg in regs:
                        if B["first"]:
                            P.cp("dve", dst[:, h, :], reg)
                        else:
                            P.tt("dve", dst[:, h, :], reg, dst[:, h, :], ALU.add)

            def chain_steps(B):
                bs = (B["gblk"] % 2) * 4
                T = tsets[B["gblk"] % 2]
                gblk = B["gblk"]

                def c1():
                    state_part(B, (0, 1), (ps[bs], ps[bs + 1]))

                def c2():
                    for kc in range(2):
                        P.tr(psb[bs + 3][:, kc * 128:(kc + 1) * 128], T["dkvn"][:, kc * 128:(kc + 1) * 128], ident[:], inc=(kc == 1))
                    P.cp("dve", T["dkvnT"], psb[bs + 3][:, 0:256].rearrange("p (k t) -> p k t", t=128))

                def c3():
                    for t in range(2):
                        for kc in range(2):
                            P.mm(ps[bs + 1 + t][:], T["dkvnT"][:, kc, :], wukv[:, kc, t * 512:(t + 1) * 512],
                                 start=(kc == 0), stop=(kc == 1), inc=(kc == 1))
                    P.cp("act", T["kfull"][:, :, 0:64], ps[bs + 1][:].rearrange("p (h d) -> p h d", d=64))
                    P.cp("pool", T["kfull"][:, :, 64:96], T["krr"].to_broadcast([128, 8, 32]))
                    P.cp("act", T["vst"][:, :, 0:64], ps[bs + 2][:].rearrange("p (h d) -> p h d", d=64))

                def c4():
                    state_part(B, (2, 3), (ps[bs + 3], ps[bs + 1]))

                def c5():
                    for h in range(8):
                        P.tr(psb[bs][0:97, h * 128:(h + 1) * 128], T["kfull"][:, h, :], ident[:], inc=(h == 7))
                    P.cp("dve", T["kst"][0:97], psb[bs][0:97, :].rearrange("p (h t) -> p h t", t=128))
                    ktv = ktaug_s.rearrange("h r t -> r h t")
                    P.dma("sp", out=ktv[0:96, :, gblk * 128:(gblk + 1) * 128], in_=T["kst"][0:96])
                    P.dma("sp", out=ktv[96:97, :, gblk * 128:(gblk + 1) * 128], in_=T["kst"][96:97])
                    P.dma("sp", out=vaug_s.rearrange("h p c e -> p h c e")[:, :, gblk, :], in_=T["vst"])

                return [c1, c2, c3, c4, c5]

            issue_x(0)
            prep(0)
            for k in range(4):
                ip_piece(blocks[0], k)
            for bi, B in enumerate(blocks):
                head_ops(B)
                nxt = []
                if bi + 1 < len(blocks):
                    NB = blocks[bi + 1]
                    if NB["gi"] != B["gi"]:
                        nxt.append(lambda NB=NB: prep(NB["gi"]))
                    for k in range(4):
                        nxt.append(lambda NB=NB, k=k: ip_piece(NB, k))
                ch = chain_steps(B)
                for st in nxt:
                    st()
                for st in ch:
                    st()
                if B["sname"] != "ctx" and cast_q:
                    d_, s_ = cast_q.pop(0)
                    P.dma("pool", out=d_, in_=s_)
            while cast_q:
                d_, s_ = cast_q.pop(0)
                P.dma("pool", out=d_, in_=s_)
            for h in range(4):
                P.ts("pool", Sf[:, h, :], Sfc[:, h, :], wfb[:, h:h + 1], ALU.mult)
                P.stt("dve", Sf[:, h, :], Bo[:, h, :], flags[:, 0:1], Sf[:, h, :], ALU.mult, ALU.add)
                P.ts("pool", SB[:, 3, h, :], Sbc[:, h, :], wfb[:, 4 + h:5 + h], ALU.mult)
                P.stt("dve", SB[:, 3, h, :], Bo[:, h, :], flags[:, 1:2], SB[:, 3, h, :], ALU.mult, ALU.add)
                for m in (3, 2, 1):
                    P.stt("dve", SB[:, m - 1, h, :], SB[:, m, h, :], c512[:, 4 + h:5 + h], Bm[:, m, h, :], ALU.mult, ALU.add)
            P.cp("act", Sf_bf[:], Sf[:])
            tap("Sf", Sf[:].rearrange("p h d -> p (h d)"))
            tap("SB0", SB[:, 0].rearrange("p h d -> p (h d)"))
            ck(1)
            A.top = s1

            wp0 = A.top
            wpool[:] = [A.alloc([128, 8, 512], BF16) for _ in range(3)]
            _sv = A.top
            A.top = wp0
            wd2 = A.alloc([128, 22, 512], BF16)
            assert A.top <= _sv
            A.top = _sv
            hlT = A.alloc([128, 8, 512], BF16)
            xbufs = [A.alloc([128, D], BF16) for _ in range(4)]
            ropeT = A.alloc([128, 4, 2, 128], F32)
            mropeT = A.alloc([128, 4, 2, 32], F32)
            sgrT = A.alloc([128, 8, 512], BF16)

            def issue_x2(m):
                for j in range(4):
                    P.dma("pool", out=xbufs[j], in_=x_own[m * 512 + j * 128:m * 512 + (j + 1) * 128, :])

            def evac_T(dst, mtab, c0):
                for kc in range(KC):
                    srcp = psb[kc // 2][:, (kc % 2) * 512:(kc % 2) * 512 + 512]
                    if kc % 2 == 0:
                        P.act(dst[:, kc, :], srcp, AF.Identity, scale=mtab[:, kc, c0:c0 + 1], bias=mtab[:, kc, c0 + 1:c0 + 2])
                    else:
                        P.I("dve", "tensor_scalar", out=dst[:, kc, :], in0=srcp, scalar1=mtab[:, kc, c0:c0 + 1],
                            scalar2=mtab[:, kc, c0 + 1:c0 + 2], op0=ALU.mult, op1=ALU.add)

            issue_x2(0)
            sgmT = A.alloc([128, 8, 512], BF16)
            dqnT = A.alloc([128, 3, 512], BF16)
            s2 = A.top
            for m in range(4):
                A.top = s2
                rows = slice(m * 512, (m + 1) * 512)
                for j in range(4):
                    for kc in range(KC):
                        P.tr(psb[kc // 2][:, (kc % 2) * 512 + j * 128:(kc % 2) * 512 + (j + 1) * 128],
                             xbufs[j][:, kc * 128:(kc + 1) * 128], ident[:], inc=(kc == KC - 1))
                if m + 1 < 4:
                    issue_x2(m + 1)
                evac_T(hlT, m1, 0)
                P.dma("sp", out=ropeT[:], in_=rope_own_d[rows].rearrange("(j p) t d -> p j t d", p=128))
                P.dma("sp", out=mropeT[:], in_=mrope_own_d[rows].rearrange("(j p) t d -> p j t d", p=128))
                pa = A.top
                q_r = A.alloc([128, 4, 4, 128], BF16)
                k_r = A.alloc([128, 4, 4, 128], BF16)
                Kf = A.alloc([128, 4, 4, 128], BF16)
                Kb = A.alloc([128, 4, 4, 128], BF16)
                V = A.alloc([128, 4, 1024], BF16)
                sg = A.alloc([128, 4, 1024], BF16)
                t1 = A.alloc([128, 4, 128], F32)
                t2 = A.alloc([128, 4, 128], F32)
                r32 = A.alloc([128, 4, 128], F32)
                junk = A.alloc([128, 512], F32)
                dqn = A.alloc([128, 384], BF16)
                QT = A.alloc([128, 4, 128], BF16)
                KT = A.alloc([128, 4, 128], BF16)
                QdfT = A.alloc([128, 4, 128], BF16)
                QdbT = A.alloc([128, 4, 128], BF16)
                AT = A.alloc([128, 4, 128], BF16)
                on = A.alloc([128, 1024], F32)
                gated = A.alloc([128, 1024], BF16)
                retgT = A.alloc([128, 8, 512], BF16)
                Sb_ch = A.alloc([128, 4, 4, 256], BF16)
                Sbw = A.alloc([128, 4, 256], F32)
                stt6 = A.alloc([128, 4, 6], F32)
                mv = A.alloc([128, 4, 2], F32)
                rs4 = A.alloc([128, 4], F32)
                nb4 = A.alloc([128, 4], F32)
                ssq = small[:, 48:49]
                rstd = small[:, 49:50]
                wk = [ps[0], ps[1], ps[2], ps[3]]
                wi = [0]

                def nextbank():
                    b = wk[wi[0] % 4]
                    wi[0] += 1
                    return b

                W = wload(w_in_s[:, :, RQ:RQ + 512])
                for j in range(4):
                    pt = nextbank()
                    for kc in range(KC):
                        P.mm(pt[:], hlT[:, kc, j * 128:(j + 1) * 128], W[:, kc, :], start=(kc == 0), stop=(kc == KC - 1), inc=(kc == KC - 1))
                    rope_tm(pt[:].rearrange("p (h d) -> p h d", d=128), 4, ropeT[:, j, 0, :], ropeT[:, j, 1, :], t1, t2, r32)
                    P.cp("act", q_r[:, j], r32)
                P.dma("sp", out=k_r[:].rearrange("p j h d -> p j (h d)"), in_=kret_s[rows, :].rearrange("(j p) d -> p j d", p=128))
                P.dma("sp", out=V[:], in_=vret_s[rows, :].rearrange("(j p) d -> p j d", p=128))
                for j in range(4):
                    P.tt("pool", Kf[:, j], k_r[:, j], kd[:, 0, :].unsqueeze(2).to_broadcast([128, 4, 128]), ALU.mult)
                    P.tt("pool", Kb[:, j], k_r[:, j], kd[:, 1, :].unsqueeze(2).to_broadcast([128, 4, 128]), ALU.mult)
                ck(20)
                for hf in range(2):
                    W = wload(w_in_s[:, :, RG + hf * 512:RG + (hf + 1) * 512])
                    for j in range(4):
                        pt = nextbank()
                        for kc in range(KC):
                            P.mm(pt[:], hlT[:, kc, j * 128:(j + 1) * 128], W[:, kc, :], start=(kc == 0), stop=(kc == KC - 1), inc=(kc == KC - 1))
                        P.act(sg[:, j, hf * 512:(hf + 1) * 512], pt[:], AF.Silu)
                ck(21)
                W = wload(w_in_s[:, :, DQ:DQ + 384])
                for j in range(4):
                    pt = nextbank()
                    for kc in range(KC):
                        P.mm(pt[:, 0:384], hlT[:, kc, j * 128:(j + 1) * 128], W[:, kc, 0:384], start=(kc == 0), stop=(kc == KC - 1), inc=(kc == KC - 1))
                    P.memset("pool", ssq, 0.0)
                    P.act(junk[:, 0:384], pt[:, 0:384], AF.Square, accum_out=ssq)
                    P.act(rstd, ssq, AF.Sqrt, scale=1.0 / 384.0, bias=RMS_EPS)
                    P.I("dve", "reciprocal", out=rstd, in_=rstd)
                    P.ts("dve", dqn, pt[:, 0:384], rstd, ALU.mult)
                    for kc in range(3):
                        P.tr(psb[7][:, kc * 128:(kc + 1) * 128], dqn[:, kc * 128:(kc + 1) * 128], ident[:], inc=(kc == 2))
                    P.cp("dve", dqnT[:, :, j * 128:(j + 1) * 128], psb[7][:, 0:384].rearrange("p (k t) -> p k t", t=128))
                ck(22)
                for (dst, c0) in ((sgrT, GR), (sgmT, GM)):
                    for hf in range(2):
                        W = wload(w_in_s[:, :, c0 + hf * 512:c0 + (hf + 1) * 512])
                        for c4 in range(4):
                            pt = nextbank()
                            for kc in range(KC):
                                P.mm(pt[:], W[:, kc, c4 * 128:(c4 + 1) * 128], hlT[:, kc, :], start=(kc == 0), stop=(kc == KC - 1), inc=(kc == KC - 1))
                            P.act(dst[:, hf * 4 + c4, :], pt[:], AF.Sigmoid)

                ck(2)
                accA = [ps[h][:, 0:256] for h in range(4)]
                oacc = [ps[4 + h // 2][:, (h % 2) * 256:(h % 2) * 256 + 256] for h in range(4)]
                P.cp("pool", Sbw[:], SB[:, m])
                P.cp("act", Sb_ch[:, 3], SB[:, m])
                ck(23)
                for c in (3, 2, 1):
                    for h in range(4):
                        P.mm(accA[h], Kb[:, c, h, :], V[:, c, h * 256:(h + 1) * 256])
                        P.ts("dve", Sbw[:, h, :], Sbw[:, h, :], cdec[:, 4 + h:5 + h], ALU.mult)
                        P.tt("dve", Sbw[:, h, :], accA[h], Sbw[:, h, :], ALU.add)
                        if c == 3 and h == 0:
                            ck(24)
                        if c == 3 and h == 3:
                            ck(28)
                        if c == 2 and h == 0:
                            ck(29)
                    P.cp("dve", Sb_ch[:, c - 1], Sbw[:])
                ck(25)
                for c in range(4):
                    if c == 1:
                        ck(26)
                    for h in range(4):
                        P.tr(psb[7][:, h * 128:(h + 1) * 128], q_r[:, c, h, :], ident[:], inc=False)
                    for h in range(4):
                        P.tr(psb[7][:, (4 + h) * 128:(5 + h) * 128], k_r[:, c, h, :], ident[:], inc=(h == 3))
                    p7q = psb[7][:, 0:512].rearrange("p (h t) -> p h t", t=128)
                    p7k = psb[7][:, 512:1024].rearrange("p (h t) -> p h t", t=128)
                    P.cp("act", QT, p7q)
                    P.cp("act", KT, p7k)
                    P.tt("dve", QdfT, p7q, qd[:, 0], ALU.mult)
                    P.tt("dve", QdbT, p7q, qd[:, 1], ALU.mult)
                    for h in range(4):
                        P.mm(ps[6][:, h * 128:(h + 1) * 128], KT[:, h, :], QT[:, h, :], inc=(h == 3))
                    P.tt("dve", AT, ps[6][:].rearrange("p (h t) -> p h t", t=128), Mcomb[:], ALU.mult)
                    for h in range(4):
                        P.mm(oacc[h], AT[:, h, :], V[:, c, h * 256:(h + 1) * 256], start=True, stop=False, inc=False)
                        P.mm(oacc[h], QdfT[:, h, :], Sf_bf[:, h, :], start=False, stop=False, inc=False)
                        P.mm(oacc[h], QdbT[:, h, :], Sb_ch[:, c, h, :], start=False, stop=True, inc=True)
                    for h in range(4):
                        P.I("dve", "bn_stats", out=stt6[:, h, :], in_=oacc[h])
                        P.I("dve", "bn_aggr", out=mv[:, h, :], in_=stt6[:, h, :])
                    P.act(rs4, mv[:, :, 1], AF.Sqrt, bias=LN_EPS)
                    P.I("dve", "reciprocal", out=rs4, in_=rs4)
                    P.stt("dve", nb4, mv[:, :, 0], -1.0, rs4, ALU.mult, ALU.mult)
                    for h in range(4):
                        P.act(on[:, h * 256:(h + 1) * 256], oacc[h], AF.Identity, scale=rs4[:, h:h + 1], bias=nb4[:, h:h + 1])
                    P.tt("pool", gated, on, sg[:, c, :], ALU.mult)
                    for kc in range(KC):
                        P.tr(psb[7][:, kc * 128:(kc + 1) * 128], gated[:, kc * 128:(kc + 1) * 128], ident[:], inc=(kc == KC - 1))
                    P.cp("act", retgT[:, :, c * 128:(c + 1) * 128], psb[7][:].rearrange("p (k t) -> p k t", t=128))
                    for h in range(4):
                        P.mm(accA[h], Kf[:, c, h, :], V[:, c, h * 256:(h + 1) * 256])
                        P.ts("dve", Sf[:, h, :], Sf[:, h, :], cdec[:, h:h + 1], ALU.mult)
                        P.tt("dve", Sf[:, h, :], accA[h], Sf[:, h, :], ALU.add)
                    P.cp("act", Sf_bf[:], Sf[:])
                ck(27)
                for hf in range(2):
                    W = wload(w_ret_o_s[:, :, hf * 512:(hf + 1) * 512])
                    for c4 in range(4):
                        pt = nextbank()
                        for kc in range(KC):
                            P.mm(pt[:], W[:, kc, c4 * 128:(c4 + 1) * 128], retgT[:, kc, :], start=(kc == 0), stop=(kc == KC - 1), inc=(kc == KC - 1))
                        P.tt("dve", sgrT[:, hf * 4 + c4, :], pt[:], sgrT[:, hf * 4 + c4, :], ALU.mult)
                if m == 0:
                    tap("m1T", sgrT[:, 0, :])

                ck(3)
                A.top = pa
                q_sb = A.alloc([128, 8, 96], F32)
                q_bf = A.alloc([128, 8, 97], BF16)
                qt1 = A.alloc([128, 8, 32], F32)
                qt2 = A.alloc([128, 8, 32], F32)
                QTa = A.alloc([128, 8, 512], BF16)
                pT = [A.alloc([128, 512], BF16) for _ in range(5)]
                o_sb = A.alloc([128, 512], F32)
                rrow = A.alloc([128, 512], F32)
                OTn = A.alloc([128, 8, 512], BF16)
                mtmp = A.alloc([128, 512], F32)
                kbuf = [A.alloc([128, NKEY], BF16) for _ in range(2)]
                vbuf = [A.alloc([128, NKB, 65], BF16) for _ in range(2)]
                P.memset("pool", q_bf[:, :, 96:97], 0.0)
                for j in range(4):
                    for hf in range(2):
                        pt = ps[hf]
                        for kc in range(3):
                            P.mm(pt[:, 0:384], dqnT[:, kc, j * 128:(j + 1) * 128], wuq[:, kc, hf * 384:(hf + 1) * 384],
                                 start=(kc == 0), stop=(kc == 2), inc=(kc == 2))
                        P.cp("act", q_sb[:, hf * 4:(hf + 1) * 4, :], pt[:, 0:384].rearrange("p (h d) -> p h d", d=96))
                    mrope(q_sb[:, :, 64:96], 8, mropeT[:, j, 0, :], mropeT[:, j, 1, :], qt1, qt2, q_bf[:, :, 64:96])
                    P.cp("pool", q_bf[:, :, 0:64], q_sb[:, :, 0:64])
                    for h in range(8):
                        P.tr(psb[7][0:97, h * 128:(h + 1) * 128], q_bf[:, h, :], ident[:], inc=(h == 7))
                    P.cp("dve", QTa[0:97, :, j * 128:(j + 1) * 128], psb[7][0:97, :].rearrange("p (h t) -> p h t", t=128))
                pend = [None]
                psc = [ps[5], ps[6], ps[0], ps[1], ps[2]]
                LA = 3
                for h in range(8):
                    KTh = kbuf[h % 2]
                    Vh = vbuf[h % 2]
                    P.dma("sp", out=KTh[0:96, :], in_=ktaug_s[h, 0:96, :])
                    P.dma("sp", out=KTh[96:97, :], in_=ktaug_s[h, 96:97, :])
                    P.dma("sp", out=Vh[:], in_=vaug_s[h])
                    po = (ps[4] if h % 2 == 0 else ps[7])[0:65, :]

                    def S(c):
                        P.mm(psc[c % 5][:], KTh[0:97, c * 128:(c + 1) * 128], QTa[0:97, h, :])

                    for c in range(LA):
                        S(c)
                    for c in range(NKB):
                        if c + LA < NKB:
                            S(c + LA)
                        P.act(pT[c % 5], psc[c % 5][:], AF.Exp, scale=ASCALE)
                        P.mm(po, Vh[:, c, :], pT[c % 5], start=(c == 0), stop=(c == NKB - 1), inc=(c == NKB - 1))
                        if c == 4 and pend[0] is not None:
                            pend[0]()
                            pend[0] = None
                    P.I("dve", "reciprocal", out=rrow[64:65, :], in_=po[64:65, :])
                    P.cp("dve", o_sb[0:64, :], po[0:64, :])

                    def epi(h=h):
                        P.mm(ps[3][0:64, :], ones_f[64:65, 0:64], rrow[64:65, :])
                        P.tt("dve", OTn[0:64, h, :], ps[3][0:64, :], o_sb[0:64, :], ALU.mult)

                    pend[0] = epi
                pend[0]()
                for hf in range(2):
                    W = wload(w_mla_o_s[:, :, hf * 512:(hf + 1) * 512], rows=64)
                    for c4 in range(4):
                        pt = nextbank()
                        for h in range(8):
                            P.mm(pt[:], W[0:64, h, c4 * 128:(c4 + 1) * 128], OTn[0:64, h, :], start=(h == 0), stop=(h == 7), inc=(h == 7))
                        oc = hf * 4 + c4
                        P.tt("dve", mtmp, pt[:], sgmT[:, oc, :], ALU.mult)
                        P.tt("pool", sgrT[:, oc, :], mtmp, sgrT[:, oc, :], ALU.add)
                if m == 0:
                    tap("mergedT", sgrT[:, 0, :])

                ck(4)
                A.top = pa
                ty = A.alloc([128, 4, D], F32)
                xr = A.alloc([128, D], F32)
                xn = A.alloc([128, D], F32)
                xnb = [A.alloc([128, D], BF16) for _ in range(2)]
                actT = A.alloc([128, 22, 512], BF16)
                sa = [A.alloc([128, 512], F32) for _ in range(2)]
                wd = A.alloc([128, 22, 512], BF16)
                st2 = A.alloc([128, 2, 6], F32)
                mv2 = A.alloc([128, 2], F32)
                rs1 = small[:, 50:51]
                nb1 = small[:, 51:52]
                for hf in range(2):
                    W = wload(w_out_s[:, :, hf * 512:(hf + 1) * 512])
                    for j in range(4):
                        pt = nextbank()
                        for kc in range(KC):
                            P.mm(pt[:], sgrT[:, kc, j * 128:(j + 1) * 128], W[:, kc, :], start=(kc == 0), stop=(kc == KC - 1), inc=(kc == KC - 1))
                        P.tt("dve", ty[:, j, hf * 512:(hf + 1) * 512], pt[:], g12[:, 0, hf * 512:(hf + 1) * 512], ALU.mult)

                def layer_norm(buf, gi, xn_out):
                    for hf in range(2):
                        P.I("dve", "bn_stats", out=st2[:, hf, :], in_=buf[:, hf * 512:(hf + 1) * 512])
                    P.I("dve", "bn_aggr", out=mv2, in_=st2[:].rearrange("p a b -> p (a b)"))
                    P.act(rs1, mv2[:, 1:2], AF.Sqrt, bias=LN_EPS)
                    P.I("dve", "reciprocal", out=rs1, in_=rs1)
                    P.stt("dve", nb1, mv2[:, 0:1], -1.0, rs1, ALU.mult, ALU.mult)
                    P.act(xn_out, buf, AF.Identity, scale=rs1, bias=nb1)
                    P.tt("pool", buf, xn_out, lnbc[:, gi, :], ALU.mult)
                    P.tt("pool", buf, buf, lnbc[:, gi + 1, :], ALU.add)

                for j in range(4):
                    P.dma("sp", out=xr, in_=x_own[m * 512 + j * 128: m * 512 + (j + 1) * 128, :])
                    P.stt("dve", ty[:, j, :], xr, ALPHA, ty[:, j, :], ALU.mult, ALU.add)
                    layer_norm(ty[:, j, :], 0, xn)
                    xb = xnb[j % 2]
                    P.cp("act", xb, xn)
                    for kc in range(KC):
                        P.tr(psb[kc // 2][:, (kc % 2) * 512 + j * 128:(kc % 2) * 512 + (j + 1) * 128],
                             xb[:, kc * 128:(kc + 1) * 128], ident[:])
                evac_T(hlT, a2b2, 0)
                if m == 0:
                    tap("x1", ty[:, 0, :])
                for i in range(6):
                    ncol = 512 if i < 5 else 256
                    Wa = wload(w_gu_s[:, :, i * 512:i * 512 + ncol])
                    Wb = wload(w_gu_s[:, :, DFF + i * 512:DFF + i * 512 + ncol])
                    for c4 in range(ncol // 128):
                        c = i * 4 + c4
                        pa_, pb_ = (ps[0], ps[1]) if c % 2 == 0 else (ps[2], ps[3])
                        for kc in range(KC):
                            P.mm(pa_[:], Wa[:, kc, c4 * 128:(c4 + 1) * 128], hlT[:, kc, :], start=(kc == 0), stop=(kc == KC - 1), inc=(kc == KC - 1))
                        for kc in range(KC):
                            P.mm(pb_[:], Wb[:, kc, c4 * 128:(c4 + 1) * 128], hlT[:, kc, :], start=(kc == 0), stop=(kc == KC - 1), inc=(kc == KC - 1))
                        P.act(sa[c % 2], pa_[:], AF.Silu)
                        P.tt("dve", actT[:, c, :], pb_[:], sa[c % 2], ALU.mult)
                for hf in range(2):
                    wdx = wd if hf == 0 else wd2
                    P.dma("sp", out=wdx[:], in_=w_down_s[:, :, hf * 512:(hf + 1) * 512])
                    for j in range(4):
                        pt = ps[4 + (j % 2)]
                        for kc in range(22):
                            P.mm(pt[:], actT[:, kc, j * 128:(j + 1) * 128], wdx[:, kc, :], start=(kc == 0), stop=(kc == 21), inc=(kc == 21))
                        P.tt("dve", sa[j % 2], pt[:], g12[:, 1, hf * 512:(hf + 1) * 512], ALU.mult)
                        P.stt("dve", ty[:, j, hf * 512:(hf + 1) * 512], ty[:, j, hf * 512:(hf + 1) * 512], ALPHA, sa[j % 2], ALU.mult, ALU.add)
                for j in range(4):
                    layer_norm(ty[:, j, :], 2, xn)
                    P.dma("sp", out=out_d[m * 512 + j * 128: m * 512 + (j + 1) * 128, :], in_=ty[:, j, :])

        except _Stop:
            pass
        P.finish()
        with nc.Block() as block:
            @block.tensor
            def _(e):
                P.emit("pe", e)

            @block.scalar
            def _(e):
                P.emit("act", e)

            @block.vector
            def _(e):
                P.emit("dve", e)

            @block.gpsimd
            def _(e):
                P.emit("pool", e)

            @block.sync
            def _(e):
                P.emit("sp", e)
    return nc


def _rope_tab(pos):
    half = 64
    freqs = (10000.0 ** (-np.arange(half, dtype=np.float32) / half)).astype(np.float32)
    ang = pos[:, None].astype(np.float32) * freqs[None, :]
    c = np.cos(ang).astype(np.float32)
    s = np.sin(ang).astype(np.float32)
    out = np.zeros((pos.shape[0], 2, 128), np.float32)
    out[:, 0, :64] = c
    out[:, 0, 64:] = c
    out[:, 1, :64] = -s
    out[:, 1, 64:] = s
    return out


def _mrope_tab(pos):
    row = (pos // 64).astype(np.float32)
    col = (pos % 64).astype(np.float32)
    freqs = (10000.0 ** (-np.arange(8, dtype=np.float32) / 8)).astype(np.float32)
    out = np.zeros((pos.shape[0], 2, 32), np.float32)
    for b, p in enumerate((row, col)):
        ang = p[:, None] * freqs[None, :]
        c = np.cos(ang).astype(np.float32)
        s = np.sin(ang).astype(np.float32)
        out[:, 0, b * 16:b * 16 + 8] = c
        out[:, 0, b * 16 + 8:b * 16 + 16] = c
        out[:, 1, b * 16:b * 16 + 8] = -s
        out[:, 1, b * 16 + 8:b * 16 + 16] = s
    return out


def _dexp_core(dexp, h):
    d = dexp.copy()
    p = np.arange(128)
    for b in range(16):
        jj = b * 128 + p
        d[:, 2 + b, 1] = jj if h == 0 else 2047 - jj
    return d


def _col(v, n):
    return np.ascontiguousarray(np.asarray(v, np.float32).reshape(n, 128).T)


def make_in_maps(x, c, ctx, c_ctx, w_ada, b_ada, w_in, ret_decay_f, ret_decay_b, w_ret_o, mla_q_norm, w_uq,
                 mla_kv_norm, w_ukv, w_mla_o, w_out, ln1_g, ln1_b, w_gu, w_down, ln2_g, ln2_b):
    f = lambda a: np.ascontiguousarray(np.asarray(a, np.float32))
    x = f(x); c = f(c); ctx = f(ctx); c_ctx = f(c_ctx)
    p = np.arange(128)
    iot = np.stack([p, 127 - p, p + 1, 128 - p], 1).astype(np.float32)
    i = np.arange(128)[None, :]
    j = np.arange(128)[:, None]
    masktab = np.stack([np.maximum(i - j, 0), np.maximum(j - i, 0), (i >= j), (j >= i)], 1).astype(np.float32)
    rowt = np.zeros((128, 2, 128), np.float32)
    rowt[:, 0, :] = np.arange(128)[None, :] + 1
    rowt[:, 1, :] = 128 - np.arange(128)[None, :]
    dexp = np.zeros((128, NKB, 2), np.float32)
    for b in range(2):
        jj = b * 128 + p
        dexp[:, b, 0] = 255 - jj
        dexp[:, b, 1] = jj
    for b in range(16):
        jj = b * 128 + p
        dexp[:, 2 + b, 0] = 2047 - jj
        dexp[:, 2 + b, 1] = jj
        dexp[:, 18 + b, 0] = 0
        dexp[:, 18 + b, 1] = (b % 4) * 128 + p
    shared = {
        "w_ada": f(w_ada[0]), "bada_col": _col(b_ada[0], 48), "bada_row": f(b_ada[0]).reshape(1, -1),
        "w_in": f(w_in[0]), "w_ret_o": f(w_ret_o[0]), "w_uq": f(w_uq[0]), "w_ukv": f(w_ukv[0]),
        "w_mla_o": f(w_mla_o[0]), "w_out": f(w_out[0]), "w_gu": f(w_gu[0]), "w_down": f(w_down[0]),
        "qn_col": _col(mla_q_norm[0], 3), "kvn_col": _col(mla_kv_norm[0], 2),
        "ln1_col": np.ascontiguousarray(np.stack([_col(ln1_g[0], 8), _col(ln1_b[0], 8)], 2)),
        "lnrows": np.ascontiguousarray(np.stack([f(ln1_g[0]), f(ln1_b[0]), f(ln2_g[0]), f(ln2_b[0])], 0)),
        "dec": np.concatenate([f(ret_decay_f[0]), f(ret_decay_b[0])]).reshape(1, 8),
        "iot": iot, "masktab": masktab, "rowt": rowt, "ident": np.eye(128, dtype=np.float32),
    }
    maps = []
    for core in range(8):
        b, h = core // 2, core % 2
        own = np.arange(h * LH, (h + 1) * LH)
        oth = np.arange((1 - h) * LH, (2 - h) * LH)
        fl = np.zeros((128, 4), np.float32)
        fl[:, 0] = h
        fl[:, 1] = 1 - h
        fl[:, 2] = 1 - h
        fl[:, 3] = h
        mp = dict(shared)
        mp.update({
            "x_own": np.ascontiguousarray(x[b, own]), "x_oth": np.ascontiguousarray(x[b, oth]),
            "ctx": np.ascontiguousarray(ctx[b]),
            "ccol": np.ascontiguousarray(np.stack([_col(c[b], 8), _col(c_ctx, 8)], 2)),
            "flags": fl,
            "dexp": _dexp_core(dexp, h),
            "rope_own": _rope_tab(own), "rope_oth": _rope_tab(oth),
            "mrope_own": _mrope_tab(own), "mrope_oth": _mrope_tab(oth),
        })
        maps.append(mp)
    return maps


_NC_CACHE = {}


def kernel(**inputs):
    if "nc" not in _NC_CACHE:
        _NC_CACHE["nc"] = build()
    nc = _NC_CACHE["nc"]
    in_maps = make_in_maps(**inputs)
    res = run_bass_kernel_spmd(nc, in_maps, core_ids=list(range(8)))
    out = np.zeros((4, 4096, D), np.float32)
    for core in range(8):
        b, h = core // 2, core % 2
        out[b, h * LH:(h + 1) * LH] = np.asarray(res.results[core]["out"], np.float32)
    return out
```
